# Optimizing a Trainium2 kernel written in Bass

```python
import jax, jax.numpy as jnp
from jax import lax
import numpy as np

D_MODEL = 1024
BATCH = 8
SEQ = 2048
DEPTH = 2

GRID_W = 64
CTX_LEN = 256
GLA_HEADS = 4
GLA_DK = 128
GLA_DV = 256
GLA_KW = GLA_HEADS * GLA_DK
GLA_VW = GLA_HEADS * GLA_DV
GLA_LOWRANK = 16
GLA_TAU = 16.0
GLA_CHUNK = 64
RNN_WIDTH = 1024
RNN_BLOCKS = 8
RNN_BLOCK = RNN_WIDTH // RNN_BLOCKS
RNN_CONV = 4
RGLRU_C = 8.0
FFN_HIDDEN = 2816
FFN_CONV = 3
NORM_EPS = 1e-6
IN_SIZES = (GLA_KW, GLA_KW, GLA_VW, GLA_VW, 2 * GLA_LOWRANK, RNN_WIDTH, RNN_WIDTH, D_MODEL, D_MODEL)
D_IN = GLA_KW * 2 + GLA_VW * 2 + 2 * GLA_LOWRANK + RNN_WIDTH * 2 + D_MODEL * 2

kernel_name = "hybrid_gla_rglru_convffn_dit"


def rmsnorm(x, w):
    xf = x.astype(jnp.float32)
    y = xf * lax.rsqrt(jnp.mean(xf * xf, axis=-1, keepdims=True) + NORM_EPS)
    return y * w.astype(jnp.float32)


def modulate(h, shift, scale):
    return h * (1.0 + scale) + shift


def split_columns(z):
    idx = np.cumsum(IN_SIZES)[:-1].tolist()
    return jnp.split(z, idx, axis=-1)


def flip(t):
    return jnp.flip(t, axis=1)


def dwconv1d(x, w, bias):
    K, ch = w.shape
    pl = (K - 1) // 2
    y = lax.conv_general_dilated(
        x, w[:, None, :].astype(x.dtype), window_strides=(1,),
        padding=[(pl, K - 1 - pl)], dimension_numbers=("NWC", "WIO", "NWC"),
        feature_group_count=ch)
    return y + bias


def dwconv2d_grid(x, w, bias):
    B, T, ch = x.shape
    rows = T // GRID_W
    xg = x.reshape(B, rows, GRID_W, ch)
    y = lax.conv_general_dilated(
        xg, w[:, :, None, :].astype(x.dtype), window_strides=(1, 1),
        padding=[(1, 1), (1, 1)], dimension_numbers=("NHWC", "HWIO", "NHWC"),
        feature_group_count=ch)
    return y.reshape(B, T, ch) + bias


def gla_chunked(q, k, v, log_a, s0, with_output):
    B, T, H, K = k.shape
    V = v.shape[-1]
    C = GLA_CHUNK
    N = T // C
    f32 = jnp.float32
    k = k.astype(f32).reshape(B, N, C, H, K)
    v = v.astype(f32).reshape(B, N, C, H, V)
    b = jnp.cumsum(log_a.astype(f32).reshape(B, N, C, H, K), axis=2)
    b_last = b[:, :, -1]
    k_dec = k * jnp.exp(b_last[:, :, None] - b)
    cf = lambda t: jnp.moveaxis(t, 1, 0)

    def update(S, kd, vc, bl):
        return S * jnp.exp(bl)[..., None] + jnp.einsum("bchk,bchv->bhkv", kd, vc)

    if not with_output:
        def step_state(S, xs):
            kd, vc, bl = xs
            return update(S, kd, vc, bl), None
        s_fin, _ = lax.scan(step_state, s0, (cf(k_dec), cf(v), cf(b_last)))
        return None, s_fin

    q = q.astype(f32).reshape(B, N, C, H, K) * (K ** -0.5)
    q_dec = q * jnp.exp(b)
    k_inv = k * jnp.exp(-b)
    causal_in_chunk = jnp.tril(jnp.ones((C, C), dtype=bool))
    scores = jnp.einsum("bnchk,bnshk->bnhcs", q_dec, k_inv)
    scores = jnp.where(causal_in_chunk, scores, 0.0)
    o_intra = jnp.einsum("bnhcs,bnshv->bnchv", scores, v)

    def step(S, xs):
        qd, kd, vc, bl = xs
        o = jnp.einsum("bchk,bhkv->bchv", qd, S)
        return update(S, kd, vc, bl), o

    s_fin, o_inter = lax.scan(step, s0, (cf(q_dec), cf(k_dec), cf(v), cf(b_last)))
    o = o_intra + jnp.moveaxis(o_inter, 0, 1)
    return o.reshape(B, T, H, V), s_fin


def linear_scan(a, u, h0):
    u = u.at[:, 0].add(a[:, 0] * h0)

    def combine(left, right):
        return left[0] * right[0], right[0] * left[1] + right[1]

    _, h = lax.associative_scan(combine, (a, u), axis=1)
    return h


def rglru(x, wa, ba, wx, bx, lam, h0):
    B, T, R = x.shape
    f32 = jnp.float32
    x = x.astype(f32)
    xb = x.reshape(B, T, RNN_BLOCKS, RNN_BLOCK)

    def gate(w, bias):
        return jax.nn.sigmoid(jnp.einsum("btgi,gij->btgj", xb, w.astype(f32)).reshape(B, T, R) + bias.astype(f32))

    r = gate(wa, ba)
    i = gate(wx, bx)
    log_a = RGLRU_C * r * jax.nn.log_sigmoid(lam.astype(f32))
    a = jnp.exp(log_a)
    u = x * i * jnp.sqrt(-jnp.expm1(2.0 * log_a))
    h = linear_scan(a, u, h0)
    return h, h[:, -1]


def mix_stream(z, gla_s0, rnn_h0, lp, with_output):
    q, k, v, g, lr, xr, yr, gate_a, gate_b = split_columns(z)
    B, T = z.shape[:2]
    heads = lambda t, d: t.reshape(B, T, GLA_HEADS, d)
    lr_f, lr_b = jnp.split(lr, 2, axis=-1)
    la_f = jax.nn.log_sigmoid(lr_f @ lp["gla_lr_w"][0] + lp["gla_lr_b"][0]) / GLA_TAU
    la_b = jax.nn.log_sigmoid(lr_b @ lp["gla_lr_w"][1] + lp["gla_lr_b"][1]) / GLA_TAU
    qh, kh, vh = heads(q, GLA_DK), heads(k, GLA_DK), heads(v, GLA_DV)
    o_f, s_f = gla_chunked(qh, kh, vh, heads(la_f, GLA_DK), gla_s0[0], with_output)
    o_b, s_b = gla_chunked(flip(qh), flip(kh), flip(vh), flip(heads(la_b, GLA_DK)), gla_s0[1], with_output)

    xc = dwconv1d(xr, lp["rnn_conv_w"], lp["rnn_conv_b"])
    h_f, hf_last = rglru(xc, lp["rnn_wa"][0], lp["rnn_ba"][0], lp["rnn_wx"][0], lp["rnn_bx"][0],
                         lp["rnn_lambda"][0], rnn_h0[0])
    h_b, hb_last = rglru(flip(xc), lp["rnn_wa"][1], lp["rnn_ba"][1], lp["rnn_wx"][1], lp["rnn_bx"][1],
                         lp["rnn_lambda"][1], rnn_h0[1])
    if not with_output:
        return None, (s_f, s_b), (hf_last, hb_last)

    o = o_f + flip(o_b)
    o = o * lax.rsqrt(jnp.mean(o * o, axis=-1, keepdims=True) + NORM_EPS)
    o = o.reshape(B, T, GLA_VW) * lp["gla_norm_w"] * jax.nn.silu(g)
    r = (h_f + flip(h_b)) * jax.nn.gelu(yr)
    merged = (jax.nn.sigmoid(gate_a) * (o @ lp["w_gla_o"])
              + jax.nn.sigmoid(gate_b) * (r @ lp["w_rnn_o"]))
    return merged @ lp["w_out"], (s_f, s_b), (hf_last, hb_last)


def conv_ffn(h, w_up, conv_w, conv_b, w_down, on_grid):
    u = h @ w_up
    a, gv = jnp.split(u, 2, axis=-1)
    a = dwconv2d_grid(a, conv_w, conv_b) if on_grid else dwconv1d(a, conv_w[1], conv_b)
    return (jax.nn.gelu(a) * gv) @ w_down


def setup_inputs(seed: int = 0) -> dict:
    key = jax.random.key(seed)
    ks = jax.random.split(key, 32)
    f32 = jnp.float32
    nrm = lambda k, shape, s: jax.random.normal(k, shape, f32) * s
    L = DEPTH
    a8 = jax.random.uniform(ks[17], (L, 2, RNN_WIDTH), f32, 0.9, 0.999)
    p = a8 ** (1.0 / RGLRU_C)
    return {
        "x": nrm(ks[0], (BATCH, SEQ, D_MODEL), 1.0),
        "c": nrm(ks[1], (BATCH, D_MODEL), 1.0),
        "ctx": nrm(ks[2], (BATCH, CTX_LEN, D_MODEL), 1.0),
        "c_ctx": nrm(ks[3], (D_MODEL,), 1.0),
        "ada_w": nrm(ks[4], (L, D_MODEL, 6 * D_MODEL), 0.5 * D_MODEL ** -0.5),
        "ada_b": nrm(ks[5], (L, 6 * D_MODEL), 0.01),
        "norm1_w": 1.0 + nrm(ks[6], (L, D_MODEL), 0.02),
        "w_in": nrm(ks[7], (L, D_MODEL, D_IN), D_MODEL ** -0.5),
        "gla_lr_w": nrm(ks[8], (L, 2, GLA_LOWRANK, GLA_KW), GLA_LOWRANK ** -0.5),
        "gla_lr_b": nrm(ks[9], (L, 2, GLA_KW), 0.1),
        "gla_norm_w": 1.0 + nrm(ks[10], (L, GLA_VW), 0.02),
        "rnn_conv_w": nrm(ks[11], (L, RNN_CONV, RNN_WIDTH), RNN_CONV ** -0.5),
        "rnn_conv_b": nrm(ks[12], (L, RNN_WIDTH), 0.01),
        "rnn_wa": nrm(ks[13], (L, 2, RNN_BLOCKS, RNN_BLOCK, RNN_BLOCK), RNN_BLOCK ** -0.5),
        "rnn_ba": nrm(ks[14], (L, 2, RNN_WIDTH), 0.01),
        "rnn_wx": nrm(ks[15], (L, 2, RNN_BLOCKS, RNN_BLOCK, RNN_BLOCK), RNN_BLOCK ** -0.5),
        "rnn_bx": nrm(ks[16], (L, 2, RNN_WIDTH), 0.01),
        "rnn_lambda": jnp.log(p) - jnp.log1p(-p),
        "w_gla_o": nrm(ks[18], (L, GLA_VW, D_MODEL), GLA_VW ** -0.5),
        "w_rnn_o": nrm(ks[19], (L, RNN_WIDTH, D_MODEL), RNN_WIDTH ** -0.5),
        "w_out": nrm(ks[20], (L, D_MODEL, D_MODEL), D_MODEL ** -0.5),
        "norm2_w": 1.0 + nrm(ks[21], (L, D_MODEL), 0.02),
        "ffn_up": nrm(ks[22], (L, D_MODEL, 2 * FFN_HIDDEN), D_MODEL ** -0.5),
        "ffn_conv_w": nrm(ks[23], (L, FFN_CONV, FFN_CONV, FFN_HIDDEN), 1.0 / FFN_CONV),
        "ffn_conv_b": nrm(ks[24], (L, FFN_HIDDEN), 0.01),
        "ffn_down": nrm(ks[25], (L, FFN_HIDDEN, D_MODEL), FFN_HIDDEN ** -0.5),
        "final_norm_w": 1.0 + nrm(ks[26], (D_MODEL,), 0.02),
    }


def reference(x, c, ctx, c_ctx, ada_w, ada_b, norm1_w, w_in, gla_lr_w, gla_lr_b, gla_norm_w,
              rnn_conv_w, rnn_conv_b, rnn_wa, rnn_ba, rnn_wx, rnn_bx, rnn_lambda,
              w_gla_o, w_rnn_o, w_out, norm2_w, ffn_up, ffn_conv_w, ffn_conv_b, ffn_down,
              final_norm_w):
    B = x.shape[0]
    f32 = jnp.float32
    gla_zero = jnp.zeros((B, GLA_HEADS, GLA_DK, GLA_DV), f32)
    rnn_zero = jnp.zeros((B, RNN_WIDTH), f32)
    for l in range(DEPTH):
        lp = dict(gla_lr_w=gla_lr_w[l], gla_lr_b=gla_lr_b[l], gla_norm_w=gla_norm_w[l],
                  rnn_conv_w=rnn_conv_w[l], rnn_conv_b=rnn_conv_b[l],
                  rnn_wa=rnn_wa[l], rnn_ba=rnn_ba[l], rnn_wx=rnn_wx[l], rnn_bx=rnn_bx[l],
                  rnn_lambda=rnn_lambda[l], w_gla_o=w_gla_o[l], w_rnn_o=w_rnn_o[l], w_out=w_out[l])
        last = l == DEPTH - 1
        mod_x = (jax.nn.silu(c) @ ada_w[l] + ada_b[l])[:, None, :]
        mod_c = jax.nn.silu(c_ctx) @ ada_w[l] + ada_b[l]
        sh1, sc1, g1, sh2, sc2, g2 = jnp.split(mod_x, 6, axis=-1)
        csh1, csc1, cg1, csh2, csc2, cg2 = jnp.split(mod_c, 6, axis=-1)

        hc = modulate(rmsnorm(ctx, norm1_w[l]), csh1, csc1)
        hx = modulate(rmsnorm(x, norm1_w[l]), sh1, sc1)
        mc, gla_states, rnn_states = mix_stream(hc @ w_in[l], (gla_zero, gla_zero),
                                                (rnn_zero, rnn_zero), lp, not last)
        mx, _, _ = mix_stream(hx @ w_in[l], gla_states, rnn_states, lp, True)
        x = (x + g1 * mx).astype(x.dtype)

        hx2 = modulate(rmsnorm(x, norm2_w[l]), sh2, sc2)
        x = (x + g2 * conv_ffn(hx2, ffn_up[l], ffn_conv_w[l], ffn_conv_b[l], ffn_down[l], True)).astype(x.dtype)

        if not last:
            ctx = (ctx + cg1 * mc).astype(ctx.dtype)
            hc2 = modulate(rmsnorm(ctx, norm2_w[l]), csh2, csc2)
            ctx = (ctx + cg2 * conv_ffn(hc2, ffn_up[l], ffn_conv_w[l], ffn_conv_b[l], ffn_down[l], False)).astype(ctx.dtype)
    return rmsnorm(x, final_norm_w).astype(x.dtype)
```

```python
import numpy as np
from contextlib import ExitStack
import concourse.bass as bass
import concourse.mybir as mybir
from concourse.bass_utils import run_bass_kernel_spmd

F32 = mybir.dt.float32
BF16 = mybir.dt.bfloat16
ALU = mybir.AluOpType
AF = mybir.ActivationFunctionType

D = 1024
NCTX = 256
SEQ = 2048
NT = NCTX + SEQ
DEPTH = 2
D_IN = 7200
FH = 2816
NHC = FH // 128
NCH = NT // 128
TILES = [(0, 256), (256, 768), (768, 1280), (1280, 1792), (1792, 2304)]
EPS = 1e-6
NDMA_SEM = 8

C_Q, C_K, C_V, C_G, C_LR, C_XR, C_YR, C_GA, C_GB = 0, 512, 1024, 2048, 3072, 3104, 4128, 5152, 6176


class T:
    __slots__ = ("ap", "w", "r", "name")

    def __init__(self, ap, name=""):
        self.ap = ap
        self.w = None
        self.r = []
        self.name = name

    def __getitem__(self, k):
        return self.ap[k]


class Eng:
    def __init__(self, name):
        self.name = name
        self.ops = []
        self.count = 0
        self.seen = {}
        self.dma_i = 0
        self.pending = {}


class Prog:
    def __init__(self, nc):
        self.nc = nc
        self.E = {n: Eng(n) for n in ("pe", "act", "dve", "pool", "sp")}
        self.semnames = ["s_" + n for n in self.E]
        for q in ("sp", "pool"):
            for j in range(NDMA_SEM):
                self.semnames.append("d_%s_%d" % (q, j))

    def _need(self, eng, ev, waits):
        if ev is None:
            return
        key, val, _ = ev
        if eng.seen.get(key, 0) >= val:
            return
        waits[key] = max(waits.get(key, 0), val)

    def op(self, engname, fn, reads=(), writes=()):
        eng = self.E[engname]
        waits = {}
        for b in reads:
            if b.w is not None and not (b.w[2] == engname and engname == "pe"):
                self._need(eng, b.w, waits)
        for b in writes:
            if b.w is not None and b.w[2] != engname:
                self._need(eng, b.w, waits)
            for ev in b.r:
                if ev[2] != engname:
                    self._need(eng, ev, waits)
        self._merge_pending(eng, waits)
        for k, v in waits.items():
            eng.seen[k] = v
        eng.count += 1
        key = "s_" + engname
        ev = (key, eng.count, engname)
        eng.ops.append((list(waits.items()), fn, (key, 1)))
        if not hasattr(self, "labels"):
            self.labels = {}
        self.labels.setdefault(engname, []).append(getattr(self, "cur", ""))
        for b in writes:
            b.w = ev
            b.r = []
        for b in reads:
            if b not in writes:
                b.r = [e for e in b.r if e[2] != engname] + [ev]
        return ev

    def dma(self, qname, fn, reads=(), writes=()):
        eng = self.E[qname]
        j = eng.dma_i % NDMA_SEM
        rnd = eng.dma_i // NDMA_SEM
        eng.dma_i += 1
        key = "d_%s_%d" % (qname, j)
        waits = {}
        if rnd > 0:
            self._need(eng, (key, 16 * rnd, "dma"), waits)
        for b in reads:
            self._need(eng, b.w, waits)
        for b in writes:
            self._need(eng, b.w, waits)
            for ev in b.r:
                self._need(eng, ev, waits)
        self._merge_pending(eng, waits)
        for k, v in waits.items():
            eng.seen[k] = v
        ev = (key, 16 * (rnd + 1), "dma_" + qname)
        eng.ops.append((list(waits.items()), fn, (key, 16)))
        for b in writes:
            b.w = ev
            b.r = []
        for b in reads:
            if b not in writes:
                b.r = b.r + [ev]
        return ev

    def _merge_pending(self, eng, waits):
        for k, v in eng.pending.items():
            if eng.seen.get(k, 0) < v:
                waits[k] = max(waits.get(k, 0), v)
        eng.pending = {}

    def mark(self, name):
        if not hasattr(self, "marks"):
            self.marks = []
        self.marks.append((name, {n: e.count for n, e in self.E.items()}))

    def barrier(self):
        snap = {}
        for n, e in self.E.items():
            if e.count > 0:
                snap["s_" + n] = e.count
            if n in ("sp", "pool"):
                for i in range(min(e.dma_i, NDMA_SEM)):
                    cnt = (e.dma_i - 1 - i) // NDMA_SEM + 1
                    snap["d_%s_%d" % (n, i)] = 16 * cnt
        for n, e in self.E.items():
            for k, v in snap.items():
                if k == "s_" + n:
                    continue
                e.pending[k] = max(e.pending.get(k, 0), v)

    def finish(self, evs):
        eng = self.E["sp"]
        waits = {}
        for ev in evs:
            self._need(eng, ev, waits)
        eng.ops.append((list(waits.items()), None, None))

    def emit(self, sems, block):
        hw = {"pe": "tensor", "act": "scalar", "dve": "vector", "pool": "gpsimd", "sp": "sync"}

        def mk(engname):
            eng = self.E[engname]

            def body(e):
                for waits, fn, inc in eng.ops:
                    for k, v in waits:
                        e.wait_ge(sems[k], v)
                    if fn is not None:
                        fn(e).then_inc(sems[inc[0]], inc[1])
            return body

        for n in self.E:
            if self.E[n].ops:
                getattr(block, hw[n])(mk(n))


def _vec_layout():
    off = {}
    n = 0

    def add(name, cols):
        nonlocal n
        off[name] = n
        n += cols
    add("c", 8)
    add("cctx", 8)
    add("fnw", 8)
    for l in range(DEPTH):
        add("adab%d" % l, 48)
        add("n1w%d" % l, 8)
        add("n2w%d" % l, 8)
        add("lrb%d" % l, 8)
        add("gnw%d" % l, 8)
        add("rcw%d" % l, 32)
        add("rcb%d" % l, 8)
        add("rba%d" % l, 16)
        add("rbx%d" % l, 16)
        add("rlam%d" % l, 16)
        add("fcw%d" % l, 9 * NHC)
        add("fcb%d" % l, NHC)
    return off, n


VOFF, NV = _vec_layout()
CI, CMF, CMB, CSF, CSB, NCST = 0, 128, 256, 384, 896, 1408


def _col(v):
    v = np.asarray(v, np.float32).reshape(-1, 128)
    return v.T


def _pack_vecs(inp, b):
    V = np.zeros((128, NV), np.float32)

    def put(name, arr):
        a = _col(arr)
        V[:, VOFF[name]:VOFF[name] + a.shape[1]] = a
    put("c", inp["c"][b])
    put("cctx", inp["c_ctx"])
    put("fnw", inp["final_norm_w"])
    for l in range(DEPTH):
        put("adab%d" % l, inp["ada_b"][l])
        put("n1w%d" % l, inp["norm1_w"][l])
        put("n2w%d" % l, inp["norm2_w"][l])
        put("lrb%d" % l, inp["gla_lr_b"][l])
        put("gnw%d" % l, inp["gla_norm_w"][l])
        put("rcw%d" % l, inp["rnn_conv_w"][l])
        put("rcb%d" % l, inp["rnn_conv_b"][l])
        put("rba%d" % l, inp["rnn_ba"][l])
        put("rbx%d" % l, inp["rnn_bx"][l])
        put("rlam%d" % l, inp["rnn_lambda"][l])
        put("fcw%d" % l, inp["ffn_conv_w"][l])
        put("fcb%d" % l, inp["ffn_conv_b"][l])
    return V


def _consts():
    C = np.zeros((128, NCST), np.float32)
    C[:, CI:CI + 128] = np.eye(128, dtype=np.float32)
    s = np.arange(128)[:, None]
    c = np.arange(128)[None, :]
    C[:, CMF:CMF + 128] = (s <= c)
    C[:, CMB:CMB + 128] = (s >= c)
    t = np.arange(512)
    C[:, CSF:CSF + 512] = (t % 128 != 0)[None, :]
    C[:, CSB:CSB + 512] = (t % 128 != 127)[None, :]
    return C


def build_nc(debug=False):
    nc = bass.Bass("TRN2", target_bir_lowering=False)
    P = Prog(nc)

    def din(name, shape):
        return nc.dram_tensor(name, list(shape), F32, kind="ExternalInput").ap()
    xs0 = din("xs0", [D, NT])
    vecs_d = din("vecs", [128, NV])
    cst_d = din("cst", [128, NCST])
    ada_w = din("ada_w", [DEPTH, D, 6 * D])
    w_in = din("w_in", [DEPTH, D, D_IN])
    lr_w = din("gla_lr_w", [DEPTH, 2, 16, 512])
    rnn_wa = din("rnn_wa", [DEPTH, 2, 8, 128, 128])
    rnn_wx = din("rnn_wx", [DEPTH, 2, 8, 128, 128])
    w_go = din("w_gla_o", [DEPTH, D, D])
    w_ro = din("w_rnn_o", [DEPTH, D, D])
    w_out = din("w_out", [DEPTH, D, D])
    ffn_up = din("ffn_up", [DEPTH, D, 2 * FH])
    ffn_dn = din("ffn_down", [DEPTH, FH, D])
    outT = nc.dram_tensor("outT", [D, SEQ], F32, kind="ExternalOutput").ap()
    skind = "ExternalOutput" if debug else "Internal"
    xs_s = [T(xs0, "xs0")] + [T(nc.dram_tensor("xs%d" % i, [D, NT], F32, kind=skind).ap(), "xs%d" % i) for i in (1, 2, 3)]
    og_d = T(nc.dram_tensor("og", [D, NT], BF16, kind=skind).ap(), "og")
    rr_d = T(nc.dram_tensor("rr", [D, NT], BF16, kind=skind).ap(), "rr")
    dbg = {}
    if debug:
        for l in range(DEPTH):
            dbg["mod%d" % l] = nc.dram_tensor("dbg_mod%d" % l, [128, 96], F32, kind="ExternalOutput").ap()
            dbg["dv%d" % l] = nc.dram_tensor("dbg_dv%d" % l, [128, 256], F32, kind="ExternalOutput").ap()
            dbg["hx%d" % l] = nc.dram_tensor("dbg_hx%d" % l, [128, 8 * NT], BF16, kind="ExternalOutput").ap()
            dbg["hx2_%d" % l] = nc.dram_tensor("dbg_hx2_%d" % l, [128, 8 * NT], BF16, kind="ExternalOutput").ap()
            dbg["og%d" % l] = nc.dram_tensor("dbg_og%d" % l, [D, NT], BF16, kind="ExternalOutput").ap()
            dbg["rr%d" % l] = nc.dram_tensor("dbg_rr%d" % l, [D, NT], BF16, kind="ExternalOutput").ap()
    outT_t = T(outT, "outT")
    dummy = T(None, "wdram")

    def kc_view(ap2d):
        return ap2d.rearrange("(kc p) n -> p kc n", p=128)

    out_evs = []
    st = ExitStack()
    with st:
        sems = {n: st.enter_context(nc.semaphore(n)) for n in P.semnames}

        _uid = [0]

        def sbuf(stack, name, shape, dt=F32):
            _uid[0] += 1
            return stack.enter_context(nc.sbuf_tensor("sb%d_%s" % (_uid[0], name), list(shape), dt))

        def ACT(out, in_, func, r, w, bias=0.0, scale=1.0):
            P.op("act", lambda e: e.activation(out=out, in_=in_, func=func, bias=bias, scale=scale), r, w)

        def TT(out, a, b, op, r, w):
            P.op("dve", lambda e: e.tensor_tensor(out, a, b, op), r, w)

        def TS(out, a, s1, s2, op0, op1, r, w):
            if s2 is None:
                P.op("dve", lambda e: e.tensor_scalar(out, a, s1, None, op0), r, w)
            else:
                P.op("dve", lambda e: e.tensor_scalar(out, a, s1, s2, op0, op1), r, w)

        def SCAN(out, d0, d1, init, r, w):
            P.op("dve", lambda e: e.tensor_tensor_scan(out, d0, d1, init, ALU.mult, ALU.add), r, w)

        def STT(out, a, s, b, op0, op1, r, w):
            P.op("dve", lambda e: e.scalar_tensor_tensor(out, a, s, b, op0, op1), r, w)

        def PCOPY(out, a, r, w):
            P.op("pool", lambda e: e.tensor_copy(out, a), r, w)

        def PTT(out, a, b, op, r, w):
            P.op("pool", lambda e: e.tensor_tensor(out, a, b, op), r, w)

        def DCOPY(out, a, r, w):
            P.op("dve", lambda e: e.tensor_copy(out, a), r, w)

        def MEMSET(out, v, w):
            P.op("dve", lambda e: e.memset(out, v), (), w)

        def MM(out, lhsT, rhs, start, stop, r, w):
            P.op("pe", lambda e: e.matmul(out, lhsT, rhs, start=start, stop=stop), r, w)

        def TR(out, in_, ident, r, w):
            P.op("pe", lambda e: e.transpose(out, in_, ident), r, w)

        def DMA(q, out, in_, r, w):
            return P.dma(q, lambda e: e.dma_start(out=out, in_=in_), r, w)

        vecs = sbuf(st, "vecs", [128, NV]); vecs_t = T(vecs, "vecs")
        cst = sbuf(st, "cst", [128, NCST]); cst_t = T(cst, "cst")
        cstb = sbuf(st, "cstb", [128, 128], BF16); cstb_t = T(cstb, "cstb")
        ones_b = sbuf(st, "ones_b", [128, 128], BF16); ones_t = T(ones_b, "ones")
        dv = sbuf(st, "dv", [128, 256]); dv_t = T(dv, "dv")
        mod = sbuf(st, "mod", [128, 96]); mod_t = T(mod, "mod")
        scb = sbuf(st, "scb", [128, 8, 2], BF16); scb_t = T(scb, "scb")
        RR = sbuf(st, "RR", [128, NHC * D], BF16)
        hx = RR[:, 0:8 * NT].rearrange("p (kc n) -> p kc n", kc=8)
        wd = RR[:, :].rearrange("p (hc n) -> p hc n", hc=NHC)
        wd_t = T(wd, "wd")
        hx_t = [T(hx[:, :, a:b], "hx%d" % i) for i, (a, b) in enumerate(TILES)]
        banks = [T(st.enter_context(nc.psum_tensor("pb%d" % i, [128, 512], F32)), "pb%d" % i) for i in range(8)]

        DMA("sp", vecs[:], vecs_d, [dummy], [vecs_t])
        DMA("sp", cst[:], cst_d, [dummy], [cst_t])
        DCOPY(cstb[:], cst[:, CI:CI + 128], [cst_t], [cstb_t])
        MEMSET(ones_b[:], 1.0, [ones_t])

        def V(name, i=0, n=1):
            o = VOFF[name] + i
            return vecs[:, o:o + n]

        def tile_of_chunk(n):
            return 0 if n < 2 else 1 + (n - 2) // 4

        DV_S1X, DV_S1C, DV_S2X, DV_S2C, DV_NLRB, DV_L, DV_SILU = 0, 8, 16, 24, 32, 40, 56

        def ada_phase(l, wst, wslots):
            ACT(scb[:, :, 0], V("c", 0, 8), AF.Silu, [vecs_t], [scb_t])
            ACT(scb[:, :, 1], V("cctx", 0, 8), AF.Silu, [vecs_t], [scb_t])
            pb = banks[0]
            aw = kc_view(ada_w[l])
            for grp in range(12):
                ws = wslots[grp % len(wslots)]
                DMA("pool", ws.ap[:, :].rearrange("p (kc n) -> p kc n", kc=8), aw[:, :, grp * 512:(grp + 1) * 512], [dummy], [ws])
                wv = ws.ap[:, :].rearrange("p (kc n) -> p kc n", kc=8)
                for jj in range(4):
                    j = grp * 4 + jj
                    for kc in range(8):
                        MM(pb[:, j * 2:j * 2 + 2], wv[:, kc, jj * 128:(jj + 1) * 128], scb[:, kc, :],
                           kc == 0, kc == 7, [ws, scb_t], [pb])
            m3 = mod[:, :].rearrange("p (j s) -> p j s", s=2)
            p3 = pb[:, 0:96].rearrange("p (j s) -> p j s", s=2)
            for s in range(2):
                TT(m3[:, :, s], p3[:, :, s], V("adab%d" % l, 0, 48), ALU.add, [pb, vecs_t], [mod_t])
            for s, (o1, o2) in enumerate(((DV_S1X, DV_S2X), (DV_S1C, DV_S2C))):
                STT(dv[:, o1:o1 + 8], m3[:, 8:16, s], 1.0, V("n1w%d" % l, 0, 8), ALU.add, ALU.mult, [mod_t, vecs_t], [dv_t])
                STT(dv[:, o2:o2 + 8], m3[:, 32:40, s], 1.0, V("n2w%d" % l, 0, 8), ALU.add, ALU.mult, [mod_t, vecs_t], [dv_t])
            TS(dv[:, DV_NLRB:DV_NLRB + 8], V("lrb%d" % l, 0, 8), -1.0, None, ALU.mult, ALU.bypass, [vecs_t], [dv_t])
            ACT(dv[:, DV_L:DV_L + 16], V("rlam%d" % l, 0, 16), AF.Exp, [vecs_t], [dv_t], scale=-1.0)
            ACT(dv[:, DV_L:DV_L + 16], dv[:, DV_L:DV_L + 16], AF.Ln, [dv_t], [dv_t], bias=1.0)
            TS(dv[:, DV_L:DV_L + 16], dv[:, DV_L:DV_L + 16], -8.0, None, ALU.mult, ALU.bypass, [dv_t], [dv_t])

        def modcol(part, kc, s):
            j = part * 8 + kc
            return mod[:, j * 2 + s:j * 2 + s + 1]

        def norm_tile(xt_ap, xt_T, ti, sc_off, sh_part, out_tile_T, out_ap, tmp, nw_scale_from_dv=True, final=False):
            a, b = TILES[ti]
            n = b - a
            s = 0 if ti > 0 else 1
            sq, sq_t, rs, rs_t, t1l, t1l_t = tmp
            ACT(sq[:, :, :n], xt_ap[:, :, :n], AF.Square, [xt_T], [sq_t])
            pb = banks[7]
            for kc in range(8):
                MM(pb[:, :n], ones_b[:, :], sq[:, kc, :n], kc == 0, kc == 7, [ones_t, sq_t], [pb])
            ACT(rs[:, :n], pb[:, :n], AF.Ln, [pb], [rs_t], bias=EPS, scale=1.0 / D)
            ACT(rs[:, :n], rs[:, :n], AF.Exp, [rs_t], [rs_t], scale=-0.5)
            for kc in range(8):
                t1, t1_t = t1l[kc % 2], t1l_t[kc % 2]
                TT(t1[:, :n], xt_ap[:, kc, :n], rs[:, :n], ALU.mult, [xt_T, rs_t], [t1_t])
                if final:
                    TS(out_ap[:, kc, :n], t1[:, :n], V("fnw", kc), None, ALU.mult, ALU.bypass, [t1_t, vecs_t], [out_tile_T])
                else:
                    so = (sc_off[0] if s == 0 else sc_off[1]) + kc
                    ACT(out_ap[:, kc, :n], t1[:, :n], AF.Identity, [t1_t, dv_t, mod_t], [out_tile_T],
                        bias=modcol(sh_part, kc, s), scale=dv[:, so:so + 1])

        def norm1_phase(l, src, stack):
            xt = [sbuf(stack, "n1x%d" % i, [128, 8, 512]) for i in range(2)]
            xt_A = [T(x[:, 0:4, :], "n1xa") for x in xt]
            xt_B = [T(x[:, 4:8, :], "n1xb") for x in xt]
            sq = [sbuf(stack, "n1sq%d" % i, [128, 8, 512], BF16) for i in range(2)]; sq_t = [T(sq[0]), T(sq[1])]
            rs = [sbuf(stack, "n1rs%d" % i, [128, 512]) for i in range(2)]; rs_t = [T(rs[0]), T(rs[1])]
            t1 = [sbuf(stack, "n1t1%d" % i, [128, 512]) for i in range(2)]; t1_t = [T(t1[0]), T(t1[1])]
            srcv = kc_view(src.ap)

            def stat1(ti):
                a, b = TILES[ti]
                n = b - a
                k = ti % 2
                DMA("sp", xt[k][:, 0:4, :n], srcv[:, 0:4, a:b], [src], [xt_A[k]])
                DMA("pool", xt[k][:, 4:8, :n], srcv[:, 4:8, a:b], [src], [xt_B[k]])
                ACT(sq[k][:, :, :n], xt[k][:, :, :n], AF.Square, [xt_A[k], xt_B[k]], [sq_t[k]])
                pb = banks[6 + k]
                for kc in range(8):
                    MM(pb[:, :n], ones_b[:, :], sq[k][:, kc, :n], kc == 0, kc == 7, [ones_t, sq_t[k]], [pb])

            def stat2(ti):
                a, b = TILES[ti]
                n = b - a
                k = ti % 2
                pb = banks[6 + k]
                ACT(rs[k][:, :n], pb[:, :n], AF.Ln, [pb], [rs_t[k]], bias=EPS, scale=1.0 / D)
                ACT(rs[k][:, :n], rs[k][:, :n], AF.Exp, [rs_t[k]], [rs_t[k]], scale=-0.5)

            def apply(ti):
                a, b = TILES[ti]
                n = b - a
                k = ti % 2
                s_ = 0 if ti > 0 else 1
                for kc in range(8):
                    tt, tt_t = t1[kc % 2], t1_t[kc % 2]
                    TT(tt[:, :n], xt[k][:, kc, :n], rs[k][:, :n], ALU.mult, [(xt_A if kc < 4 else xt_B)[k], rs_t[k]], [tt_t])
                    so = (DV_S1X if s_ == 0 else DV_S1C) + kc
                    ACT(hx[:, kc, a:b], tt[:, :n], AF.Identity, [tt_t, dv_t, mod_t], [hx_t[ti]],
                        bias=modcol(0, kc, s_), scale=dv[:, so:so + 1])

            stat1(0)
            stat2(0)
            for ti in range(len(TILES)):
                if ti + 1 < len(TILES):
                    stat1(ti + 1)
                apply(ti)
                if ti + 1 < len(TILES):
                    stat2(ti + 1)

        def gla_phase(l, last, stack):
            win = kc_view(w_in[l])
            ws = []
            for i in range(2):
                d = {}
                for nm, cols in (("q", 128), ("k", 128), ("v", 256), ("g", 256)):
                    tns = sbuf(stack, "gw%s%d" % (nm, i), [128, 8, cols], BF16)
                    d[nm] = T(tns, "gw" + nm)
                ws.append(d)

            def load_head(h):
                d = ws[h % 2]
                DMA("pool", d["q"].ap[:], win[:, :, C_Q + h * 128:C_Q + (h + 1) * 128], [dummy], [d["q"]])
                DMA("pool", d["k"].ap[:], win[:, :, C_K + h * 128:C_K + (h + 1) * 128], [dummy], [d["k"]])
                DMA("pool", d["v"].ap[:], win[:, :, C_V + h * 256:C_V + (h + 1) * 256], [dummy], [d["v"]])
                DMA("pool", d["g"].ap[:], win[:, :, C_G + h * 256:C_G + (h + 1) * 256], [dummy], [d["g"]])
            wlrc = sbuf(stack, "wlrc", [128, 8, 32], BF16); wlrc_t = T(wlrc)
            wlr = sbuf(stack, "wlr", [16, 2, 512], BF16); wlr_t = T(wlr)
            DMA("pool", wlrc[:], win[:, :, C_LR:C_LR + 32], [dummy], [wlrc_t])
            DMA("pool", wlr[:], lr_w[l].rearrange("d r k -> r d k"), [dummy], [wlr_t])
            load_head(0)
            lrT = sbuf(stack, "lrT", [16, 2, NT], BF16); lrT_t = [T(lrT[:, :, a:b]) for (a, b) in TILES]
            for ti, (a, b) in enumerate(TILES):
                n = b - a
                for d in range(2):
                    pb = banks[d]
                    for kc in range(8):
                        MM(pb[0:16, :n], wlrc[:, kc, d * 16:(d + 1) * 16], hx[:, kc, a:b], kc == 0, kc == 7, [wlrc_t, hx_t[ti]], [pb])
                    ACT(lrT[:, d, a:b], pb[0:16, :n], AF.Identity, [pb], [lrT_t[ti]])

            qd = [sbuf(stack, "qd%d" % d, [128, NT], BF16) for d in range(2)]
            ki = [sbuf(stack, "ki%d" % d, [128, NT], BF16) for d in range(2)]
            qd_t = [[T(qd[d][:, a:b]) for (a, b) in TILES] for d in range(2)]
            ki_t = [[T(ki[d][:, a:b]) for (a, b) in TILES] for d in range(2)]
            kt = [sbuf(stack, "kt%d" % d, [128, NCH, 128], BF16) for d in range(2)]
            kt_t = [[T(kt[d][:, 0:1, :]) for _ in TILES] for d in range(2)]
            vt = sbuf(stack, "vt", [128, NCH, 256], BF16); vt_t = [T(vt[:, 0:1, :]) for _ in TILES]
            sb_ = [sbuf(stack, "sb%d" % d, [128, NCH, 256], BF16) for d in range(2)]
            sb_t = [[T(sb_[d][:, n, :]) for n in range(NCH)] for d in range(2)]
            el = sbuf(stack, "el", [128, 2, NCH]); el_t = [[T(el[:, d, 0:1]) for _ in TILES] for d in range(2)]
            S = [sbuf(stack, "S%d" % d, [128, 256]) for d in range(2)]; S_t = [T(S[0]), T(S[1])]
            Stmp = [sbuf(stack, "Stmp%d" % d, [128, 256]) for d in range(2)]; Stmp_t = [T(Stmp[0]), T(Stmp[1])]
            tm = {}
            for d in range(2):
                for p_ in range(2):
                    for nm in ("A", "B", "C"):
                        tns = sbuf(stack, "g%s%d%d" % (nm, d, p_), [128, 512])
                        tm[(nm, d, p_)] = (tns, T(tns))
            scm_all = sbuf(stack, "scm_all", [128, 2, NCH, 128], BF16)
            scm_t = [[T(scm_all[:, d, n, :]) for n in range(NCH)] for d in range(2)]
            sq = sbuf(stack, "gsq", [128, 2, 512], BF16); sq_t = T(sq)
            rs = sbuf(stack, "grs", [128, 512]); rs_t = T(rs)
            sg = sbuf(stack, "gsg", [128, 2, 512]); sg_t = T(sg)
            t1 = sbuf(stack, "gt1", [128, 512]); t1_t = T(t1)
            ogt = [sbuf(stack, "ogt%d" % i, [128, 2, 512], BF16) for i in range(2)]; ogt_t = [T(ogt[0]), T(ogt[1])]
            ogv = og_d.ap.rearrange("(c p) n -> p c n", p=128)
            og_i = 0
            border = [1, 0] + list(range(NCH - 1, 1, -1))
            forder = list(range(NCH))

            kdt = {}
            for d in range(2):
                for p_ in range(2):
                    tns = sbuf(stack, "gkd%d%d" % (d, p_), [128, 512], BF16)
                    kdt[(d, p_)] = (tns, T(tns))

            for h in range(4):
                W = ws[h % 2]
                if h + 1 < 4:
                    load_head(h + 1)
                P.mark("gla_prep")

                def stage1(ti):
                    a, b = TILES[ti]
                    n = b - a
                    nch = n // 128
                    c0 = a // 128
                    p_ = ti % 2
                    pq, pk = banks[0 + p_], banks[2 + p_]
                    for kc in range(8):
                        MM(pq[:, :n], W["q"].ap[:, kc, :], hx[:, kc, a:b], kc == 0, kc == 7, [W["q"], hx_t[ti]], [pq])
                    for kc in range(8):
                        MM(pk[:, :n], W["k"].ap[:, kc, :], hx[:, kc, a:b], kc == 0, kc == 7, [W["k"], hx_t[ti]], [pk])
                    for j in range(nch):
                        pv = banks[6 + (j // 2) % 2]
                        hs_ = (j % 2) * 256
                        for kc in range(8):
                            MM(pv[:, hs_:hs_ + 256], hx[:, kc, a + j * 128:a + (j + 1) * 128], W["v"].ap[:, kc, :], kc == 0, kc == 7, [hx_t[ti], W["v"]], [pv])
                        if j % 2 == 1 or j == nch - 1:
                            j0 = j - (j % 2)
                            w_ = (j - j0 + 1) * 256
                            DCOPY(vt[:, c0 + j0:c0 + j + 1, :], pv[:, 0:w_].rearrange("p (c v) -> p c v", v=256), [pv], [vt_t[ti]])
                    for d in range(2):
                        pl = banks[4]
                        MM(pl[:, :n], wlr[:, d, h * 128:(h + 1) * 128], lrT[:, d, a:b], True, True, [wlr_t, lrT_t[ti]], [pl])
                        A_, A_t = tm[("A", d, p_)]
                        nb = dv[:, DV_NLRB + d * 4 + h:DV_NLRB + d * 4 + h + 1]
                        ACT(A_[:, :n], pl[:, :n], AF.Exp, [pl, dv_t], [A_t], bias=nb, scale=-1.0)

                def stage2(ti):
                    a, b = TILES[ti]
                    n = b - a
                    nch = n // 128
                    c0 = a // 128
                    p_ = ti % 2
                    pq, pk = banks[0 + p_], banks[2 + p_]
                    X = [(tm[("A", d, p_)], tm[("B", d, p_)], tm[("C", d, p_)]) for d in range(2)]
                    for d in range(2):
                        (A_, A_t), (B_, B_t), (C_, C_t) = X[d]
                        ACT(B_[:, :n], A_[:, :n], AF.Ln, [A_t], [B_t], bias=1.0)
                    for d in range(2):
                        (A_, A_t), (B_, B_t), (C_, C_t) = X[d]
                        if d == 0:
                            SCAN(C_[:, :n], cst[:, CSF:CSF + n], B_[:, :n], 0.0, [cst_t, B_t], [C_t])
                        else:
                            SCAN(C_[:, :n][:, ::-1], cst[:, CSB + 512 - n:CSB + 512][:, ::-1], B_[:, :n][:, ::-1], 0.0, [cst_t, B_t], [C_t])
                    for d in range(2):
                        (A_, A_t), (B_, B_t), (C_, C_t) = X[d]
                        if d == 0:
                            ACT(el[:, 0, c0:c0 + nch], C_[:, 127:n:128], AF.Exp, [C_t], [el_t[0][ti]], scale=-1.0 / 16)
                        else:
                            ACT(el[:, 1, c0:c0 + nch], C_[:, 0:n:128], AF.Exp, [C_t], [el_t[1][ti]], scale=-1.0 / 16)
                        ACT(A_[:, :n], C_[:, :n], AF.Exp, [C_t], [A_t], scale=-1.0 / 16)
                        ACT(B_[:, :n], C_[:, :n], AF.Exp, [C_t], [B_t], scale=1.0 / 16)
                    for d in range(2):
                        (A_, A_t), (B_, B_t), (C_, C_t) = X[d]
                        kd, kd_t = kdt[(d, p_)]
                        STT(qd[d][:, a:b], pq[:, :n], 128.0 ** -0.5, A_[:, :n], ALU.mult, ALU.mult, [pq, A_t], [qd_t[d][ti]])
                        TT(ki[d][:, a:b], pk[:, :n], B_[:, :n], ALU.mult, [pk, B_t], [ki_t[d][ti]])
                        TT(kd[:, :n].rearrange("p (c k) -> p c k", k=128), ki[d][:, a:b].rearrange("p (c k) -> p c k", k=128),
                           el[:, d, c0:c0 + nch].to_broadcast([128, nch, 128]) if False else el[:, d, c0:c0 + nch, None].to_broadcast([128, nch, 128]),
                           ALU.mult, [ki_t[d][ti], el_t[d][ti]], [kd_t])

                def stage3(ti):
                    a, b = TILES[ti]
                    n = b - a
                    nch = n // 128
                    c0 = a // 128
                    p_ = ti % 2
                    ptr = banks[5]
                    ptb = ptr.ap[:, :].bitcast(BF16)
                    for d in range(2):
                        kd, kd_t = kdt[(d, p_)]
                        for j in range(nch):
                            TR(ptb[:, d * 512 + j * 128:d * 512 + (j + 1) * 128], kd[:, j * 128:(j + 1) * 128], cstb[:, :], [kd_t, cstb_t], [ptr])
                    for d in range(2):
                        ACT(kt[d][:, c0:c0 + nch, :], ptb[:, d * 512:d * 512 + nch * 128].rearrange("p (c k) -> p c k", k=128), AF.Copy,
                            [ptr], [kt_t[d][ti]])

                NTI = len(TILES)
                stage1(0)
                for ti in range(NTI):
                    stage2(ti)
                    if ti + 1 < NTI:
                        stage1(ti + 1)
                    stage3(ti)

                P.mark("gla_state")
                chunks = [n_ for n_ in range(NCH) if not (last and n_ < 2)]

                def scores(n_):
                    P.cur = "sc%d_%d" % (h, n_)
                    ti = tile_of_chunk(n_)
                    cs = slice(n_ * 128, (n_ + 1) * 128)
                    p_ = n_ % 2
                    psc = banks[4 + p_]
                    for d in range(2):
                        MM(psc[:, d * 128:(d + 1) * 128], ki[d][:, cs], qd[d][:, cs], True, True, [ki_t[d][ti], qd_t[d][ti]], [psc])
                    for d in range(2):
                        TT(scm_all[:, d, n_, :], psc[:, d * 128:(d + 1) * 128], cst[:, (CMF, CMB)[d]:(CMF, CMB)[d] + 128], ALU.mult, [psc, cst_t], [scm_t[d][n_]])

                for d in range(2):
                    MEMSET(S[d][:], 0.0, [S_t[d]])
                for idx in range(NCH):
                    if idx < len(chunks):
                        scores(chunks[idx])
                    P.cur = "state"
                    for d, order in ((1, border), (0, forder)):
                        n_ = order[idx]
                        ti = tile_of_chunk(n_)
                        ACT(sb_[d][:, n_, :], S[d][:], AF.Copy, [S_t[d]], [sb_t[d][n_]])
                        if idx == NCH - 1:
                            continue
                        pp = banks[2 * d + (idx % 2)]
                        MM(pp[:, 0:256], kt[d][:, n_, :], vt[:, n_, :], True, True, [kt_t[d][ti], vt_t[ti]], [pp])
                        STT(S[d][:], S[d][:], el[:, d, n_:n_ + 1], pp[:, 0:256], ALU.mult, ALU.add, [S_t[d], el_t[d][ti], pp], [S_t[d]])

                P.mark("gla_out")
                def outs(n_):
                    P.cur = "out%d_%d" % (h, n_)
                    ti = tile_of_chunk(n_)
                    a, b = TILES[ti]
                    j = n_ - a // 128
                    ob = [banks[0 + 2 * (ti % 2)], banks[1 + 2 * (ti % 2)]]
                    cs = slice(n_ * 128, (n_ + 1) * 128)
                    p_ = n_ % 2
                    for vh in range(2):
                        o = ob[vh][:, j * 128:(j + 1) * 128]
                        vs = slice(vh * 128, (vh + 1) * 128)
                        MM(o, sb_[0][:, n_, vs], qd[0][:, cs], True, False, [sb_t[0][n_], qd_t[0][ti]], [ob[vh]])
                        MM(o, sb_[1][:, n_, vs], qd[1][:, cs], False, False, [sb_t[1][n_], qd_t[1][ti]], [ob[vh]])
                        MM(o, vt[:, n_, vs], scm_all[:, 0, n_, :], False, False, [vt_t[ti], scm_t[0][n_]], [ob[vh]])
                        MM(o, vt[:, n_, vs], scm_all[:, 1, n_, :], False, True, [vt_t[ti], scm_t[1][n_]], [ob[vh]])

                def epilogue(ti, k2):
                    P.cur = "epi%d_%d" % (h, ti)
                    a, b = TILES[ti]
                    nn = b - a
                    ob = [banks[0 + 2 * (ti % 2)], banks[1 + 2 * (ti % 2)]]
                    pg = banks[7]
                    pss = banks[6]
                    for vh in range(2):
                        ACT(sq[:, vh, :nn], ob[vh][:, :nn], AF.Square, [ob[vh]], [sq_t])
                    for kc in range(8):
                        MM(pg[:, :nn], W["g"].ap[:, kc, 0:128], hx[:, kc, a:b], kc == 0, kc == 7, [W["g"], hx_t[ti]], [pg])
                    yield
                    P.cur = "epi%d_%d" % (h, ti)
                    for vh in range(2):
                        MM(pss[:, :nn], ones_b[:, :], sq[:, vh, :nn], vh == 0, vh == 1, [ones_t, sq_t], [pss])
                    ACT(rs[:, :nn], pss[:, :nn], AF.Ln, [pss], [rs_t], bias=EPS, scale=1.0 / 256)
                    ACT(rs[:, :nn], rs[:, :nn], AF.Exp, [rs_t], [rs_t], scale=-0.5)
                    ACT(sg[:, 0, :nn], pg[:, :nn], AF.Silu, [pg], [sg_t])
                    TT(t1[:, :nn], ob[0][:, :nn], rs[:, :nn], ALU.mult, [ob[0], rs_t], [t1_t])
                    STT(ogt[k2][:, 0, :nn], t1[:, :nn], V("gnw%d" % l, h * 2 + 0), sg[:, 0, :nn], ALU.mult, ALU.mult,
                        [t1_t, vecs_t, sg_t], [ogt_t[k2]])
                    yield
                    P.cur = "epi%d_%d" % (h, ti)
                    for kc in range(8):
                        MM(pg[:, :nn], W["g"].ap[:, kc, 128:256], hx[:, kc, a:b], kc == 0, kc == 7, [W["g"], hx_t[ti]], [pg])
                    ACT(sg[:, 1, :nn], pg[:, :nn], AF.Silu, [pg], [sg_t])
                    TT(t1[:, :nn], ob[1][:, :nn], rs[:, :nn], ALU.mult, [ob[1], rs_t], [t1_t])
                    STT(ogt[k2][:, 1, :nn], t1[:, :nn], V("gnw%d" % l, h * 2 + 1), sg[:, 1, :nn], ALU.mult, ALU.mult,
                        [t1_t, vecs_t, sg_t], [ogt_t[k2]])
                    DMA("sp", ogv[:, h * 2:h * 2 + 2, a:b], ogt[k2][:, :, :nn], [ogt_t[k2]], [og_d])

                pending = []

                def advance():
                    if pending:
                        try:
                            next(pending[0])
                        except StopIteration:
                            pending.pop(0)
                            advance()

                for ci, n_ in enumerate(chunks):
                    outs(n_)
                    advance()
                    ti = tile_of_chunk(n_)
                    if (n_ + 1) * 128 == TILES[ti][1]:
                        pending.append(epilogue(ti, og_i % 2))
                        og_i += 1
                while pending:
                    advance()

        def rnn_phase(l, last, stack):
            win = kc_view(w_in[l])
            wsl = []
            for i in range(3):
                d = {}
                d["xr"] = T(sbuf(stack, "rwx%d" % i, [128, 8, 128], BF16))
                d["yr"] = T(sbuf(stack, "rwy%d" % i, [128, 8, 128], BF16))
                d["g"] = T(sbuf(stack, "rwg%d" % i, [128, 4, 128], BF16))
                wsl.append(d)

            def load_blk(g):
                d = wsl[g % 3]
                DMA("pool", d["xr"].ap[:], win[:, :, C_XR + g * 128:C_XR + (g + 1) * 128], [dummy], [d["xr"]])
                DMA("pool", d["yr"].ap[:], win[:, :, C_YR + g * 128:C_YR + (g + 1) * 128], [dummy], [d["yr"]])
                DMA("pool", d["g"].ap[:, 0:2, :], rnn_wa[l, :, g].rearrange("d i j -> i d j"), [dummy], [d["g"]])
                DMA("pool", d["g"].ap[:, 2:4, :], rnn_wx[l, :, g].rearrange("d i j -> i d j"), [dummy], [d["g"]])
            load_blk(0)
            load_blk(1)
            XP = 2312
            xrp = sbuf(stack, "xrp", [128, XP], BF16); xrp_t = T(xrp)
            MEMSET(xrp[:], 0.0, [xrp_t])
            xc2 = [sbuf(stack, "xc%d" % i, [128, NT]) for i in range(2)]; xc2_t = [T(xc2[0]), T(xc2[1])]
            xcb2 = [sbuf(stack, "xcb%d" % i, [128, NT], BF16) for i in range(2)]; xcb2_t = [T(xcb2[0]), T(xcb2[1])]
            _gy = sbuf(stack, "rgy", [128, NT]); _gyt = T(_gy)
            gy = [_gy, _gy, _gy]; gy_t = [_gyt, _gyt, _gyt]
            dgr2 = [sbuf(stack, "dgr%d" % i, [128, 4, 128], BF16) for i in range(2)]; dgr2_t = [T(dgr2[0]), T(dgr2[1])]
            rb = [[sbuf(stack, "rb%d%d" % (q, d), [128, NT]) for d in range(2)] for q in range(2)]
            ib = [[sbuf(stack, "ib%d%d" % (q, d), [128, NT]) for d in range(2)] for q in range(2)]
            rb_t = [[T(rb[q][d]) for d in range(2)] for q in range(2)]
            ib_t = [[T(ib[q][d]) for d in range(2)] for q in range(2)]
            hv = [sbuf(stack, "rh%d" % d, [128, NT]) for d in range(2)]; hv_t = [T(hv[0]), T(hv[1])]
            hvb = hv[1][:, :].bitcast(BF16)
            rrv = rr_d.ap.rearrange("(c p) n -> p c n", p=128)
            o0 = NCTX if last else 0

            def pos(t):
                return 1 + t if t < NCTX else 260 + (t - NCTX)

            def front_x_tiles(g):
                W = wsl[g % 3]
                steps = []

                def mk(ti, a, b):
                    def step():
                        P.cur = "frontx%d" % g
                        if ti == 0:
                            for tap in range(4):
                                TS(dgr2[g % 2][:, tap, :], cst[:, CI:CI + 128], V("rcw%d" % l, tap * 8 + g), None, ALU.mult, ALU.bypass,
                                   [cst_t, vecs_t], [dgr2_t[g % 2]])
                        n = b - a
                        pb = banks[ti % 2]
                        for kc in range(8):
                            MM(pb[:, :n], W["xr"].ap[:, kc, :], hx[:, kc, a:b], kc == 0, kc == 7, [W["xr"], hx_t[ti]], [pb])
                        ACT(xrp[:, pos(a):pos(a) + n], pb[:, :n], AF.Copy, [pb], [xrp_t])
                    return step
                for ti, (a, b) in enumerate(TILES):
                    steps.append(mk(ti, a, b))
                return steps

            def front_x(g):
                for st_ in front_x_tiles(g):
                    st_()

            def front_c(g):
                P.cur = "frontc%d" % g
                W = wsl[g % 3]
                q_ = g % 2
                xc, xc_t, xcb, xcb_t = xc2[q_], xc2_t[q_], xcb2[q_], xcb2_t[q_]
                for ti, (a, b) in enumerate(TILES):
                    n = b - a
                    pb = banks[2 + (ti % 2)]
                    for tap in range(4):
                        o_ = pos(a) + tap - 1
                        MM(pb[:, :n], dgr2[g % 2][:, tap, :], xrp[:, o_:o_ + n], tap == 0, tap == 3, [dgr2_t[g % 2], xrp_t], [pb])
                    ACT(xc[:, a:b], pb[:, :n], AF.Identity, [pb, vecs_t], [xc_t], bias=V("rcb%d" % l, g))
                    DCOPY(xcb[:, a:b], xc[:, a:b], [xc_t], [xcb_t])

            def yr_tiles(g):
                W = wsl[g % 3]
                steps = []

                def mk(ti, a, b):
                    def step():
                        P.cur = "yr%d" % g
                        n = b - a
                        pb = banks[(4, 5, 6, 7, 0)[ti]]
                        for kc in range(8):
                            MM(pb[:, :n], W["yr"].ap[:, kc, :], hx[:, kc, a:b], kc == 0, kc == 7, [W["yr"], hx_t[ti]], [pb])
                        ACT(gy[g % 3][:, a:b], pb[:, :n], AF.Gelu_apprx_tanh, [pb], [gy_t[g % 3]])
                    return step
                for ti, (a, b) in enumerate(TILES):
                    if last and ti == 0:
                        continue
                    steps.append(mk(ti, a, b))
                return steps

            def gate_steps(g):
                W = wsl[g % 3]
                q_ = g % 2
                xc, xc_t, xcb, xcb_t = xc2[q_], xc2_t[q_], xcb2[q_], xcb2_t[q_]
                steps = []
                bi = [0]

                def mk(d, ti, a, b):
                    def step():
                        P.cur = "gates%d" % g
                        n = b - a
                        pr, pi = banks[4 + (bi[0] % 4)], banks[4 + ((bi[0] + 1) % 4)]
                        bi[0] += 2
                        MM(pr[:, :n], W["g"].ap[:, d, :], xcb[:, a:b], True, True, [W["g"], xcb_t], [pr])
                        MM(pi[:, :n], W["g"].ap[:, 2 + d, :], xcb[:, a:b], True, True, [W["g"], xcb_t], [pi])
                        ACT(rb[q_][d][:, a:b], pr[:, :n], AF.Sigmoid, [pr, vecs_t], [rb_t[q_][d]], bias=V("rba%d" % l, d * 8 + g))
                        ACT(ib[q_][d][:, a:b], pi[:, :n], AF.Sigmoid, [pi, vecs_t], [ib_t[q_][d]], bias=V("rbx%d" % l, d * 8 + g))
                        if ti == len(TILES) - 1:
                            TT(ib[q_][d][:, :], ib[q_][d][:, :], xc[:, :], ALU.mult, [ib_t[q_][d], xc_t], [ib_t[q_][d]])
                    return step
                for d in range(2):
                    for ti, (a, b) in enumerate(TILES):
                        steps.append(mk(d, ti, a, b))
                return steps

            def mid_steps(g):
                q_ = g % 2
                L0 = dv[:, DV_L + 0 * 8 + g:DV_L + 0 * 8 + g + 1]
                L1 = dv[:, DV_L + 1 * 8 + g:DV_L + 1 * 8 + g + 1]

                def s0():
                    P.cur = "mid%d" % g
                    ACT(rb[q_][0][:, :], rb[q_][0][:, :], AF.Exp, [rb_t[q_][0], dv_t], [rb_t[q_][0]], scale=L0)

                def s1():
                    P.cur = "mid%d" % g
                    ACT(rb[q_][1][:, :], rb[q_][1][:, :], AF.Exp, [rb_t[q_][1], dv_t], [rb_t[q_][1]], scale=L1)
                    for d in range(2):
                        TT(hv[d][:, :], rb[q_][d][:, :], rb[q_][d][:, :], ALU.mult, [rb_t[q_][d]], [hv_t[d]])

                def s2():
                    P.cur = "mid%d" % g
                    ACT(hv[0][:, :], hv[0][:, :], AF.Ln, [hv_t[0]], [hv_t[0]], bias=1.0, scale=-1.0)

                def s3():
                    P.cur = "mid%d" % g
                    ACT(hv[0][:, :], hv[0][:, :], AF.Exp, [hv_t[0]], [hv_t[0]], scale=0.5)

                def s4():
                    P.cur = "mid%d" % g
                    ACT(hv[1][:, :], hv[1][:, :], AF.Ln, [hv_t[1]], [hv_t[1]], bias=1.0, scale=-1.0)
                    TT(ib[q_][0][:, :], ib[q_][0][:, :], hv[0][:, :], ALU.mult, [ib_t[q_][0], hv_t[0]], [ib_t[q_][0]])

                def s5():
                    P.cur = "mid%d" % g
                    ACT(hv[1][:, :], hv[1][:, :], AF.Exp, [hv_t[1]], [hv_t[1]], scale=0.5)
                    TT(ib[q_][1][:, :], ib[q_][1][:, :], hv[1][:, :], ALU.mult, [ib_t[q_][1], hv_t[1]], [ib_t[q_][1]])
                return [s0, s1, s2, s3, s4, s5]

            def mid_a(g):
                P.cur = "mid_a%d" % g
                q_ = g % 2
                for d in range(2):
                    Lc = dv[:, DV_L + d * 8 + g:DV_L + d * 8 + g + 1]
                    ACT(rb[q_][d][:, :], rb[q_][d][:, :], AF.Exp, [rb_t[q_][d], dv_t], [rb_t[q_][d]], scale=Lc)
                for d in range(2):
                    TT(hv[d][:, :], rb[q_][d][:, :], rb[q_][d][:, :], ALU.mult, [rb_t[q_][d]], [hv_t[d]])

            def mid_b(g):
                P.cur = "mid_b%d" % g
                q_ = g % 2
                for d in range(2):
                    ACT(hv[d][:, :], hv[d][:, :], AF.Ln, [hv_t[d]], [hv_t[d]], bias=1.0, scale=-1.0)
                    ACT(hv[d][:, :], hv[d][:, :], AF.Exp, [hv_t[d]], [hv_t[d]], scale=0.5)
                for d in range(2):
                    TT(ib[q_][d][:, :], ib[q_][d][:, :], hv[d][:, :], ALU.mult, [ib_t[q_][d], hv_t[d]], [ib_t[q_][d]])

            def tail(g):
                P.cur = "tail%d" % g
                q_ = g % 2
                A0, A1, U0, U1 = rb[q_][0], rb[q_][1], ib[q_][0], ib[q_][1]
                SCAN(hv[0][:, :], A0[:, :], U0[:, :], 0.0, [rb_t[q_][0], ib_t[q_][0]], [hv_t[0]])
                SCAN(hv[1][:, 0:NCTX][:, ::-1], A1[:, 0:NCTX][:, ::-1], U1[:, 0:NCTX][:, ::-1], 0.0, [rb_t[q_][1], ib_t[q_][1]], [hv_t[1]])
                SCAN(hv[1][:, NCTX:NT][:, ::-1], A1[:, NCTX:NT][:, ::-1], U1[:, NCTX:NT][:, ::-1], hv[1][:, 0:1],
                     [rb_t[q_][1], ib_t[q_][1], hv_t[1]], [hv_t[1]])
                TT(hv[0][:, o0:NT], hv[0][:, o0:NT], hv[1][:, o0:NT], ALU.add, [hv_t[0], hv_t[1]], [hv_t[0]])
                TT(hvb[:, o0:NT], hv[0][:, o0:NT], gy[g % 3][:, o0:NT], ALU.mult, [hv_t[0], gy_t[g % 3]], [hv_t[1]])
                DMA("sp", rrv[:, g, o0:NT], hvb[:, o0:NT], [hv_t[1]], [rr_d])

            front_x(0)
            front_c(0)
            front_x(1)
            front_c(1)
            for g in range(8):
                fx = []
                if g + 2 < 8:
                    load_blk(g + 2)
                    fx = front_x_tiles(g + 2)
                for k_, st_ in enumerate(gate_steps(g)):
                    st_()
                    if k_ % 2 == 1 and fx:
                        fx.pop(0)()
                while fx:
                    fx.pop(0)()
                if g + 2 < 8:
                    front_c(g + 2)
                for st_ in mid_steps(g):
                    st_()
                for st_ in yr_tiles(g):
                    st_()
                tail(g)

        def merge_phase(l, last, src, dst, stack):
            win = kc_view(w_in[l])
            names = ["go", "ga", "ro", "gb", "out"]
            srcs = [kc_view(w_go[l]), win[:, :, C_GA:C_GA + D], kc_view(w_ro[l]), win[:, :, C_GB:C_GB + D], kc_view(w_out[l])]
            Wm = {}
            Wh = {}
            for nm, sv in zip(names, srcs):
                tns = sbuf(stack, "mw" + nm, [128, 8, D], BF16)
                Wm[nm] = T(tns)
                Wh[nm] = [T(tns[:, :, 0:512]), T(tns[:, :, 512:1024])]
            for half in range(2):
                for nm, sv in zip(names, srcs):
                    DMA("pool", Wm[nm].ap[:, :, half * 512:(half + 1) * 512], sv[:, :, half * 512:(half + 1) * 512], [dummy], [Wh[nm][half]])
            ogs2 = [sbuf(stack, "m_og%d" % i, [128, 8, 512], BF16) for i in range(2)]; ogs2_t = [T(ogs2[0]), T(ogs2[1])]
            rrs2 = [sbuf(stack, "m_rr%d" % i, [128, 8, 512], BF16) for i in range(2)]; rrs2_t = [T(rrs2[0]), T(rrs2[1])]
            xt = sbuf(stack, "m_x", [128, 8, 512]); xt_t = T(xt)
            xn, xn_t = xt, xt_t
            mg = sbuf(stack, "m_mg", [128, 8, 512], BF16); mg_t = T(mg)
            sa = sbuf(stack, "m_sa", [128, 512]); sa_t = T(sa)
            sb_ = sbuf(stack, "m_sb", [128, 512]); sb_t = T(sb_)
            m1 = sbuf(stack, "m_m1", [128, 512]); m1_t = T(m1)
            m2 = sbuf(stack, "m_m2", [128, 512]); m2_t = T(m2)
            rs = sbuf(stack, "m_rs", [128, 512])
            t1 = [sbuf(stack, "m_t1%d" % i, [128, 512]) for i in range(2)]
            tmp = (mg, mg_t, rs, T(rs), t1, [T(t1[0]), T(t1[1])])
            ogv = og_d.ap.rearrange("(c p) n -> p c n", p=128)
            rrv = rr_d.ap.rearrange("(c p) n -> p c n", p=128)
            srcv = kc_view(src.ap)
            dstv = kc_view(dst.ap)
            m_tiles = [ti for ti in range(len(TILES)) if not (last and ti == 0)]

            def m_loads(ix):
                ti_ = m_tiles[ix]
                a_, b_ = TILES[ti_]
                DMA("sp", ogs2[ix % 2][:, :, :b_ - a_], ogv[:, :, a_:b_], [og_d], [ogs2_t[ix % 2]])
                DMA("sp", rrs2[ix % 2][:, :, :b_ - a_], rrv[:, :, a_:b_], [rr_d], [rrs2_t[ix % 2]])

            def n2_square(tp):
                ap_, bp_ = TILES[tp]
                ACT(mg[:, :, :bp_ - ap_], xn[:, :, :bp_ - ap_], AF.Square, [xn_t], [mg_t])

            def n2_stat(tp):
                ap_, bp_ = TILES[tp]
                np_ = bp_ - ap_
                pb = banks[7]
                for kc in range(8):
                    MM(pb[:, :np_], ones_b[:, :], mg[:, kc, :np_], kc == 0, kc == 7, [ones_t, mg_t], [pb])

            def n2_rest(tp):
                ap_, bp_ = TILES[tp]
                np_ = bp_ - ap_
                sp_ = 0 if tp > 0 else 1
                rs_t_, t1l, t1l_t = tmp[3], tmp[4], tmp[5]
                pb = banks[7]
                ACT(rs[:, :np_], pb[:, :np_], AF.Ln, [pb], [rs_t_], bias=EPS, scale=1.0 / D)
                ACT(rs[:, :np_], rs[:, :np_], AF.Exp, [rs_t_], [rs_t_], scale=-0.5)
                for kc in range(8):
                    TT(t1l[kc % 2][:, :np_], xn[:, kc, :np_], rs[:, :np_], ALU.mult, [xn_t, rs_t_], [t1l_t[kc % 2]])
                    so = (DV_S2X if sp_ == 0 else DV_S2C) + kc
                    ACT(hx[:, kc, ap_:bp_], t1l[kc % 2][:, :np_], AF.Identity, [t1l_t[kc % 2], dv_t, mod_t], [hx_t[tp]],
                        bias=modcol(3, kc, sp_), scale=dv[:, so:so + 1])

            m_loads(0)
            for ix, ti in enumerate(m_tiles):
                a, b = TILES[ti]
                n = b - a
                s = 0 if ti > 0 else 1
                ogs, ogs_t = ogs2[ix % 2], ogs2_t[ix % 2]
                rrs, rrs_t = rrs2[ix % 2], rrs2_t[ix % 2]
                if ix + 1 < len(m_tiles):
                    m_loads(ix + 1)
                if ix > 0:
                    n2_square(m_tiles[ix - 1])
                else:
                    DMA("sp", xt[:, :, :n], srcv[:, :, a:b], [src], [xt_t])
                for j in range(8):
                    if j == 1 and ix > 0:
                        n2_rest(m_tiles[ix - 1])
                        DMA("sp", xt[:, :, :n], srcv[:, :, a:b], [src], [xt_t])
                    k4 = 4 * (j % 2)
                    pA, pGA, pB, pGB = banks[k4], banks[k4 + 1], banks[k4 + 2], banks[k4 + 3]
                    js = slice(j * 128, (j + 1) * 128)
                    for kc in range(8):
                        MM(pA[:, :n], Wm["go"].ap[:, kc, js], ogs[:, kc, :n], kc == 0, kc == 7, [Wh["go"][j // 4], ogs_t], [pA])
                    for kc in range(8):
                        MM(pGA[:, :n], Wm["ga"].ap[:, kc, js], hx[:, kc, a:b], kc == 0, kc == 7, [Wh["ga"][j // 4], hx_t[ti]], [pGA])
                    for kc in range(8):
                        MM(pB[:, :n], Wm["ro"].ap[:, kc, js], rrs[:, kc, :n], kc == 0, kc == 7, [Wh["ro"][j // 4], rrs_t], [pB])
                    for kc in range(8):
                        MM(pGB[:, :n], Wm["gb"].ap[:, kc, js], hx[:, kc, a:b], kc == 0, kc == 7, [Wh["gb"][j // 4], hx_t[ti]], [pGB])
                    if j == 0 and ix > 0:
                        n2_stat(m_tiles[ix - 1])
                    ACT(sa[:, :n], pGA[:, :n], AF.Sigmoid, [pGA], [sa_t])
                    ACT(sb_[:, :n], pGB[:, :n], AF.Sigmoid, [pGB], [sb_t])
                    TT(m1[:, :n], pA[:, :n], sa[:, :n], ALU.mult, [pA, sa_t], [m1_t])
                    TT(m2[:, :n], pB[:, :n], sb_[:, :n], ALU.mult, [pB, sb_t], [m2_t])
                    TT(mg[:, j, :n], m1[:, :n], m2[:, :n], ALU.add, [m1_t, m2_t], [mg_t])
                for j in range(8):
                    pM = banks[j % 4]
                    js = slice(j * 128, (j + 1) * 128)
                    for kc in range(8):
                        MM(pM[:, :n], Wm["out"].ap[:, kc, js], mg[:, kc, :n], kc == 0, kc == 7, [Wh["out"][j // 4], mg_t], [pM])
                    STT(xn[:, j, :n], pM[:, :n], modcol(2, j, s), xt[:, j, :n], ALU.mult, ALU.add, [pM, mod_t, xt_t], [xn_t])
                DMA("sp", dstv[:, :, a:b], xn[:, :, :n], [xn_t], [dst])
                if ix == len(m_tiles) - 1:
                    norm_tile(xn, xn_t, ti, (DV_S2X, DV_S2C), 3, hx_t[ti], hx[:, :, a:b], tmp)

        def ffn_phase(l, last, src, dst, stack):
            upv = kc_view(ffn_up[l])
            act = sbuf(stack, "f_act", [128, NHC, NT], BF16)
            act_t = [T(act[:, :, a:b]) for (a, b) in TILES]
            with ExitStack() as s2:
                wsl = [T(sbuf(s2, "fw%d" % i, [128, 8, 256], BF16)) for i in range(3)]

                def load_hc(hc):
                    w_ = wsl[hc % 3]
                    DMA("pool", w_.ap[:, :, 0:128], upv[:, :, hc * 128:(hc + 1) * 128], [dummy], [w_])
                    DMA("pool", w_.ap[:, :, 128:256], upv[:, :, FH + hc * 128:FH + (hc + 1) * 128], [dummy], [w_])
                load_hc(0)
                load_hc(1)
                apad = sbuf(s2, "f_apad", [128, 34, 66], BF16); apad_t = T(apad)
                cpad = sbuf(s2, "f_cpad", [128, 258], BF16); cpad_t = T(cpad)
                dg = [sbuf(s2, "f_dg%d" % i, [128, 9, 128], BF16) for i in range(2)]; dg_t = [T(dg[0]), T(dg[1])]
                gl = sbuf(s2, "f_gl", [128, 512]); gl_t = T(gl)
                MEMSET(apad[:], 0.0, [apad_t])
                MEMSET(cpad[:], 0.0, [cpad_t])
                for hc in range(NHC):
                    W = wsl[hc % 3]
                    if hc + 2 < NHC:
                        load_hc(hc + 2)
                    D_ = dg[hc % 2]
                    D_t = dg_t[hc % 2]
                    for tap in range(9):
                        TS(D_[:, tap, :], cst[:, CI:CI + 128], V("fcw%d" % l, tap * NHC + hc), None, ALU.mult, ALU.bypass, [cst_t, vecs_t], [D_t])
                    for ti, (a, b) in enumerate(TILES):
                        if last and ti == 0:
                            continue
                        n = b - a
                        pb = banks[ti % 2]
                        for kc in range(8):
                            MM(pb[:, :n], W.ap[:, kc, 0:128], hx[:, kc, a:b], kc == 0, kc == 7, [W, hx_t[ti]], [pb])
                        if ti == 0:
                            ACT(cpad[:, 1:257], pb[:, :n], AF.Copy, [pb], [cpad_t])
                        else:
                            r0 = (ti - 1) * 8
                            ACT(apad[:, r0 + 1:r0 + 9, 1:65], pb[:, :n].rearrange("p (r c) -> p r c", c=64), AF.Copy, [pb], [apad_t])
                    for ti, (a, b) in enumerate(TILES):
                        if last and ti == 0:
                            continue
                        n = b - a
                        pc = banks[2 + (ti % 2)]
                        pg = banks[4 + (ti % 2)]
                        if ti == 0:
                            for i_, dc in enumerate((-1, 0, 1)):
                                MM(pc[:, :n], D_[:, 3 + dc + 1, :], cpad[:, 1 + dc:257 + dc], i_ == 0, i_ == 2, [D_t, cpad_t], [pc])
                        else:
                            r0 = (ti - 1) * 8
                            i_ = 0
                            for dr in (-1, 0, 1):
                                for dc in (-1, 0, 1):
                                    tap = (dr + 1) * 3 + (dc + 1)
                                    MM(pc[:, :n].rearrange("p (r c) -> p r c", c=64), D_[:, tap, :],
                                       apad[:, r0 + 1 + dr:r0 + 9 + dr, 1 + dc:65 + dc], i_ == 0, i_ == 8, [D_t, apad_t], [pc])
                                    i_ += 1
                        for kc in range(8):
                            MM(pg[:, :n], W.ap[:, kc, 128:256], hx[:, kc, a:b], kc == 0, kc == 7, [W, hx_t[ti]], [pg])
                        ACT(gl[:, :n], pc[:, :n], AF.Gelu_apprx_tanh, [pc, vecs_t], [gl_t], bias=V("fcb%d" % l, hc))
                        TT(act[:, hc, a:b], pg[:, :n], gl[:, :n], ALU.mult, [pg, gl_t], [act_t[ti]])
            P.barrier()
            with ExitStack() as s3:
                dnv = ffn_dn[l].rearrange("(hc p) n -> p hc n", p=128)
                wdp_t = [T(wd[:, q:q + 2, :]) for q in range(0, NHC, 2)]
                for q in range(0, NHC, 2):
                    DMA("pool", wd[:, q:q + 2, :], dnv[:, q:q + 2, :], [dummy], [wdp_t[q // 2]] + (hx_t if q == 0 else []) + ([wd_t] if q == 0 else []))
                xt = sbuf(s3, "d_x", [128, 8, 512]); xt_t = T(xt)
                xn, xn_t = xt, xt_t
                yo, yo_t = xt, xt_t
                sq = sbuf(s3, "d_sq", [128, 8, 512], BF16); rs = sbuf(s3, "d_rs", [128, 512])
                t1 = [sbuf(s3, "d_t1%d" % i, [128, 512]) for i in range(2)]
                tmp = (sq, T(sq), rs, T(rs), t1, [T(t1[0]), T(t1[1])])
                srcv = kc_view(src.ap)
                for ti, (a, b) in enumerate(TILES):
                    if last and ti == 0:
                        continue
                    n = b - a
                    s = 0 if ti > 0 else 1
                    DMA("sp", xt[:, :, :n], srcv[:, :, a:b], [src], [xt_t])
                    for j in range(8):
                        pb = banks[j]
                        for hc in range(NHC):
                            MM(pb[:, :n], wd[:, hc, j * 128:(j + 1) * 128], act[:, hc, a:b], hc == 0, hc == NHC - 1, [wdp_t[hc // 2], wd_t, act_t[ti]], [pb])
                    for j in range(8):
                        STT(xn[:, j, :n], banks[j][:, :n], modcol(5, j, s), xt[:, j, :n], ALU.mult, ALU.add, [banks[j], mod_t, xt_t], [xn_t])
                    if not last:
                        DMA("sp", kc_view(dst.ap)[:, :, a:b], xn[:, :, :n], [xn_t], [dst])
                        if ti == len(TILES) - 1:
                            ACT(dv[:, 255:256], vecs[:, 0:1], AF.Copy, [vecs_t], [wd_t])
                    else:
                        norm_tile(xn, xn_t, ti, None, None, yo_t, yo, tmp, final=True)
                        ev = DMA("sp", kc_view(outT)[:, :, a - NCTX:b - NCTX], yo[:, :, :n], [yo_t], [outT_t])
                        out_evs.append(ev)

        for l in range(DEPTH):
            last = (l == DEPTH - 1)
            src0 = xs_s[2 * l]
            mid = xs_s[2 * l + 1]
            nxt = xs_s[2 * l + 2] if not last else None
            P.barrier()
            with ExitStack() as s1:
                wslots = [T(sbuf(s1, "aw%d" % i, [128, 4096], BF16)) for i in range(3)]
                ada_phase(l, s1, wslots)
            P.barrier()
            if debug:
                out_evs.append(DMA("sp", dbg["mod%d" % l], mod[:, :], [mod_t], [T(None)]))
                out_evs.append(DMA("sp", dbg["dv%d" % l], dv[:, :], [dv_t], [T(None)]))
            with ExitStack() as s1:
                norm1_phase(l, src0, s1)
            if debug:
                out_evs.append(DMA("sp", dbg["hx%d" % l], RR[:, 0:8 * NT], hx_t, [T(None)]))
            P.barrier()
            with ExitStack() as s1:
                gla_phase(l, last, s1)
            P.barrier()
            with ExitStack() as s1:
                rnn_phase(l, last, s1)
            P.barrier()
            if debug:
                out_evs.append(DMA("sp", dbg["og%d" % l], og_d.ap, [og_d], [T(None)]))
                out_evs.append(DMA("sp", dbg["rr%d" % l], rr_d.ap, [rr_d], [T(None)]))
            with ExitStack() as s1:
                merge_phase(l, last, src0, mid, s1)
            if debug:
                out_evs.append(DMA("sp", dbg["hx2_%d" % l], RR[:, 0:8 * NT], hx_t, [T(None)]))
            P.barrier()
            with ExitStack() as s1:
                ffn_phase(l, last, mid, nxt, s1)
        P.finish(out_evs)
        with nc.Block() as block:
            P.emit(sems, block)
    return nc


_NC_CACHE = {}


def kernel(**inp):
    inp = {k: np.asarray(v) for k, v in inp.items()}
    B = inp["x"].shape[0]
    if "nc" not in _NC_CACHE:
        _NC_CACHE["nc"] = build_nc()
    nc = _NC_CACHE["nc"]
    cst = _consts()
    shared = {
        "cst": cst,
        "ada_w": np.ascontiguousarray(inp["ada_w"], np.float32),
        "w_in": np.ascontiguousarray(inp["w_in"], np.float32),
        "gla_lr_w": np.ascontiguousarray(inp["gla_lr_w"], np.float32),
        "rnn_wa": np.ascontiguousarray(inp["rnn_wa"], np.float32),
        "rnn_wx": np.ascontiguousarray(inp["rnn_wx"], np.float32),
        "w_gla_o": np.ascontiguousarray(inp["w_gla_o"], np.float32),
        "w_rnn_o": np.ascontiguousarray(inp["w_rnn_o"], np.float32),
        "w_out": np.ascontiguousarray(inp["w_out"], np.float32),
        "ffn_up": np.ascontiguousarray(inp["ffn_up"], np.float32),
        "ffn_down": np.ascontiguousarray(inp["ffn_down"], np.float32),
    }
    in_maps = []
    for b in range(B):
        m = dict(shared)
        m["xs0"] = np.ascontiguousarray(np.concatenate([inp["ctx"][b].T, inp["x"][b].T], axis=1), np.float32)
        m["vecs"] = _pack_vecs(inp, b)
        in_maps.append(m)
    res = run_bass_kernel_spmd(nc, in_maps, core_ids=list(range(B)))
    out = np.stack([np.asarray(r["outT"]).T for r in res.results], axis=0)
    return np.ascontiguousarray(out, np.float32)
```

```python
import numpy as np
from contextlib import ExitStack
import concourse.bass as bass
import concourse.mybir as mybir
from concourse.bass_utils import run_bass_kernel_spmd

F32 = mybir.dt.float32
BF16 = mybir.dt.bfloat16
ALU = mybir.AluOpType
AF = mybir.ActivationFunctionType

D = 1024
NCTX = 256
SEQ = 2048
NT = NCTX + SEQ
DEPTH = 2
D_IN = 7200
FH = 2816
NHC = FH // 128
NCH = NT // 128
TILES = [(0, 256), (256, 768), (768, 1280), (1280, 1792), (1792, 2304)]
EPS = 1e-6
NDMA_SEM = 8

C_Q, C_K, C_V, C_G, C_LR, C_XR, C_YR, C_GA, C_GB = 0, 512, 1024, 2048, 3072, 3104, 4128, 5152, 6176


class T:
    __slots__ = ("ap", "w", "r", "name")

    def __init__(self, ap, name=""):
        self.ap = ap
        self.w = None
        self.r = []
        self.name = name

    def __getitem__(self, k):
        return self.ap[k]


class Eng:
    def __init__(self, name):
        self.name = name
        self.ops = []
        self.count = 0
        self.seen = {}
        self.dma_i = 0
        self.pending = {}


class Prog:
    def __init__(self, nc):
        self.nc = nc
        self.E = {n: Eng(n) for n in ("pe", "act", "dve", "pool", "sp")}
        self.semnames = ["s_" + n for n in self.E]
        for q in ("sp", "pool"):
            for j in range(NDMA_SEM):
                self.semnames.append("d_%s_%d" % (q, j))

    def _need(self, eng, ev, waits):
        if ev is None:
            return
        key, val, _ = ev
        if eng.seen.get(key, 0) >= val:
            return
        waits[key] = max(waits.get(key, 0), val)

    def op(self, engname, fn, reads=(), writes=()):
        eng = self.E[engname]
        waits = {}
        for b in reads:
            if b.w is not None and not (b.w[2] == engname and engname == "pe"):
                self._need(eng, b.w, waits)
        for b in writes:
            if b.w is not None and b.w[2] != engname:
                self._need(eng, b.w, waits)
            for ev in b.r:
                if ev[2] != engname:
                    self._need(eng, ev, waits)
        self._merge_pending(eng, waits)
        for k, v in waits.items():
            eng.seen[k] = v
        eng.count += 1
        key = "s_" + engname
        ev = (key, eng.count, engname)
        eng.ops.append((list(waits.items()), fn, (key, 1)))
        if not hasattr(self, "labels"):
            self.labels = {}
        self.labels.setdefault(engname, []).append(getattr(self, "cur", ""))
        for b in writes:
            b.w = ev
            b.r = []
        for b in reads:
            if b not in writes:
                b.r = [e for e in b.r if e[2] != engname] + [ev]
        return ev

    def dma(self, qname, fn, reads=(), writes=()):
        eng = self.E[qname]
        j = eng.dma_i % NDMA_SEM
        rnd = eng.dma_i // NDMA_SEM
        eng.dma_i += 1
        key = "d_%s_%d" % (qname, j)
        waits = {}
        if rnd > 0:
            self._need(eng, (key, 16 * rnd, "dma"), waits)
        for b in reads:
            self._need(eng, b.w, waits)
        for b in writes:
            self._need(eng, b.w, waits)
            for ev in b.r:
                self._need(eng, ev, waits)
        self._merge_pending(eng, waits)
        for k, v in waits.items():
            eng.seen[k] = v
        ev = (key, 16 * (rnd + 1), "dma_" + qname)
        eng.ops.append((list(waits.items()), fn, (key, 16)))
        for b in writes:
            b.w = ev
            b.r = []
        for b in reads:
            if b not in writes:
                b.r = b.r + [ev]
        return ev

    def _merge_pending(self, eng, waits):
        for k, v in eng.pending.items():
            if eng.seen.get(k, 0) < v:
                waits[k] = max(waits.get(k, 0), v)
        eng.pending = {}

    def mark(self, name):
        if not hasattr(self, "marks"):
            self.marks = []
        self.marks.append((name, {n: e.count for n, e in self.E.items()}))

    def barrier(self):
        snap = {}
        for n, e in self.E.items():
            if e.count > 0:
                snap["s_" + n] = e.count
            if n in ("sp", "pool"):
                for i in range(min(e.dma_i, NDMA_SEM)):
                    cnt = (e.dma_i - 1 - i) // NDMA_SEM + 1
                    snap["d_%s_%d" % (n, i)] = 16 * cnt
        for n, e in self.E.items():
            for k, v in snap.items():
                if k == "s_" + n:
                    continue
                e.pending[k] = max(e.pending.get(k, 0), v)

    def finish(self, evs):
        eng = self.E["sp"]
        waits = {}
        for ev in evs:
            self._need(eng, ev, waits)
        eng.ops.append((list(waits.items()), None, None))

    def emit(self, sems, block):
        hw = {"pe": "tensor", "act": "scalar", "dve": "vector", "pool": "gpsimd", "sp": "sync"}

        def mk(engname):
            eng = self.E[engname]

            def body(e):
                for waits, fn, inc in eng.ops:
                    for k, v in waits:
                        e.wait_ge(sems[k], v)
                    if fn is not None:
                        fn(e).then_inc(sems[inc[0]], inc[1])
            return body

        for n in self.E:
            if self.E[n].ops:
                getattr(block, hw[n])(mk(n))


def _vec_layout():
    off = {}
    n = 0

    def add(name, cols):
        nonlocal n
        off[name] = n
        n += cols
    add("c", 8)
    add("cctx", 8)
    add("fnw", 8)
    for l in range(DEPTH):
        add("adab%d" % l, 48)
        add("n1w%d" % l, 8)
        add("n2w%d" % l, 8)
        add("lrb%d" % l, 8)
        add("gnw%d" % l, 8)
        add("rcw%d" % l, 32)
        add("rcb%d" % l, 8)
        add("rba%d" % l, 16)
        add("rbx%d" % l, 16)
        add("rlam%d" % l, 16)
        add("fcw%d" % l, 9 * NHC)
        add("fcb%d" % l, NHC)
    return off, n


VOFF, NV = _vec_layout()
CI, CMF, CMB, CSF, CSB, NCST = 0, 128, 256, 384, 896, 1408


def _col(v):
    v = np.asarray(v, np.float32).reshape(-1, 128)
    return v.T


def _pack_vecs(inp, b):
    V = np.zeros((128, NV), np.float32)

    def put(name, arr):
        a = _col(arr)
        V[:, VOFF[name]:VOFF[name] + a.shape[1]] = a
    put("c", inp["c"][b])
    put("cctx", inp["c_ctx"])
    put("fnw", inp["final_norm_w"])
    for l in range(DEPTH):
        put("adab%d" % l, inp["ada_b"][l])
        put("n1w%d" % l, inp["norm1_w"][l])
        put("n2w%d" % l, inp["norm2_w"][l])
        put("lrb%d" % l, inp["gla_lr_b"][l])
        put("gnw%d" % l, inp["gla_norm_w"][l])
        put("rcw%d" % l, inp["rnn_conv_w"][l])
        put("rcb%d" % l, inp["rnn_conv_b"][l])
        put("rba%d" % l, inp["rnn_ba"][l])
        put("rbx%d" % l, inp["rnn_bx"][l])
        put("rlam%d" % l, inp["rnn_lambda"][l])
        put("fcw%d" % l, inp["ffn_conv_w"][l])
        put("fcb%d" % l, inp["ffn_conv_b"][l])
    return V


def _consts():
    C = np.zeros((128, NCST), np.float32)
    C[:, CI:CI + 128] = np.eye(128, dtype=np.float32)
    s = np.arange(128)[:, None]
    c = np.arange(128)[None, :]
    C[:, CMF:CMF + 128] = (s <= c)
    C[:, CMB:CMB + 128] = (s >= c)
    t = np.arange(512)
    C[:, CSF:CSF + 512] = (t % 128 != 0)[None, :]
    C[:, CSB:CSB + 512] = (t % 128 != 127)[None, :]
    return C


def build_nc(debug=False):
    nc = bass.Bass("TRN2", target_bir_lowering=False)
    P = Prog(nc)

    def din(name, shape):
        return nc.dram_tensor(name, list(shape), F32, kind="ExternalInput").ap()
    xs0 = din("xs0", [D, NT])
    vecs_d = din("vecs", [128, NV])
    cst_d = din("cst", [128, NCST])
    ada_w = din("ada_w", [DEPTH, D, 6 * D])
    w_in = din("w_in", [DEPTH, D, D_IN])
    lr_w = din("gla_lr_w", [DEPTH, 2, 16, 512])
    rnn_wa = din("rnn_wa", [DEPTH, 2, 8, 128, 128])
    rnn_wx = din("rnn_wx", [DEPTH, 2, 8, 128, 128])
    w_go = din("w_gla_o", [DEPTH, D, D])
    w_ro = din("w_rnn_o", [DEPTH, D, D])
    w_out = din("w_out", [DEPTH, D, D])
    ffn_up = din("ffn_up", [DEPTH, D, 2 * FH])
    ffn_dn = din("ffn_down", [DEPTH, FH, D])
    outT = nc.dram_tensor("outT", [D, SEQ], F32, kind="ExternalOutput").ap()
    skind = "ExternalOutput" if debug else "Internal"
    xs_s = [T(xs0, "xs0")] + [T(nc.dram_tensor("xs%d" % i, [D, NT], F32, kind=skind).ap(), "xs%d" % i) for i in (1, 2, 3)]
    og_d = T(nc.dram_tensor("og", [D, NT], BF16, kind=skind).ap(), "og")
    rr_d = T(nc.dram_tensor("rr", [D, NT], BF16, kind=skind).ap(), "rr")
    dbg = {}
    if debug:
        for l in range(DEPTH):
            dbg["mod%d" % l] = nc.dram_tensor("dbg_mod%d" % l, [128, 96], F32, kind="ExternalOutput").ap()
            dbg["dv%d" % l] = nc.dram_tensor("dbg_dv%d" % l, [128, 256], F32, kind="ExternalOutput").ap()
            dbg["hx%d" % l] = nc.dram_tensor("dbg_hx%d" % l, [128, 8 * NT], BF16, kind="ExternalOutput").ap()
            dbg["hx2_%d" % l] = nc.dram_tensor("dbg_hx2_%d" % l, [128, 8 * NT], BF16, kind="ExternalOutput").ap()
            dbg["og%d" % l] = nc.dram_tensor("dbg_og%d" % l, [D, NT], BF16, kind="ExternalOutput").ap()
            dbg["rr%d" % l] = nc.dram_tensor("dbg_rr%d" % l, [D, NT], BF16, kind="ExternalOutput").ap()
    outT_t = T(outT, "outT")
    dummy = T(None, "wdram")

    def kc_view(ap2d):
        return ap2d.rearrange("(kc p) n -> p kc n", p=128)

    out_evs = []
    st = ExitStack()
    with st:
        sems = {n: st.enter_context(nc.semaphore(n)) for n in P.semnames}

        _uid = [0]

        def sbuf(stack, name, shape, dt=F32):
            _uid[0] += 1
            return stack.enter_context(nc.sbuf_tensor("sb%d_%s" % (_uid[0], name), list(shape), dt))

        def ACT(out, in_, func, r, w, bias=0.0, scale=1.0):
            P.op("act", lambda e: e.activation(out=out, in_=in_, func=func, bias=bias, scale=scale), r, w)

        def TT(out, a, b, op, r, w):
            P.op("dve", lambda e: e.tensor_tensor(out, a, b, op), r, w)

        def TS(out, a, s1, s2, op0, op1, r, w):
            if s2 is None:
                P.op("dve", lambda e: e.tensor_scalar(out, a, s1, None, op0), r, w)
            else:
                P.op("dve", lambda e: e.tensor_scalar(out, a, s1, s2, op0, op1), r, w)

        def SCAN(out, d0, d1, init, r, w):
            P.op("dve", lambda e: e.tensor_tensor_scan(out, d0, d1, init, ALU.mult, ALU.add), r, w)

        def STT(out, a, s, b, op0, op1, r, w):
            P.op("dve", lambda e: e.scalar_tensor_tensor(out, a, s, b, op0, op1), r, w)

        def PCOPY(out, a, r, w):
            P.op("pool", lambda e: e.tensor_copy(out, a), r, w)

        def PTT(out, a, b, op, r, w):
            P.op("pool", lambda e: e.tensor_tensor(out, a, b, op), r, w)

        def DCOPY(out, a, r, w):
            P.op("dve", lambda e: e.tensor_copy(out, a), r, w)

        def MEMSET(out, v, w):
            P.op("dve", lambda e: e.memset(out, v), (), w)

        def MM(out, lhsT, rhs, start, stop, r, w):
            P.op("pe", lambda e: e.matmul(out, lhsT, rhs, start=start, stop=stop), r, w)

        def TR(out, in_, ident, r, w):
            P.op("pe", lambda e: e.transpose(out, in_, ident), r, w)

        def DMA(q, out, in_, r, w):
            return P.dma(q, lambda e: e.dma_start(out=out, in_=in_), r, w)

        vecs = sbuf(st, "vecs", [128, NV]); vecs_t = T(vecs, "vecs")
        cst = sbuf(st, "cst", [128, NCST]); cst_t = T(cst, "cst")
        cstb = sbuf(st, "cstb", [128, 128], BF16); cstb_t = T(cstb, "cstb")
        ones_b = sbuf(st, "ones_b", [128, 128], BF16); ones_t = T(ones_b, "ones")
        dv = sbuf(st, "dv", [128, 256]); dv_t = T(dv, "dv")
        mod = sbuf(st, "mod", [128, 96]); mod_t = T(mod, "mod")
        scb = sbuf(st, "scb", [128, 8, 2], BF16); scb_t = T(scb, "scb")
        RR = sbuf(st, "RR", [128, NHC * D], BF16)
        hx = RR[:, 0:8 * NT].rearrange("p (kc n) -> p kc n", kc=8)
        wd = RR[:, :].rearrange("p (hc n) -> p hc n", hc=NHC)
        wd_t = T(wd, "wd")
        hx_t = [T(hx[:, :, a:b], "hx%d" % i) for i, (a, b) in enumerate(TILES)]
        banks = [T(st.enter_context(nc.psum_tensor("pb%d" % i, [128, 512], F32)), "pb%d" % i) for i in range(8)]

        DMA("sp", vecs[:], vecs_d, [dummy], [vecs_t])
        DMA("sp", cst[:], cst_d, [dummy], [cst_t])
        DCOPY(cstb[:], cst[:, CI:CI + 128], [cst_t], [cstb_t])
        MEMSET(ones_b[:], 1.0, [ones_t])

        def V(name, i=0, n=1):
            o = VOFF[name] + i
            return vecs[:, o:o + n]

        def tile_of_chunk(n):
            return 0 if n < 2 else 1 + (n - 2) // 4

        DV_S1X, DV_S1C, DV_S2X, DV_S2C, DV_NLRB, DV_L, DV_SILU = 0, 8, 16, 24, 32, 40, 56

        def ada_phase(l, wst, wslots):
            ACT(scb[:, :, 0], V("c", 0, 8), AF.Silu, [vecs_t], [scb_t])
            ACT(scb[:, :, 1], V("cctx", 0, 8), AF.Silu, [vecs_t], [scb_t])
            pb = banks[0]
            aw = kc_view(ada_w[l])
            for grp in range(12):
                ws = wslots[grp % len(wslots)]
                DMA("pool", ws.ap[:, :].rearrange("p (kc n) -> p kc n", kc=8), aw[:, :, grp * 512:(grp + 1) * 512], [dummy], [ws])
                wv = ws.ap[:, :].rearrange("p (kc n) -> p kc n", kc=8)
                for jj in range(4):
                    j = grp * 4 + jj
                    for kc in range(8):
                        MM(pb[:, j * 2:j * 2 + 2], wv[:, kc, jj * 128:(jj + 1) * 128], scb[:, kc, :],
                           kc == 0, kc == 7, [ws, scb_t], [pb])
            m3 = mod[:, :].rearrange("p (j s) -> p j s", s=2)
            p3 = pb[:, 0:96].rearrange("p (j s) -> p j s", s=2)
            for s in range(2):
                TT(m3[:, :, s], p3[:, :, s], V("adab%d" % l, 0, 48), ALU.add, [pb, vecs_t], [mod_t])
            for s, (o1, o2) in enumerate(((DV_S1X, DV_S2X), (DV_S1C, DV_S2C))):
                STT(dv[:, o1:o1 + 8], m3[:, 8:16, s], 1.0, V("n1w%d" % l, 0, 8), ALU.add, ALU.mult, [mod_t, vecs_t], [dv_t])
                STT(dv[:, o2:o2 + 8], m3[:, 32:40, s], 1.0, V("n2w%d" % l, 0, 8), ALU.add, ALU.mult, [mod_t, vecs_t], [dv_t])
            TS(dv[:, DV_NLRB:DV_NLRB + 8], V("lrb%d" % l, 0, 8), -1.0, None, ALU.mult, ALU.bypass, [vecs_t], [dv_t])
            ACT(dv[:, DV_L:DV_L + 16], V("rlam%d" % l, 0, 16), AF.Exp, [vecs_t], [dv_t], scale=-1.0)
            ACT(dv[:, DV_L:DV_L + 16], dv[:, DV_L:DV_L + 16], AF.Ln, [dv_t], [dv_t], bias=1.0)
            TS(dv[:, DV_L:DV_L + 16], dv[:, DV_L:DV_L + 16], -8.0, None, ALU.mult, ALU.bypass, [dv_t], [dv_t])

        def modcol(part, kc, s):
            j = part * 8 + kc
            return mod[:, j * 2 + s:j * 2 + s + 1]

        def norm_tile(xt_ap, xt_T, ti, sc_off, sh_part, out_tile_T, out_ap, tmp, nw_scale_from_dv=True, final=False):
            a, b = TILES[ti]
            n = b - a
            s = 0 if ti > 0 else 1
            sq, sq_t, rs, rs_t, t1l, t1l_t = tmp
            ACT(sq[:, :, :n], xt_ap[:, :, :n], AF.Square, [xt_T], [sq_t])
            pb = banks[7]
            for kc in range(8):
                MM(pb[:, :n], ones_b[:, :], sq[:, kc, :n], kc == 0, kc == 7, [ones_t, sq_t], [pb])
            ACT(rs[:, :n], pb[:, :n], AF.Ln, [pb], [rs_t], bias=EPS, scale=1.0 / D)
            ACT(rs[:, :n], rs[:, :n], AF.Exp, [rs_t], [rs_t], scale=-0.5)
            for kc in range(8):
                t1, t1_t = t1l[kc % 2], t1l_t[kc % 2]
                TT(t1[:, :n], xt_ap[:, kc, :n], rs[:, :n], ALU.mult, [xt_T, rs_t], [t1_t])
                if final:
                    TS(out_ap[:, kc, :n], t1[:, :n], V("fnw", kc), None, ALU.mult, ALU.bypass, [t1_t, vecs_t], [out_tile_T])
                else:
                    so = (sc_off[0] if s == 0 else sc_off[1]) + kc
                    ACT(out_ap[:, kc, :n], t1[:, :n], AF.Identity, [t1_t, dv_t, mod_t], [out_tile_T],
                        bias=modcol(sh_part, kc, s), scale=dv[:, so:so + 1])

        def norm1_phase(l, src, stack):
            xt = [sbuf(stack, "n1x%d" % i, [128, 8, 512]) for i in range(2)]
            xt_T = [T(x, "n1x") for x in xt]
            sq = [sbuf(stack, "n1sq%d" % i, [128, 8, 512], BF16) for i in range(2)]; sq_t = [T(sq[0]), T(sq[1])]
            rs = [sbuf(stack, "n1rs%d" % i, [128, 512]) for i in range(2)]; rs_t = [T(rs[0]), T(rs[1])]
            t1 = [sbuf(stack, "n1t1%d" % i, [128, 512]) for i in range(2)]; t1_t = [T(t1[0]), T(t1[1])]
            srcv = kc_view(src.ap)

            def stat1(ti):
                a, b = TILES[ti]
                n = b - a
                k = ti % 2
                DMA("sp", xt[k][:, :, :n], srcv[:, :, a:b], [src], [xt_T[k]])
                ACT(sq[k][:, :, :n], xt[k][:, :, :n], AF.Square, [xt_T[k]], [sq_t[k]])
                pb = banks[6 + k]
                for kc in range(8):
                    MM(pb[:, :n], ones_b[:, :], sq[k][:, kc, :n], kc == 0, kc == 7, [ones_t, sq_t[k]], [pb])

            def stat2(ti):
                a, b = TILES[ti]
                n = b - a
                k = ti % 2
                pb = banks[6 + k]
                ACT(rs[k][:, :n], pb[:, :n], AF.Ln, [pb], [rs_t[k]], bias=EPS, scale=1.0 / D)
                ACT(rs[k][:, :n], rs[k][:, :n], AF.Exp, [rs_t[k]], [rs_t[k]], scale=-0.5)

            def apply(ti):
                a, b = TILES[ti]
                n = b - a
                k = ti % 2
                s_ = 0 if ti > 0 else 1
                for kc in range(8):
                    tt, tt_t = t1[kc % 2], t1_t[kc % 2]
                    TT(tt[:, :n], xt[k][:, kc, :n], rs[k][:, :n], ALU.mult, [xt_T[k], rs_t[k]], [tt_t])
                    so = (DV_S1X if s_ == 0 else DV_S1C) + kc
                    ACT(hx[:, kc, a:b], tt[:, :n], AF.Identity, [tt_t, dv_t, mod_t], [hx_t[ti]],
                        bias=modcol(0, kc, s_), scale=dv[:, so:so + 1])

            stat1(0)
            stat2(0)
            for ti in range(len(TILES)):
                if ti + 1 < len(TILES):
                    stat1(ti + 1)
                apply(ti)
                if ti + 1 < len(TILES):
                    stat2(ti + 1)

        def gla_phase(l, last, stack):
            win = kc_view(w_in[l])
            ws = []
            for i in range(2):
                d = {}
                for nm, cols in (("q", 128), ("k", 128), ("v", 256), ("g", 256)):
                    tns = sbuf(stack, "gw%s%d" % (nm, i), [128, 8, cols], BF16)
                    d[nm] = T(tns, "gw" + nm)
                ws.append(d)

            def load_head(h):
                d = ws[h % 2]
                DMA("pool", d["q"].ap[:], win[:, :, C_Q + h * 128:C_Q + (h + 1) * 128], [dummy], [d["q"]])
                DMA("pool", d["k"].ap[:], win[:, :, C_K + h * 128:C_K + (h + 1) * 128], [dummy], [d["k"]])
                DMA("pool", d["v"].ap[:], win[:, :, C_V + h * 256:C_V + (h + 1) * 256], [dummy], [d["v"]])
                DMA("pool", d["g"].ap[:], win[:, :, C_G + h * 256:C_G + (h + 1) * 256], [dummy], [d["g"]])
            wlrc = sbuf(stack, "wlrc", [128, 8, 32], BF16); wlrc_t = T(wlrc)
            wlr = sbuf(stack, "wlr", [16, 2, 512], BF16); wlr_t = T(wlr)
            DMA("pool", wlrc[:], win[:, :, C_LR:C_LR + 32], [dummy], [wlrc_t])
            DMA("pool", wlr[:], lr_w[l].rearrange("d r k -> r d k"), [dummy], [wlr_t])
            load_head(0)
            lrT = sbuf(stack, "lrT", [16, 2, NT], BF16); lrT_t = [T(lrT[:, :, a:b]) for (a, b) in TILES]
            for ti, (a, b) in enumerate(TILES):
                n = b - a
                for d in range(2):
                    pb = banks[d]
                    for kc in range(8):
                        MM(pb[0:16, :n], wlrc[:, kc, d * 16:(d + 1) * 16], hx[:, kc, a:b], kc == 0, kc == 7, [wlrc_t, hx_t[ti]], [pb])
                    ACT(lrT[:, d, a:b], pb[0:16, :n], AF.Identity, [pb], [lrT_t[ti]])

            qd = [sbuf(stack, "qd%d" % d, [128, NT], BF16) for d in range(2)]
            ki = [sbuf(stack, "ki%d" % d, [128, NT], BF16) for d in range(2)]
            qd_t = [[T(qd[d][:, a:b]) for (a, b) in TILES] for d in range(2)]
            ki_t = [[T(ki[d][:, a:b]) for (a, b) in TILES] for d in range(2)]
            kt = [sbuf(stack, "kt%d" % d, [128, NCH, 128], BF16) for d in range(2)]
            kt_t = [[T(kt[d][:, 0:1, :]) for _ in TILES] for d in range(2)]
            vt = sbuf(stack, "vt", [128, NCH, 256], BF16); vt_t = [T(vt[:, 0:1, :]) for _ in TILES]
            sb_ = [sbuf(stack, "sb%d" % d, [128, NCH, 256], BF16) for d in range(2)]
            sb_t = [[T(sb_[d][:, n, :]) for n in range(NCH)] for d in range(2)]
            el = sbuf(stack, "el", [128, 2, NCH]); el_t = [[T(el[:, d, 0:1]) for _ in TILES] for d in range(2)]
            S = [sbuf(stack, "S%d" % d, [128, 256]) for d in range(2)]; S_t = [T(S[0]), T(S[1])]
            Stmp = [sbuf(stack, "Stmp%d" % d, [128, 256]) for d in range(2)]; Stmp_t = [T(Stmp[0]), T(Stmp[1])]
            tm = {}
            for d in range(2):
                for p_ in range(2):
                    for nm in ("A", "B", "C"):
                        tns = sbuf(stack, "g%s%d%d" % (nm, d, p_), [128, 512])
                        tm[(nm, d, p_)] = (tns, T(tns))
            scm_all = sbuf(stack, "scm_all", [128, 2, NCH, 128], BF16)
            scm_t = [[T(scm_all[:, d, n, :]) for n in range(NCH)] for d in range(2)]
            sq = sbuf(stack, "gsq", [128, 2, 512], BF16); sq_t = T(sq)
            rs = sbuf(stack, "grs", [128, 512]); rs_t = T(rs)
            sg = sbuf(stack, "gsg", [128, 2, 512]); sg_t = T(sg)
            t1 = sbuf(stack, "gt1", [128, 512]); t1_t = T(t1)
            ogt = [sbuf(stack, "ogt%d" % i, [128, 2, 512], BF16) for i in range(2)]; ogt_t = [T(ogt[0]), T(ogt[1])]
            ogv = og_d.ap.rearrange("(c p) n -> p c n", p=128)
            og_i = 0
            border = [1, 0] + list(range(NCH - 1, 1, -1))
            forder = list(range(NCH))

            kdt = {}
            for d in range(2):
                for p_ in range(2):
                    tns = sbuf(stack, "gkd%d%d" % (d, p_), [128, 512], BF16)
                    kdt[(d, p_)] = (tns, T(tns))

            for h in range(4):
                W = ws[h % 2]
                if h + 1 < 4:
                    load_head(h + 1)
                P.mark("gla_prep")

                def stage1(ti):
                    a, b = TILES[ti]
                    n = b - a
                    nch = n // 128
                    c0 = a // 128
                    p_ = ti % 2
                    pq, pk = banks[0 + p_], banks[2 + p_]
                    for kc in range(8):
                        MM(pq[:, :n], W["q"].ap[:, kc, :], hx[:, kc, a:b], kc == 0, kc == 7, [W["q"], hx_t[ti]], [pq])
                    for kc in range(8):
                        MM(pk[:, :n], W["k"].ap[:, kc, :], hx[:, kc, a:b], kc == 0, kc == 7, [W["k"], hx_t[ti]], [pk])
                    for j in range(nch):
                        pv = banks[6 + (j // 2) % 2]
                        hs_ = (j % 2) * 256
                        for kc in range(8):
                            MM(pv[:, hs_:hs_ + 256], hx[:, kc, a + j * 128:a + (j + 1) * 128], W["v"].ap[:, kc, :], kc == 0, kc == 7, [hx_t[ti], W["v"]], [pv])
                        if j % 2 == 1 or j == nch - 1:
                            j0 = j - (j % 2)
                            w_ = (j - j0 + 1) * 256
                            DCOPY(vt[:, c0 + j0:c0 + j + 1, :], pv[:, 0:w_].rearrange("p (c v) -> p c v", v=256), [pv], [vt_t[ti]])
                    for d in range(2):
                        pl = banks[4]
                        MM(pl[:, :n], wlr[:, d, h * 128:(h + 1) * 128], lrT[:, d, a:b], True, True, [wlr_t, lrT_t[ti]], [pl])
                        A_, A_t = tm[("A", d, p_)]
                        nb = dv[:, DV_NLRB + d * 4 + h:DV_NLRB + d * 4 + h + 1]
                        ACT(A_[:, :n], pl[:, :n], AF.Exp, [pl, dv_t], [A_t], bias=nb, scale=-1.0)

                def stage2(ti):
                    a, b = TILES[ti]
                    n = b - a
                    nch = n // 128
                    c0 = a // 128
                    p_ = ti % 2
                    pq, pk = banks[0 + p_], banks[2 + p_]
                    X = [(tm[("A", d, p_)], tm[("B", d, p_)], tm[("C", d, p_)]) for d in range(2)]
                    for d in range(2):
                        (A_, A_t), (B_, B_t), (C_, C_t) = X[d]
                        ACT(B_[:, :n], A_[:, :n], AF.Ln, [A_t], [B_t], bias=1.0)
                    for d in range(2):
                        (A_, A_t), (B_, B_t), (C_, C_t) = X[d]
                        if d == 0:
                            SCAN(C_[:, :n], cst[:, CSF:CSF + n], B_[:, :n], 0.0, [cst_t, B_t], [C_t])
                        else:
                            SCAN(C_[:, :n][:, ::-1], cst[:, CSB + 512 - n:CSB + 512][:, ::-1], B_[:, :n][:, ::-1], 0.0, [cst_t, B_t], [C_t])
                    for d in range(2):
                        (A_, A_t), (B_, B_t), (C_, C_t) = X[d]
                        if d == 0:
                            ACT(el[:, 0, c0:c0 + nch], C_[:, 127:n:128], AF.Exp, [C_t], [el_t[0][ti]], scale=-1.0 / 16)
                        else:
                            ACT(el[:, 1, c0:c0 + nch], C_[:, 0:n:128], AF.Exp, [C_t], [el_t[1][ti]], scale=-1.0 / 16)
                        ACT(A_[:, :n], C_[:, :n], AF.Exp, [C_t], [A_t], scale=-1.0 / 16)
                        ACT(B_[:, :n], C_[:, :n], AF.Exp, [C_t], [B_t], scale=1.0 / 16)
                    for d in range(2):
                        (A_, A_t), (B_, B_t), (C_, C_t) = X[d]
                        kd, kd_t = kdt[(d, p_)]
                        STT(qd[d][:, a:b], pq[:, :n], 128.0 ** -0.5, A_[:, :n], ALU.mult, ALU.mult, [pq, A_t], [qd_t[d][ti]])
                        TT(ki[d][:, a:b], pk[:, :n], B_[:, :n], ALU.mult, [pk, B_t], [ki_t[d][ti]])
                        TT(kd[:, :n].rearrange("p (c k) -> p c k", k=128), ki[d][:, a:b].rearrange("p (c k) -> p c k", k=128),
                           el[:, d, c0:c0 + nch].to_broadcast([128, nch, 128]) if False else el[:, d, c0:c0 + nch, None].to_broadcast([128, nch, 128]),
                           ALU.mult, [ki_t[d][ti], el_t[d][ti]], [kd_t])

                def stage3(ti):
                    a, b = TILES[ti]
                    n = b - a
                    nch = n // 128
                    c0 = a // 128
                    p_ = ti % 2
                    ptr = banks[5]
                    ptb = ptr.ap[:, :].bitcast(BF16)
                    for d in range(2):
                        kd, kd_t = kdt[(d, p_)]
                        for j in range(nch):
                            TR(ptb[:, d * 512 + j * 128:d * 512 + (j + 1) * 128], kd[:, j * 128:(j + 1) * 128], cstb[:, :], [kd_t, cstb_t], [ptr])
                    for d in range(2):
                        ACT(kt[d][:, c0:c0 + nch, :], ptb[:, d * 512:d * 512 + nch * 128].rearrange("p (c k) -> p c k", k=128), AF.Copy,
                            [ptr], [kt_t[d][ti]])

                NTI = len(TILES)
                stage1(0)
                for ti in range(NTI):
                    stage2(ti)
                    if ti + 1 < NTI:
                        stage1(ti + 1)
                    stage3(ti)

                P.mark("gla_state")
                chunks = [n_ for n_ in range(NCH) if not (last and n_ < 2)]

                def scores(n_):
                    P.cur = "sc%d_%d" % (h, n_)
                    ti = tile_of_chunk(n_)
                    cs = slice(n_ * 128, (n_ + 1) * 128)
                    p_ = n_ % 2
                    psc = banks[4 + p_]
                    for d in range(2):
                        MM(psc[:, d * 128:(d + 1) * 128], ki[d][:, cs], qd[d][:, cs], True, True, [ki_t[d][ti], qd_t[d][ti]], [psc])
                    for d in range(2):
                        TT(scm_all[:, d, n_, :], psc[:, d * 128:(d + 1) * 128], cst[:, (CMF, CMB)[d]:(CMF, CMB)[d] + 128], ALU.mult, [psc, cst_t], [scm_t[d][n_]])

                for d in range(2):
                    MEMSET(S[d][:], 0.0, [S_t[d]])
                for idx in range(NCH):
                    if idx < len(chunks):
                        scores(chunks[idx])
                    P.cur = "state"
                    for d, order in ((1, border), (0, forder)):
                        n_ = order[idx]
                        ti = tile_of_chunk(n_)
                        ACT(sb_[d][:, n_, :], S[d][:], AF.Copy, [S_t[d]], [sb_t[d][n_]])
                        if idx == NCH - 1:
                            continue
                        pp = banks[2 * d + (idx % 2)]
                        MM(pp[:, 0:256], kt[d][:, n_, :], vt[:, n_, :], True, True, [kt_t[d][ti], vt_t[ti]], [pp])
                        STT(S[d][:], S[d][:], el[:, d, n_:n_ + 1], pp[:, 0:256], ALU.mult, ALU.add, [S_t[d], el_t[d][ti], pp], [S_t[d]])

                P.mark("gla_out")
                def outs(n_):
                    P.cur = "out%d_%d" % (h, n_)
                    ti = tile_of_chunk(n_)
                    a, b = TILES[ti]
                    j = n_ - a // 128
                    ob = [banks[0 + 2 * (ti % 2)], banks[1 + 2 * (ti % 2)]]
                    cs = slice(n_ * 128, (n_ + 1) * 128)
                    p_ = n_ % 2
                    for vh in range(2):
                        o = ob[vh][:, j * 128:(j + 1) * 128]
                        vs = slice(vh * 128, (vh + 1) * 128)
                        MM(o, sb_[0][:, n_, vs], qd[0][:, cs], True, False, [sb_t[0][n_], qd_t[0][ti]], [ob[vh]])
                        MM(o, sb_[1][:, n_, vs], qd[1][:, cs], False, False, [sb_t[1][n_], qd_t[1][ti]], [ob[vh]])
                        MM(o, vt[:, n_, vs], scm_all[:, 0, n_, :], False, False, [vt_t[ti], scm_t[0][n_]], [ob[vh]])
                        MM(o, vt[:, n_, vs], scm_all[:, 1, n_, :], False, True, [vt_t[ti], scm_t[1][n_]], [ob[vh]])

                def epilogue(ti, k2):
                    P.cur = "epi%d_%d" % (h, ti)
                    a, b = TILES[ti]
                    nn = b - a
                    ob = [banks[0 + 2 * (ti % 2)], banks[1 + 2 * (ti % 2)]]
                    pg = banks[7]
                    pss = banks[6]
                    for vh in range(2):
                        ACT(sq[:, vh, :nn], ob[vh][:, :nn], AF.Square, [ob[vh]], [sq_t])
                    for kc in range(8):
                        MM(pg[:, :nn], W["g"].ap[:, kc, 0:128], hx[:, kc, a:b], kc == 0, kc == 7, [W["g"], hx_t[ti]], [pg])
                    yield
                    P.cur = "epi%d_%d" % (h, ti)
                    for vh in range(2):
                        MM(pss[:, :nn], ones_b[:, :], sq[:, vh, :nn], vh == 0, vh == 1, [ones_t, sq_t], [pss])
                    ACT(rs[:, :nn], pss[:, :nn], AF.Ln, [pss], [rs_t], bias=EPS, scale=1.0 / 256)
                    ACT(rs[:, :nn], rs[:, :nn], AF.Exp, [rs_t], [rs_t], scale=-0.5)
                    ACT(sg[:, 0, :nn], pg[:, :nn], AF.Silu, [pg], [sg_t])
                    TT(t1[:, :nn], ob[0][:, :nn], rs[:, :nn], ALU.mult, [ob[0], rs_t], [t1_t])
                    STT(ogt[k2][:, 0, :nn], t1[:, :nn], V("gnw%d" % l, h * 2 + 0), sg[:, 0, :nn], ALU.mult, ALU.mult,
                        [t1_t, vecs_t, sg_t], [ogt_t[k2]])
                    yield
                    P.cur = "epi%d_%d" % (h, ti)
                    for kc in range(8):
                        MM(pg[:, :nn], W["g"].ap[:, kc, 128:256], hx[:, kc, a:b], kc == 0, kc == 7, [W["g"], hx_t[ti]], [pg])
                    ACT(sg[:, 1, :nn], pg[:, :nn], AF.Silu, [pg], [sg_t])
                    TT(t1[:, :nn], ob[1][:, :nn], rs[:, :nn], ALU.mult, [ob[1], rs_t], [t1_t])
                    STT(ogt[k2][:, 1, :nn], t1[:, :nn], V("gnw%d" % l, h * 2 + 1), sg[:, 1, :nn], ALU.mult, ALU.mult,
                        [t1_t, vecs_t, sg_t], [ogt_t[k2]])
                    DMA("sp", ogv[:, h * 2:h * 2 + 2, a:b], ogt[k2][:, :, :nn], [ogt_t[k2]], [og_d])

                pending = []

                def advance():
                    if pending:
                        try:
                            next(pending[0])
                        except StopIteration:
                            pending.pop(0)
                            advance()

                for ci, n_ in enumerate(chunks):
                    outs(n_)
                    advance()
                    ti = tile_of_chunk(n_)
                    if (n_ + 1) * 128 == TILES[ti][1]:
                        pending.append(epilogue(ti, og_i % 2))
                        og_i += 1
                while pending:
                    advance()

        def rnn_phase(l, last, stack):
            win = kc_view(w_in[l])
            wsl = []
            for i in range(3):
                d = {}
                d["xr"] = T(sbuf(stack, "rwx%d" % i, [128, 8, 128], BF16))
                d["yr"] = T(sbuf(stack, "rwy%d" % i, [128, 8, 128], BF16))
                d["g"] = T(sbuf(stack, "rwg%d" % i, [128, 4, 128], BF16))
                wsl.append(d)

            def load_blk(g):
                d = wsl[g % 3]
                DMA("pool", d["xr"].ap[:], win[:, :, C_XR + g * 128:C_XR + (g + 1) * 128], [dummy], [d["xr"]])
                DMA("pool", d["yr"].ap[:], win[:, :, C_YR + g * 128:C_YR + (g + 1) * 128], [dummy], [d["yr"]])
                DMA("pool", d["g"].ap[:, 0:2, :], rnn_wa[l, :, g].rearrange("d i j -> i d j"), [dummy], [d["g"]])
                DMA("pool", d["g"].ap[:, 2:4, :], rnn_wx[l, :, g].rearrange("d i j -> i d j"), [dummy], [d["g"]])
            load_blk(0)
            load_blk(1)
            XP = 2312
            xrp = sbuf(stack, "xrp", [128, XP], BF16); xrp_t = T(xrp)
            MEMSET(xrp[:], 0.0, [xrp_t])
            xc2 = [sbuf(stack, "xc%d" % i, [128, NT]) for i in range(2)]; xc2_t = [T(xc2[0]), T(xc2[1])]
            xcb2 = [sbuf(stack, "xcb%d" % i, [128, NT], BF16) for i in range(2)]; xcb2_t = [T(xcb2[0]), T(xcb2[1])]
            _gy = sbuf(stack, "rgy", [128, NT]); _gyt = T(_gy)
            gy = [_gy, _gy, _gy]; gy_t = [_gyt, _gyt, _gyt]
            dgr2 = [sbuf(stack, "dgr%d" % i, [128, 4, 128], BF16) for i in range(2)]; dgr2_t = [T(dgr2[0]), T(dgr2[1])]
            rb = [[sbuf(stack, "rb%d%d" % (q, d), [128, NT]) for d in range(2)] for q in range(2)]
            ib = [[sbuf(stack, "ib%d%d" % (q, d), [128, NT]) for d in range(2)] for q in range(2)]
            rb_t = [[T(rb[q][d]) for d in range(2)] for q in range(2)]
            ib_t = [[T(ib[q][d]) for d in range(2)] for q in range(2)]
            hv = [sbuf(stack, "rh%d" % d, [128, NT]) for d in range(2)]; hv_t = [T(hv[0]), T(hv[1])]
            hvb = hv[1][:, :].bitcast(BF16)
            rrv = rr_d.ap.rearrange("(c p) n -> p c n", p=128)
            o0 = NCTX if last else 0

            def pos(t):
                return 1 + t if t < NCTX else 260 + (t - NCTX)

            def front_x_tiles(g):
                W = wsl[g % 3]
                steps = []

                def mk(ti, a, b):
                    def step():
                        P.cur = "frontx%d" % g
                        if ti == 0:
                            for tap in range(4):
                                TS(dgr2[g % 2][:, tap, :], cst[:, CI:CI + 128], V("rcw%d" % l, tap * 8 + g), None, ALU.mult, ALU.bypass,
                                   [cst_t, vecs_t], [dgr2_t[g % 2]])
                        n = b - a
                        pb = banks[ti % 2]
                        for kc in range(8):
                            MM(pb[:, :n], W["xr"].ap[:, kc, :], hx[:, kc, a:b], kc == 0, kc == 7, [W["xr"], hx_t[ti]], [pb])
                        ACT(xrp[:, pos(a):pos(a) + n], pb[:, :n], AF.Copy, [pb], [xrp_t])
                    return step
                for ti, (a, b) in enumerate(TILES):
                    steps.append(mk(ti, a, b))
                return steps

            def front_x(g):
                for st_ in front_x_tiles(g):
                    st_()

            def front_c(g):
                P.cur = "frontc%d" % g
                W = wsl[g % 3]
                q_ = g % 2
                xc, xc_t, xcb, xcb_t = xc2[q_], xc2_t[q_], xcb2[q_], xcb2_t[q_]
                for ti, (a, b) in enumerate(TILES):
                    n = b - a
                    pb = banks[2 + (ti % 2)]
                    for tap in range(4):
                        o_ = pos(a) + tap - 1
                        MM(pb[:, :n], dgr2[g % 2][:, tap, :], xrp[:, o_:o_ + n], tap == 0, tap == 3, [dgr2_t[g % 2], xrp_t], [pb])
                    ACT(xc[:, a:b], pb[:, :n], AF.Identity, [pb, vecs_t], [xc_t], bias=V("rcb%d" % l, g))
                    DCOPY(xcb[:, a:b], xc[:, a:b], [xc_t], [xcb_t])

            def yr_tiles(g):
                W = wsl[g % 3]
                steps = []

                def mk(ti, a, b):
                    def step():
                        P.cur = "yr%d" % g
                        n = b - a
                        pb = banks[(4, 5, 6, 7, 0)[ti]]
                        for kc in range(8):
                            MM(pb[:, :n], W["yr"].ap[:, kc, :], hx[:, kc, a:b], kc == 0, kc == 7, [W["yr"], hx_t[ti]], [pb])
                        ACT(gy[g % 3][:, a:b], pb[:, :n], AF.Gelu_apprx_tanh, [pb], [gy_t[g % 3]])
                    return step
                for ti, (a, b) in enumerate(TILES):
                    if last and ti == 0:
                        continue
                    steps.append(mk(ti, a, b))
                return steps

            def gate_steps(g):
                W = wsl[g % 3]
                q_ = g % 2
                xc, xc_t, xcb, xcb_t = xc2[q_], xc2_t[q_], xcb2[q_], xcb2_t[q_]
                steps = []
                bi = [0]

                def mk(d, ti, a, b):
                    def step():
                        P.cur = "gates%d" % g
                        n = b - a
                        pr, pi = banks[4 + (bi[0] % 4)], banks[4 + ((bi[0] + 1) % 4)]
                        bi[0] += 2
                        MM(pr[:, :n], W["g"].ap[:, d, :], xcb[:, a:b], True, True, [W["g"], xcb_t], [pr])
                        MM(pi[:, :n], W["g"].ap[:, 2 + d, :], xcb[:, a:b], True, True, [W["g"], xcb_t], [pi])
                        ACT(rb[q_][d][:, a:b], pr[:, :n], AF.Sigmoid, [pr, vecs_t], [rb_t[q_][d]], bias=V("rba%d" % l, d * 8 + g))
                        ACT(ib[q_][d][:, a:b], pi[:, :n], AF.Sigmoid, [pi, vecs_t], [ib_t[q_][d]], bias=V("rbx%d" % l, d * 8 + g))
                        if ti == len(TILES) - 1:
                            TT(ib[q_][d][:, :], ib[q_][d][:, :], xc[:, :], ALU.mult, [ib_t[q_][d], xc_t], [ib_t[q_][d]])
                    return step
                for d in range(2):
                    for ti, (a, b) in enumerate(TILES):
                        steps.append(mk(d, ti, a, b))
                return steps

            def mid_steps(g):
                q_ = g % 2
                L0 = dv[:, DV_L + 0 * 8 + g:DV_L + 0 * 8 + g + 1]
                L1 = dv[:, DV_L + 1 * 8 + g:DV_L + 1 * 8 + g + 1]

                def s0():
                    P.cur = "mid%d" % g
                    ACT(rb[q_][0][:, :], rb[q_][0][:, :], AF.Exp, [rb_t[q_][0], dv_t], [rb_t[q_][0]], scale=L0)

                def s1():
                    P.cur = "mid%d" % g
                    ACT(rb[q_][1][:, :], rb[q_][1][:, :], AF.Exp, [rb_t[q_][1], dv_t], [rb_t[q_][1]], scale=L1)
                    for d in range(2):
                        TT(hv[d][:, :], rb[q_][d][:, :], rb[q_][d][:, :], ALU.mult, [rb_t[q_][d]], [hv_t[d]])

                def s2():
                    P.cur = "mid%d" % g
                    ACT(hv[0][:, :], hv[0][:, :], AF.Ln, [hv_t[0]], [hv_t[0]], bias=1.0, scale=-1.0)

                def s3():
                    P.cur = "mid%d" % g
                    ACT(hv[0][:, :], hv[0][:, :], AF.Exp, [hv_t[0]], [hv_t[0]], scale=0.5)

                def s4():
                    P.cur = "mid%d" % g
                    ACT(hv[1][:, :], hv[1][:, :], AF.Ln, [hv_t[1]], [hv_t[1]], bias=1.0, scale=-1.0)
                    TT(ib[q_][0][:, :], ib[q_][0][:, :], hv[0][:, :], ALU.mult, [ib_t[q_][0], hv_t[0]], [ib_t[q_][0]])

                def s5():
                    P.cur = "mid%d" % g
                    ACT(hv[1][:, :], hv[1][:, :], AF.Exp, [hv_t[1]], [hv_t[1]], scale=0.5)
                    TT(ib[q_][1][:, :], ib[q_][1][:, :], hv[1][:, :], ALU.mult, [ib_t[q_][1], hv_t[1]], [ib_t[q_][1]])
                return [s0, s1, s2, s3, s4, s5]

            def mid_a(g):
                P.cur = "mid_a%d" % g
                q_ = g % 2
                for d in range(2):
                    Lc = dv[:, DV_L + d * 8 + g:DV_L + d * 8 + g + 1]
                    ACT(rb[q_][d][:, :], rb[q_][d][:, :], AF.Exp, [rb_t[q_][d], dv_t], [rb_t[q_][d]], scale=Lc)
                for d in range(2):
                    TT(hv[d][:, :], rb[q_][d][:, :], rb[q_][d][:, :], ALU.mult, [rb_t[q_][d]], [hv_t[d]])

            def mid_b(g):
                P.cur = "mid_b%d" % g
                q_ = g % 2
                for d in range(2):
                    ACT(hv[d][:, :], hv[d][:, :], AF.Ln, [hv_t[d]], [hv_t[d]], bias=1.0, scale=-1.0)
                    ACT(hv[d][:, :], hv[d][:, :], AF.Exp, [hv_t[d]], [hv_t[d]], scale=0.5)
                for d in range(2):
                    TT(ib[q_][d][:, :], ib[q_][d][:, :], hv[d][:, :], ALU.mult, [ib_t[q_][d], hv_t[d]], [ib_t[q_][d]])

            def tail(g):
                P.cur = "tail%d" % g
                q_ = g % 2
                A0, A1, U0, U1 = rb[q_][0], rb[q_][1], ib[q_][0], ib[q_][1]
                SCAN(hv[0][:, :], A0[:, :], U0[:, :], 0.0, [rb_t[q_][0], ib_t[q_][0]], [hv_t[0]])
                SCAN(hv[1][:, 0:NCTX][:, ::-1], A1[:, 0:NCTX][:, ::-1], U1[:, 0:NCTX][:, ::-1], 0.0, [rb_t[q_][1], ib_t[q_][1]], [hv_t[1]])
                SCAN(hv[1][:, NCTX:NT][:, ::-1], A1[:, NCTX:NT][:, ::-1], U1[:, NCTX:NT][:, ::-1], hv[1][:, 0:1],
                     [rb_t[q_][1], ib_t[q_][1], hv_t[1]], [hv_t[1]])
                TT(hv[0][:, o0:NT], hv[0][:, o0:NT], hv[1][:, o0:NT], ALU.add, [hv_t[0], hv_t[1]], [hv_t[0]])
                TT(hvb[:, o0:NT], hv[0][:, o0:NT], gy[g % 3][:, o0:NT], ALU.mult, [hv_t[0], gy_t[g % 3]], [hv_t[1]])
                DMA("sp", rrv[:, g, o0:NT], hvb[:, o0:NT], [hv_t[1]], [rr_d])

            front_x(0)
            front_c(0)
            front_x(1)
            front_c(1)
            for g in range(8):
                fx = []
                if g + 2 < 8:
                    load_blk(g + 2)
                    fx = front_x_tiles(g + 2)
                for k_, st_ in enumerate(gate_steps(g)):
                    st_()
                    if k_ % 2 == 1 and fx:
                        fx.pop(0)()
                while fx:
                    fx.pop(0)()
                if g + 2 < 8:
                    front_c(g + 2)
                for st_ in mid_steps(g):
                    st_()
                for st_ in yr_tiles(g):
                    st_()
                tail(g)

        def merge_phase(l, last, src, dst, stack):
            win = kc_view(w_in[l])
            names = ["go", "ga", "ro", "gb", "out"]
            srcs = [kc_view(w_go[l]), win[:, :, C_GA:C_GA + D], kc_view(w_ro[l]), win[:, :, C_GB:C_GB + D], kc_view(w_out[l])]
            Wm = {}
            Wh = {}
            for nm, sv in zip(names, srcs):
                tns = sbuf(stack, "mw" + nm, [128, 8, D], BF16)
                Wm[nm] = T(tns)
                Wh[nm] = [T(tns[:, :, 0:512]), T(tns[:, :, 512:1024])]
            order = [(nm, sv, half) for half in range(2) for nm, sv in list(zip(names, srcs))[:4]]
            order += [(names[4], srcs[4], half) for half in range(2)]
            for nm, sv, half in order:
                DMA("pool", Wm[nm].ap[:, :, half * 512:(half + 1) * 512], sv[:, :, half * 512:(half + 1) * 512], [dummy], [Wh[nm][half]])
            ogs2 = [sbuf(stack, "m_og%d" % i, [128, 8, 512], BF16) for i in range(2)]; ogs2_t = [T(ogs2[0]), T(ogs2[1])]
            rrs2 = [sbuf(stack, "m_rr%d" % i, [128, 8, 512], BF16) for i in range(2)]; rrs2_t = [T(rrs2[0]), T(rrs2[1])]
            xt = sbuf(stack, "m_x", [128, 8, 512]); xt_t = T(xt)
            xn, xn_t = xt, xt_t
            mg = sbuf(stack, "m_mg", [128, 8, 512], BF16); mg_t = T(mg)
            sa = sbuf(stack, "m_sa", [128, 512]); sa_t = T(sa)
            sb_ = sbuf(stack, "m_sb", [128, 512]); sb_t = T(sb_)
            m1 = sbuf(stack, "m_m1", [128, 512]); m1_t = T(m1)
            m2 = sbuf(stack, "m_m2", [128, 512]); m2_t = T(m2)
            rs = sbuf(stack, "m_rs", [128, 512])
            t1 = [sbuf(stack, "m_t1%d" % i, [128, 512]) for i in range(2)]
            tmp = (mg, mg_t, rs, T(rs), t1, [T(t1[0]), T(t1[1])])
            ogv = og_d.ap.rearrange("(c p) n -> p c n", p=128)
            rrv = rr_d.ap.rearrange("(c p) n -> p c n", p=128)
            srcv = kc_view(src.ap)
            dstv = kc_view(dst.ap)
            m_tiles = [ti for ti in range(len(TILES)) if not (last and ti == 0)]

            def m_loads(ix):
                ti_ = m_tiles[ix]
                a_, b_ = TILES[ti_]
                DMA("sp", ogs2[ix % 2][:, :, :b_ - a_], ogv[:, :, a_:b_], [og_d], [ogs2_t[ix % 2]])
                DMA("sp", rrs2[ix % 2][:, :, :b_ - a_], rrv[:, :, a_:b_], [rr_d], [rrs2_t[ix % 2]])

            def n2_square(tp):
                ap_, bp_ = TILES[tp]
                ACT(mg[:, :, :bp_ - ap_], xn[:, :, :bp_ - ap_], AF.Square, [xn_t], [mg_t])

            def n2_stat(tp):
                ap_, bp_ = TILES[tp]
                np_ = bp_ - ap_
                pb = banks[7]
                for kc in range(8):
                    MM(pb[:, :np_], ones_b[:, :], mg[:, kc, :np_], kc == 0, kc == 7, [ones_t, mg_t], [pb])

            def n2_rest(tp):
                ap_, bp_ = TILES[tp]
                np_ = bp_ - ap_
                sp_ = 0 if tp > 0 else 1
                rs_t_, t1l, t1l_t = tmp[3], tmp[4], tmp[5]
                pb = banks[7]
                ACT(rs[:, :np_], pb[:, :np_], AF.Ln, [pb], [rs_t_], bias=EPS, scale=1.0 / D)
                ACT(rs[:, :np_], rs[:, :np_], AF.Exp, [rs_t_], [rs_t_], scale=-0.5)
                for kc in range(8):
                    TT(t1l[kc % 2][:, :np_], xn[:, kc, :np_], rs[:, :np_], ALU.mult, [xn_t, rs_t_], [t1l_t[kc % 2]])
                    so = (DV_S2X if sp_ == 0 else DV_S2C) + kc
                    ACT(hx[:, kc, ap_:bp_], t1l[kc % 2][:, :np_], AF.Identity, [t1l_t[kc % 2], dv_t, mod_t], [hx_t[tp]],
                        bias=modcol(3, kc, sp_), scale=dv[:, so:so + 1])

            m_loads(0)
            for ix, ti in enumerate(m_tiles):
                a, b = TILES[ti]
                n = b - a
                s = 0 if ti > 0 else 1
                ogs, ogs_t = ogs2[ix % 2], ogs2_t[ix % 2]
                rrs, rrs_t = rrs2[ix % 2], rrs2_t[ix % 2]
                if ix + 1 < len(m_tiles):
                    m_loads(ix + 1)
                if ix > 0:
                    n2_square(m_tiles[ix - 1])
                else:
                    DMA("sp", xt[:, :, :n], srcv[:, :, a:b], [src], [xt_t])
                for j in range(8):
                    if j == 1 and ix > 0:
                        n2_rest(m_tiles[ix - 1])
                        DMA("sp", xt[:, :, :n], srcv[:, :, a:b], [src], [xt_t])
                    k4 = 4 * (j % 2)
                    pA, pGA, pB, pGB = banks[k4], banks[k4 + 1], banks[k4 + 2], banks[k4 + 3]
                    js = slice(j * 128, (j + 1) * 128)
                    for kc in range(8):
                        MM(pA[:, :n], Wm["go"].ap[:, kc, js], ogs[:, kc, :n], kc == 0, kc == 7, [Wh["go"][j // 4], ogs_t], [pA])
                    for kc in range(8):
                        MM(pGA[:, :n], Wm["ga"].ap[:, kc, js], hx[:, kc, a:b], kc == 0, kc == 7, [Wh["ga"][j // 4], hx_t[ti]], [pGA])
                    for kc in range(8):
                        MM(pB[:, :n], Wm["ro"].ap[:, kc, js], rrs[:, kc, :n], kc == 0, kc == 7, [Wh["ro"][j // 4], rrs_t], [pB])
                    for kc in range(8):
                        MM(pGB[:, :n], Wm["gb"].ap[:, kc, js], hx[:, kc, a:b], kc == 0, kc == 7, [Wh["gb"][j // 4], hx_t[ti]], [pGB])
                    if j == 0 and ix > 0:
                        n2_stat(m_tiles[ix - 1])
                    ACT(sa[:, :n], pGA[:, :n], AF.Sigmoid, [pGA], [sa_t])
                    ACT(sb_[:, :n], pGB[:, :n], AF.Sigmoid, [pGB], [sb_t])
                    TT(m1[:, :n], pA[:, :n], sa[:, :n], ALU.mult, [pA, sa_t], [m1_t])
                    TT(m2[:, :n], pB[:, :n], sb_[:, :n], ALU.mult, [pB, sb_t], [m2_t])
                    TT(mg[:, j, :n], m1[:, :n], m2[:, :n], ALU.add, [m1_t, m2_t], [mg_t])
                for j in range(8):
                    pM = banks[j % 4]
                    js = slice(j * 128, (j + 1) * 128)
                    for kc in range(8):
                        MM(pM[:, :n], Wm["out"].ap[:, kc, js], mg[:, kc, :n], kc == 0, kc == 7, [Wh["out"][j // 4], mg_t], [pM])
                    STT(xn[:, j, :n], pM[:, :n], modcol(2, j, s), xt[:, j, :n], ALU.mult, ALU.add, [pM, mod_t, xt_t], [xn_t])
                DMA("sp", dstv[:, :, a:b], xn[:, :, :n], [xn_t], [dst])
                if ix == len(m_tiles) - 1:
                    norm_tile(xn, xn_t, ti, (DV_S2X, DV_S2C), 3, hx_t[ti], hx[:, :, a:b], tmp)

        def ffn_phase(l, last, src, dst, stack):
            upv = kc_view(ffn_up[l])
            act = sbuf(stack, "f_act", [128, NHC, NT], BF16)
            act_t = [T(act[:, :, a:b]) for (a, b) in TILES]
            with ExitStack() as s2:
                wsl = [T(sbuf(s2, "fw%d" % i, [128, 8, 256], BF16)) for i in range(3)]

                def load_hc(hc):
                    w_ = wsl[hc % 3]
                    DMA("pool", w_.ap[:, :, 0:128], upv[:, :, hc * 128:(hc + 1) * 128], [dummy], [w_])
                    DMA("pool", w_.ap[:, :, 128:256], upv[:, :, FH + hc * 128:FH + (hc + 1) * 128], [dummy], [w_])
                load_hc(0)
                load_hc(1)
                apad = sbuf(s2, "f_apad", [128, 34, 66], BF16); apad_t = T(apad)
                cpad = sbuf(s2, "f_cpad", [128, 258], BF16); cpad_t = T(cpad)
                dg = [sbuf(s2, "f_dg%d" % i, [128, 9, 128], BF16) for i in range(2)]; dg_t = [T(dg[0]), T(dg[1])]
                gl = sbuf(s2, "f_gl", [128, 512]); gl_t = T(gl)
                MEMSET(apad[:], 0.0, [apad_t])
                MEMSET(cpad[:], 0.0, [cpad_t])
                for hc in range(NHC):
                    W = wsl[hc % 3]
                    if hc + 2 < NHC:
                        load_hc(hc + 2)
                    D_ = dg[hc % 2]
                    D_t = dg_t[hc % 2]
                    for tap in range(9):
                        TS(D_[:, tap, :], cst[:, CI:CI + 128], V("fcw%d" % l, tap * NHC + hc), None, ALU.mult, ALU.bypass, [cst_t, vecs_t], [D_t])
                    for ti, (a, b) in enumerate(TILES):
                        if last and ti == 0:
                            continue
                        n = b - a
                        pb = banks[ti % 2]
                        for kc in range(8):
                            MM(pb[:, :n], W.ap[:, kc, 0:128], hx[:, kc, a:b], kc == 0, kc == 7, [W, hx_t[ti]], [pb])
                        if ti == 0:
                            ACT(cpad[:, 1:257], pb[:, :n], AF.Copy, [pb], [cpad_t])
                        else:
                            r0 = (ti - 1) * 8
                            ACT(apad[:, r0 + 1:r0 + 9, 1:65], pb[:, :n].rearrange("p (r c) -> p r c", c=64), AF.Copy, [pb], [apad_t])
                    for ti, (a, b) in enumerate(TILES):
                        if last and ti == 0:
                            continue
                        n = b - a
                        pc = banks[2 + (ti % 2)]
                        pg = banks[4 + (ti % 2)]
                        if ti == 0:
                            for i_, dc in enumerate((-1, 0, 1)):
                                MM(pc[:, :n], D_[:, 3 + dc + 1, :], cpad[:, 1 + dc:257 + dc], i_ == 0, i_ == 2, [D_t, cpad_t], [pc])
                        else:
                            r0 = (ti - 1) * 8
                            i_ = 0
                            for dr in (-1, 0, 1):
                                for dc in (-1, 0, 1):
                                    tap = (dr + 1) * 3 + (dc + 1)
                                    MM(pc[:, :n].rearrange("p (r c) -> p r c", c=64), D_[:, tap, :],
                                       apad[:, r0 + 1 + dr:r0 + 9 + dr, 1 + dc:65 + dc], i_ == 0, i_ == 8, [D_t, apad_t], [pc])
                                    i_ += 1
                        for kc in range(8):
                            MM(pg[:, :n], W.ap[:, kc, 128:256], hx[:, kc, a:b], kc == 0, kc == 7, [W, hx_t[ti]], [pg])
                        ACT(gl[:, :n], pc[:, :n], AF.Gelu_apprx_tanh, [pc, vecs_t], [gl_t], bias=V("fcb%d" % l, hc))
                        TT(act[:, hc, a:b], pg[:, :n], gl[:, :n], ALU.mult, [pg, gl_t], [act_t[ti]])
            P.barrier()
            with ExitStack() as s3:
                dnv = ffn_dn[l].rearrange("(hc p) n -> p hc n", p=128)
                wdp_t = [T(wd[:, q:q + 2, :]) for q in range(0, NHC, 2)]
                for q in range(0, NHC, 2):
                    DMA("pool", wd[:, q:q + 2, :], dnv[:, q:q + 2, :], [dummy], [wdp_t[q // 2]] + (hx_t if q == 0 else []) + ([wd_t] if q == 0 else []))
                xt = sbuf(s3, "d_x", [128, 8, 512]); xt_t = T(xt)
                xn, xn_t = xt, xt_t
                yo, yo_t = xt, xt_t
                sq = sbuf(s3, "d_sq", [128, 8, 512], BF16); rs = sbuf(s3, "d_rs", [128, 512])
                t1 = [sbuf(s3, "d_t1%d" % i, [128, 512]) for i in range(2)]
                tmp = (sq, T(sq), rs, T(rs), t1, [T(t1[0]), T(t1[1])])
                srcv = kc_view(src.ap)
                for ti, (a, b) in enumerate(TILES):
                    if last and ti == 0:
                        continue
                    n = b - a
                    s = 0 if ti > 0 else 1
                    DMA("sp", xt[:, :, :n], srcv[:, :, a:b], [src], [xt_t])
                    for j in range(8):
                        pb = banks[j]
                        for hc in range(NHC):
                            MM(pb[:, :n], wd[:, hc, j * 128:(j + 1) * 128], act[:, hc, a:b], hc == 0, hc == NHC - 1, [wdp_t[hc // 2], wd_t, act_t[ti]], [pb])
                    for j in range(8):
                        STT(xn[:, j, :n], banks[j][:, :n], modcol(5, j, s), xt[:, j, :n], ALU.mult, ALU.add, [banks[j], mod_t, xt_t], [xn_t])
                    if not last:
                        DMA("sp", kc_view(dst.ap)[:, :, a:b], xn[:, :, :n], [xn_t], [dst])
                        if ti == len(TILES) - 1:
                            ACT(dv[:, 255:256], vecs[:, 0:1], AF.Copy, [vecs_t], [wd_t])
                    else:
                        norm_tile(xn, xn_t, ti, None, None, yo_t, yo, tmp, final=True)
                        ev = DMA("sp", kc_view(outT)[:, :, a - NCTX:b - NCTX], yo[:, :, :n], [yo_t], [outT_t])
                        out_evs.append(ev)

        for l in range(DEPTH):
            last = (l == DEPTH - 1)
            src0 = xs_s[2 * l]
            mid = xs_s[2 * l + 1]
            nxt = xs_s[2 * l + 2] if not last else None
            P.barrier()
            with ExitStack() as s1:
                wslots = [T(sbuf(s1, "aw%d" % i, [128, 4096], BF16)) for i in range(3)]
                ada_phase(l, s1, wslots)
            P.barrier()
            if debug:
                out_evs.append(DMA("sp", dbg["mod%d" % l], mod[:, :], [mod_t], [T(None)]))
                out_evs.append(DMA("sp", dbg["dv%d" % l], dv[:, :], [dv_t], [T(None)]))
            with ExitStack() as s1:
                norm1_phase(l, src0, s1)
            if debug:
                out_evs.append(DMA("sp", dbg["hx%d" % l], RR[:, 0:8 * NT], hx_t, [T(None)]))
            P.barrier()
            with ExitStack() as s1:
                gla_phase(l, last, s1)
            P.barrier()
            with ExitStack() as s1:
                rnn_phase(l, last, s1)
            P.barrier()
            if debug:
                out_evs.append(DMA("sp", dbg["og%d" % l], og_d.ap, [og_d], [T(None)]))
                out_evs.append(DMA("sp", dbg["rr%d" % l], rr_d.ap, [rr_d], [T(None)]))
            with ExitStack() as s1:
                merge_phase(l, last, src0, mid, s1)
            if debug:
                out_evs.append(DMA("sp", dbg["hx2_%d" % l], RR[:, 0:8 * NT], hx_t, [T(None)]))
            P.barrier()
            with ExitStack() as s1:
                ffn_phase(l, last, mid, nxt, s1)
        P.finish(out_evs)
        with nc.Block() as block:
            P.emit(sems, block)
    return nc


_NC_CACHE = {}


def kernel(**inp):
    inp = {k: np.asarray(v) for k, v in inp.items()}
    B = inp["x"].shape[0]
    if "nc" not in _NC_CACHE:
        _NC_CACHE["nc"] = build_nc()
    nc = _NC_CACHE["nc"]
    cst = _consts()
    shared = {
        "cst": cst,
        "ada_w": np.ascontiguousarray(inp["ada_w"], np.float32),
        "w_in": np.ascontiguousarray(inp["w_in"], np.float32),
        "gla_lr_w": np.ascontiguousarray(inp["gla_lr_w"], np.float32),
        "rnn_wa": np.ascontiguousarray(inp["rnn_wa"], np.float32),
        "rnn_wx": np.ascontiguousarray(inp["rnn_wx"], np.float32),
        "w_gla_o": np.ascontiguousarray(inp["w_gla_o"], np.float32),
        "w_rnn_o": np.ascontiguousarray(inp["w_rnn_o"], np.float32),
        "w_out": np.ascontiguousarray(inp["w_out"], np.float32),
        "ffn_up": np.ascontiguousarray(inp["ffn_up"], np.float32),
        "ffn_down": np.ascontiguousarray(inp["ffn_down"], np.float32),
    }
    in_maps = []
    for b in range(B):
        m = dict(shared)
        m["xs0"] = np.ascontiguousarray(np.concatenate([inp["ctx"][b].T, inp["x"][b].T], axis=1), np.float32)
        m["vecs"] = _pack_vecs(inp, b)
        in_maps.append(m)
    res = run_bass_kernel_spmd(nc, in_maps, core_ids=list(range(B)))
    out = np.stack([np.asarray(r["outT"]).T for r in res.results], axis=0)
    return np.ascontiguousarray(out, np.float32)
```

```python
import numpy as np
from contextlib import ExitStack
import concourse.bass as bass
import concourse.mybir as mybir
from concourse.bass_utils import run_bass_kernel_spmd

F32 = mybir.dt.float32
BF16 = mybir.dt.bfloat16
ALU = mybir.AluOpType
AF = mybir.ActivationFunctionType

D = 1024
NCTX = 256
SEQ = 2048
NT = NCTX + SEQ
DEPTH = 2
D_IN = 7200
FH = 2816
NHC = FH // 128
NCH = NT // 128
TILES = [(0, 256), (256, 768), (768, 1280), (1280, 1792), (1792, 2304)]
EPS = 1e-6
NDMA_SEM = 8

C_Q, C_K, C_V, C_G, C_LR, C_XR, C_YR, C_GA, C_GB = 0, 512, 1024, 2048, 3072, 3104, 4128, 5152, 6176


class T:
    __slots__ = ("ap", "w", "r", "name")

    def __init__(self, ap, name=""):
        self.ap = ap
        self.w = None
        self.r = []
        self.name = name

    def __getitem__(self, k):
        return self.ap[k]


class Eng:
    def __init__(self, name):
        self.name = name
        self.ops = []
        self.count = 0
        self.seen = {}
        self.dma_i = 0
        self.pending = {}


class Prog:
    def __init__(self, nc):
        self.nc = nc
        self.E = {n: Eng(n) for n in ("pe", "act", "dve", "pool", "sp")}
        self.semnames = ["s_" + n for n in self.E]
        for q in ("sp", "pool"):
            for j in range(NDMA_SEM):
                self.semnames.append("d_%s_%d" % (q, j))

    def _need(self, eng, ev, waits):
        if ev is None:
            return
        key, val, _ = ev
        if eng.seen.get(key, 0) >= val:
            return
        waits[key] = max(waits.get(key, 0), val)

    def op(self, engname, fn, reads=(), writes=()):
        eng = self.E[engname]
        waits = {}
        for b in reads:
            if b.w is not None and not (b.w[2] == engname and engname == "pe"):
                self._need(eng, b.w, waits)
        for b in writes:
            if b.w is not None and b.w[2] != engname:
                self._need(eng, b.w, waits)
            for ev in b.r:
                if ev[2] != engname:
                    self._need(eng, ev, waits)
        self._merge_pending(eng, waits)
        for k, v in waits.items():
            eng.seen[k] = v
        eng.count += 1
        key = "s_" + engname
        ev = (key, eng.count, engname)
        eng.ops.append((list(waits.items()), fn, (key, 1)))
        if not hasattr(self, "labels"):
            self.labels = {}
        self.labels.setdefault(engname, []).append(getattr(self, "cur", ""))
        for b in writes:
            b.w = ev
            b.r = []
        for b in reads:
            if b not in writes:
                b.r = [e for e in b.r if e[2] != engname] + [ev]
        return ev

    def dma(self, qname, fn, reads=(), writes=()):
        eng = self.E[qname]
        j = eng.dma_i % NDMA_SEM
        rnd = eng.dma_i // NDMA_SEM
        eng.dma_i += 1
        key = "d_%s_%d" % (qname, j)
        waits = {}
        if rnd > 0:
            self._need(eng, (key, 16 * rnd, "dma"), waits)
        for b in reads:
            self._need(eng, b.w, waits)
        for b in writes:
            self._need(eng, b.w, waits)
            for ev in b.r:
                self._need(eng, ev, waits)
        self._merge_pending(eng, waits)
        for k, v in waits.items():
            eng.seen[k] = v
        ev = (key, 16 * (rnd + 1), "dma_" + qname)
        eng.ops.append((list(waits.items()), fn, (key, 16)))
        for b in writes:
            b.w = ev
            b.r = []
        for b in reads:
            if b not in writes:
                b.r = b.r + [ev]
        return ev

    def _merge_pending(self, eng, waits):
        for k, v in eng.pending.items():
            if eng.seen.get(k, 0) < v:
                waits[k] = max(waits.get(k, 0), v)
        eng.pending = {}

    def mark(self, name):
        if not hasattr(self, "marks"):
            self.marks = []
        self.marks.append((name, {n: e.count for n, e in self.E.items()}))

    def barrier(self):
        snap = {}
        for n, e in self.E.items():
            if e.count > 0:
                snap["s_" + n] = e.count
            if n in ("sp", "pool"):
                for i in range(min(e.dma_i, NDMA_SEM)):
                    cnt = (e.dma_i - 1 - i) // NDMA_SEM + 1
                    snap["d_%s_%d" % (n, i)] = 16 * cnt
        for n, e in self.E.items():
            for k, v in snap.items():
                if k == "s_" + n:
                    continue
                e.pending[k] = max(e.pending.get(k, 0), v)

    def finish(self, evs):
        eng = self.E["sp"]
        waits = {}
        for ev in evs:
            self._need(eng, ev, waits)
        eng.ops.append((list(waits.items()), None, None))

    def emit(self, sems, block):
        hw = {"pe": "tensor", "act": "scalar", "dve": "vector", "pool": "gpsimd", "sp": "sync"}

        def mk(engname):
            eng = self.E[engname]

            def body(e):
                for waits, fn, inc in eng.ops:
                    for k, v in waits:
                        e.wait_ge(sems[k], v)
                    if fn is not None:
                        fn(e).then_inc(sems[inc[0]], inc[1])
            return body

        for n in self.E:
            if self.E[n].ops:
                getattr(block, hw[n])(mk(n))


def _vec_layout():
    off = {}
    n = 0

    def add(name, cols):
        nonlocal n
        off[name] = n
        n += cols
    add("c", 8)
    add("cctx", 8)
    add("fnw", 8)
    for l in range(DEPTH):
        add("adab%d" % l, 48)
        add("n1w%d" % l, 8)
        add("n2w%d" % l, 8)
        add("lrb%d" % l, 8)
        add("gnw%d" % l, 8)
        add("rcw%d" % l, 32)
        add("rcb%d" % l, 8)
        add("rba%d" % l, 16)
        add("rbx%d" % l, 16)
        add("rlam%d" % l, 16)
        add("fcw%d" % l, 9 * NHC)
        add("fcb%d" % l, NHC)
    return off, n


VOFF, NV = _vec_layout()
CI, CMF, CMB, CSF, CSB, NCST = 0, 128, 256, 384, 896, 1408


def _col(v):
    v = np.asarray(v, np.float32).reshape(-1, 128)
    return v.T


def _pack_vecs(inp, b):
    V = np.zeros((128, NV), np.float32)

    def put(name, arr):
        a = _col(arr)
        V[:, VOFF[name]:VOFF[name] + a.shape[1]] = a
    put("c", inp["c"][b])
    put("cctx", inp["c_ctx"])
    put("fnw", inp["final_norm_w"])
    for l in range(DEPTH):
        put("adab%d" % l, inp["ada_b"][l])
        put("n1w%d" % l, inp["norm1_w"][l])
        put("n2w%d" % l, inp["norm2_w"][l])
        put("lrb%d" % l, inp["gla_lr_b"][l])
        put("gnw%d" % l, inp["gla_norm_w"][l])
        put("rcw%d" % l, inp["rnn_conv_w"][l])
        put("rcb%d" % l, inp["rnn_conv_b"][l])
        put("rba%d" % l, inp["rnn_ba"][l])
        put("rbx%d" % l, inp["rnn_bx"][l])
        put("rlam%d" % l, inp["rnn_lambda"][l])
        put("fcw%d" % l, inp["ffn_conv_w"][l])
        put("fcb%d" % l, inp["ffn_conv_b"][l])
    return V


def _consts():
    C = np.zeros((128, NCST), np.float32)
    C[:, CI:CI + 128] = np.eye(128, dtype=np.float32)
    s = np.arange(128)[:, None]
    c = np.arange(128)[None, :]
    C[:, CMF:CMF + 128] = (s <= c)
    C[:, CMB:CMB + 128] = (s >= c)
    t = np.arange(512)
    C[:, CSF:CSF + 512] = (t % 128 != 0)[None, :]
    C[:, CSB:CSB + 512] = (t % 128 != 127)[None, :]
    return C


def build_nc(debug=False):
    nc = bass.Bass("TRN2", target_bir_lowering=False)
    P = Prog(nc)

    def din(name, shape):
        return nc.dram_tensor(name, list(shape), F32, kind="ExternalInput").ap()
    xs0 = din("xs0", [D, NT])
    vecs_d = din("vecs", [128, NV])
    cst_d = din("cst", [128, NCST])
    ada_w = din("ada_w", [DEPTH, D, 6 * D])
    w_in = din("w_in", [DEPTH, D, D_IN])
    lr_w = din("gla_lr_w", [DEPTH, 2, 16, 512])
    rnn_wa = din("rnn_wa", [DEPTH, 2, 8, 128, 128])
    rnn_wx = din("rnn_wx", [DEPTH, 2, 8, 128, 128])
    w_go = din("w_gla_o", [DEPTH, D, D])
    w_ro = din("w_rnn_o", [DEPTH, D, D])
    w_out = din("w_out", [DEPTH, D, D])
    ffn_up = din("ffn_up", [DEPTH, D, 2 * FH])
    ffn_dn = din("ffn_down", [DEPTH, FH, D])
    outT = nc.dram_tensor("outT", [D, SEQ], F32, kind="ExternalOutput").ap()
    skind = "ExternalOutput" if debug else "Internal"
    xs_s = [T(xs0, "xs0")] + [T(nc.dram_tensor("xs%d" % i, [D, NT], F32, kind=skind).ap(), "xs%d" % i) for i in (1, 2, 3)]
    og_d = T(nc.dram_tensor("og", [D, NT], BF16, kind=skind).ap(), "og")
    rr_d = T(nc.dram_tensor("rr", [D, NT], BF16, kind=skind).ap(), "rr")
    dbg = {}
    if debug:
        for l in range(DEPTH):
            dbg["mod%d" % l] = nc.dram_tensor("dbg_mod%d" % l, [128, 96], F32, kind="ExternalOutput").ap()
            dbg["dv%d" % l] = nc.dram_tensor("dbg_dv%d" % l, [128, 256], F32, kind="ExternalOutput").ap()
            dbg["hx%d" % l] = nc.dram_tensor("dbg_hx%d" % l, [128, 8 * NT], BF16, kind="ExternalOutput").ap()
            dbg["hx2_%d" % l] = nc.dram_tensor("dbg_hx2_%d" % l, [128, 8 * NT], BF16, kind="ExternalOutput").ap()
            dbg["og%d" % l] = nc.dram_tensor("dbg_og%d" % l, [D, NT], BF16, kind="ExternalOutput").ap()
            dbg["rr%d" % l] = nc.dram_tensor("dbg_rr%d" % l, [D, NT], BF16, kind="ExternalOutput").ap()
    outT_t = T(outT, "outT")
    dummy = T(None, "wdram")

    def kc_view(ap2d):
        return ap2d.rearrange("(kc p) n -> p kc n", p=128)

    out_evs = []
    st = ExitStack()
    with st:
        sems = {n: st.enter_context(nc.semaphore(n)) for n in P.semnames}

        _uid = [0]

        def sbuf(stack, name, shape, dt=F32):
            _uid[0] += 1
            return stack.enter_context(nc.sbuf_tensor("sb%d_%s" % (_uid[0], name), list(shape), dt))

        def ACT(out, in_, func, r, w, bias=0.0, scale=1.0):
            P.op("act", lambda e: e.activation(out=out, in_=in_, func=func, bias=bias, scale=scale), r, w)

        def TT(out, a, b, op, r, w):
            P.op("dve", lambda e: e.tensor_tensor(out, a, b, op), r, w)

        def TS(out, a, s1, s2, op0, op1, r, w):
            if s2 is None:
                P.op("dve", lambda e: e.tensor_scalar(out, a, s1, None, op0), r, w)
            else:
                P.op("dve", lambda e: e.tensor_scalar(out, a, s1, s2, op0, op1), r, w)

        def SCAN(out, d0, d1, init, r, w):
            P.op("dve", lambda e: e.tensor_tensor_scan(out, d0, d1, init, ALU.mult, ALU.add), r, w)

        def STT(out, a, s, b, op0, op1, r, w):
            P.op("dve", lambda e: e.scalar_tensor_tensor(out, a, s, b, op0, op1), r, w)

        def PCOPY(out, a, r, w):
            P.op("pool", lambda e: e.tensor_copy(out, a), r, w)

        def PTT(out, a, b, op, r, w):
            P.op("pool", lambda e: e.tensor_tensor(out, a, b, op), r, w)

        def DCOPY(out, a, r, w):
            P.op("dve", lambda e: e.tensor_copy(out, a), r, w)

        def MEMSET(out, v, w):
            P.op("dve", lambda e: e.memset(out, v), (), w)

        def MM(out, lhsT, rhs, start, stop, r, w):
            P.op("pe", lambda e: e.matmul(out, lhsT, rhs, start=start, stop=stop), r, w)

        def TR(out, in_, ident, r, w):
            P.op("pe", lambda e: e.transpose(out, in_, ident), r, w)

        def DMA(q, out, in_, r, w):
            return P.dma(q, lambda e: e.dma_start(out=out, in_=in_), r, w)

        vecs = sbuf(st, "vecs", [128, NV]); vecs_t = T(vecs, "vecs")
        cst = sbuf(st, "cst", [128, NCST]); cst_t = T(cst, "cst")
        cstb = sbuf(st, "cstb", [128, 128], BF16); cstb_t = T(cstb, "cstb")
        ones_b = sbuf(st, "ones_b", [128, 128], BF16); ones_t = T(ones_b, "ones")
        dv = sbuf(st, "dv", [128, 256]); dv_t = T(dv, "dv")
        mod = sbuf(st, "mod", [128, 96]); mod_t = T(mod, "mod")
        scb = sbuf(st, "scb", [128, 8, 2], BF16); scb_t = T(scb, "scb")
        RR = sbuf(st, "RR", [128, NHC * D], BF16)
        hx = RR[:, 0:8 * NT].rearrange("p (kc n) -> p kc n", kc=8)
        wd = RR[:, :].rearrange("p (hc n) -> p hc n", hc=NHC)
        wd_t = T(wd, "wd")
        hx_t = [T(hx[:, :, a:b], "hx%d" % i) for i, (a, b) in enumerate(TILES)]
        banks = [T(st.enter_context(nc.psum_tensor("pb%d" % i, [128, 512], F32)), "pb%d" % i) for i in range(8)]

        DMA("sp", vecs[:], vecs_d, [dummy], [vecs_t])
        DMA("sp", cst[:], cst_d, [dummy], [cst_t])
        DCOPY(cstb[:], cst[:, CI:CI + 128], [cst_t], [cstb_t])
        MEMSET(ones_b[:], 1.0, [ones_t])

        def V(name, i=0, n=1):
            o = VOFF[name] + i
            return vecs[:, o:o + n]

        def tile_of_chunk(n):
            return 0 if n < 2 else 1 + (n - 2) // 4

        DV_S1X, DV_S1C, DV_S2X, DV_S2C, DV_NLRB, DV_L, DV_SILU = 0, 8, 16, 24, 32, 40, 56

        def ada_phase(l, wst, wslots):
            ACT(scb[:, :, 0], V("c", 0, 8), AF.Silu, [vecs_t], [scb_t])
            ACT(scb[:, :, 1], V("cctx", 0, 8), AF.Silu, [vecs_t], [scb_t])
            pb = banks[0]
            aw = kc_view(ada_w[l])
            for grp in range(12):
                ws = wslots[grp % len(wslots)]
                DMA("pool", ws.ap[:, :].rearrange("p (kc n) -> p kc n", kc=8), aw[:, :, grp * 512:(grp + 1) * 512], [dummy], [ws])
                wv = ws.ap[:, :].rearrange("p (kc n) -> p kc n", kc=8)
                for jj in range(4):
                    j = grp * 4 + jj
                    for kc in range(8):
                        MM(pb[:, j * 2:j * 2 + 2], wv[:, kc, jj * 128:(jj + 1) * 128], scb[:, kc, :],
                           kc == 0, kc == 7, [ws, scb_t], [pb])
            m3 = mod[:, :].rearrange("p (j s) -> p j s", s=2)
            p3 = pb[:, 0:96].rearrange("p (j s) -> p j s", s=2)
            for s in range(2):
                TT(m3[:, :, s], p3[:, :, s], V("adab%d" % l, 0, 48), ALU.add, [pb, vecs_t], [mod_t])
            for s, (o1, o2) in enumerate(((DV_S1X, DV_S2X), (DV_S1C, DV_S2C))):
                STT(dv[:, o1:o1 + 8], m3[:, 8:16, s], 1.0, V("n1w%d" % l, 0, 8), ALU.add, ALU.mult, [mod_t, vecs_t], [dv_t])
                STT(dv[:, o2:o2 + 8], m3[:, 32:40, s], 1.0, V("n2w%d" % l, 0, 8), ALU.add, ALU.mult, [mod_t, vecs_t], [dv_t])
            TS(dv[:, DV_NLRB:DV_NLRB + 8], V("lrb%d" % l, 0, 8), -1.0, None, ALU.mult, ALU.bypass, [vecs_t], [dv_t])
            ACT(dv[:, DV_L:DV_L + 16], V("rlam%d" % l, 0, 16), AF.Exp, [vecs_t], [dv_t], scale=-1.0)
            ACT(dv[:, DV_L:DV_L + 16], dv[:, DV_L:DV_L + 16], AF.Ln, [dv_t], [dv_t], bias=1.0)
            TS(dv[:, DV_L:DV_L + 16], dv[:, DV_L:DV_L + 16], -8.0, None, ALU.mult, ALU.bypass, [dv_t], [dv_t])

        def modcol(part, kc, s):
            j = part * 8 + kc
            return mod[:, j * 2 + s:j * 2 + s + 1]

        def norm_tile(xt_ap, xt_T, ti, sc_off, sh_part, out_tile_T, out_ap, tmp, nw_scale_from_dv=True, final=False):
            a, b = TILES[ti]
            n = b - a
            s = 0 if ti > 0 else 1
            sq, sq_t, rs, rs_t, t1l, t1l_t = tmp
            ACT(sq[:, :, :n], xt_ap[:, :, :n], AF.Square, [xt_T], [sq_t])
            pb = banks[7]
            for kc in range(8):
                MM(pb[:, :n], ones_b[:, :], sq[:, kc, :n], kc == 0, kc == 7, [ones_t, sq_t], [pb])
            ACT(rs[:, :n], pb[:, :n], AF.Ln, [pb], [rs_t], bias=EPS, scale=1.0 / D)
            ACT(rs[:, :n], rs[:, :n], AF.Exp, [rs_t], [rs_t], scale=-0.5)
            for kc in range(8):
                t1, t1_t = t1l[kc % 2], t1l_t[kc % 2]
                TT(t1[:, :n], xt_ap[:, kc, :n], rs[:, :n], ALU.mult, [xt_T, rs_t], [t1_t])
                if final:
                    TS(out_ap[:, kc, :n], t1[:, :n], V("fnw", kc), None, ALU.mult, ALU.bypass, [t1_t, vecs_t], [out_tile_T])
                else:
                    so = (sc_off[0] if s == 0 else sc_off[1]) + kc
                    ACT(out_ap[:, kc, :n], t1[:, :n], AF.Identity, [t1_t, dv_t, mod_t], [out_tile_T],
                        bias=modcol(sh_part, kc, s), scale=dv[:, so:so + 1])

        def norm1_phase(l, src, stack):
            xt = [sbuf(stack, "n1x%d" % i, [128, 8, 512]) for i in range(2)]
            xt_T = [T(x, "n1x") for x in xt]
            sq = [sbuf(stack, "n1sq%d" % i, [128, 8, 512], BF16) for i in range(2)]; sq_t = [T(sq[0]), T(sq[1])]
            rs = [sbuf(stack, "n1rs%d" % i, [128, 512]) for i in range(2)]; rs_t = [T(rs[0]), T(rs[1])]
            t1 = [sbuf(stack, "n1t1%d" % i, [128, 512]) for i in range(2)]; t1_t = [T(t1[0]), T(t1[1])]
            srcv = kc_view(src.ap)

            def stat1(ti):
                a, b = TILES[ti]
                n = b - a
                k = ti % 2
                DMA("sp", xt[k][:, :, :n], srcv[:, :, a:b], [src], [xt_T[k]])
                ACT(sq[k][:, :, :n], xt[k][:, :, :n], AF.Square, [xt_T[k]], [sq_t[k]])
                pb = banks[6 + k]
                for kc in range(8):
                    MM(pb[:, :n], ones_b[:, :], sq[k][:, kc, :n], kc == 0, kc == 7, [ones_t, sq_t[k]], [pb])

            def stat2(ti):
                a, b = TILES[ti]
                n = b - a
                k = ti % 2
                pb = banks[6 + k]
                ACT(rs[k][:, :n], pb[:, :n], AF.Ln, [pb], [rs_t[k]], bias=EPS, scale=1.0 / D)
                ACT(rs[k][:, :n], rs[k][:, :n], AF.Exp, [rs_t[k]], [rs_t[k]], scale=-0.5)

            def apply(ti):
                a, b = TILES[ti]
                n = b - a
                k = ti % 2
                s_ = 0 if ti > 0 else 1
                for kc in range(8):
                    tt, tt_t = t1[kc % 2], t1_t[kc % 2]
                    TT(tt[:, :n], xt[k][:, kc, :n], rs[k][:, :n], ALU.mult, [xt_T[k], rs_t[k]], [tt_t])
                    so = (DV_S1X if s_ == 0 else DV_S1C) + kc
                    ACT(hx[:, kc, a:b], tt[:, :n], AF.Identity, [tt_t, dv_t, mod_t], [hx_t[ti]],
                        bias=modcol(0, kc, s_), scale=dv[:, so:so + 1])

            stat1(0)
            stat2(0)
            for ti in range(len(TILES)):
                if ti + 1 < len(TILES):
                    stat1(ti + 1)
                apply(ti)
                if ti + 1 < len(TILES):
                    stat2(ti + 1)

        def gla_phase(l, last, stack):
            win = kc_view(w_in[l])
            ws = []
            for i in range(2):
                d = {}
                for nm, cols in (("q", 128), ("k", 128), ("v", 256), ("g", 256)):
                    tns = sbuf(stack, "gw%s%d" % (nm, i), [128, 8, cols], BF16)
                    d[nm] = T(tns, "gw" + nm)
                ws.append(d)

            def load_head(h):
                d = ws[h % 2]
                DMA("pool", d["q"].ap[:], win[:, :, C_Q + h * 128:C_Q + (h + 1) * 128], [dummy], [d["q"]])
                DMA("pool", d["k"].ap[:], win[:, :, C_K + h * 128:C_K + (h + 1) * 128], [dummy], [d["k"]])
                DMA("pool", d["v"].ap[:], win[:, :, C_V + h * 256:C_V + (h + 1) * 256], [dummy], [d["v"]])
                DMA("pool", d["g"].ap[:], win[:, :, C_G + h * 256:C_G + (h + 1) * 256], [dummy], [d["g"]])
            wlrc = sbuf(stack, "wlrc", [128, 8, 32], BF16); wlrc_t = T(wlrc)
            wlr = sbuf(stack, "wlr", [16, 2, 512], BF16); wlr_t = T(wlr)
            DMA("pool", wlrc[:], win[:, :, C_LR:C_LR + 32], [dummy], [wlrc_t])
            DMA("pool", wlr[:], lr_w[l].rearrange("d r k -> r d k"), [dummy], [wlr_t])
            load_head(0)
            lrT = sbuf(stack, "lrT", [16, 2, NT], BF16); lrT_t = [T(lrT[:, :, a:b]) for (a, b) in TILES]
            for ti, (a, b) in enumerate(TILES):
                n = b - a
                for d in range(2):
                    pb = banks[d]
                    for kc in range(8):
                        MM(pb[0:16, :n], wlrc[:, kc, d * 16:(d + 1) * 16], hx[:, kc, a:b], kc == 0, kc == 7, [wlrc_t, hx_t[ti]], [pb])
                    ACT(lrT[:, d, a:b], pb[0:16, :n], AF.Identity, [pb], [lrT_t[ti]])

            qd = [sbuf(stack, "qd%d" % d, [128, NT], BF16) for d in range(2)]
            ki = [sbuf(stack, "ki%d" % d, [128, NT], BF16) for d in range(2)]
            qd_t = [[T(qd[d][:, a:b]) for (a, b) in TILES] for d in range(2)]
            ki_t = [[T(ki[d][:, a:b]) for (a, b) in TILES] for d in range(2)]
            kt = [sbuf(stack, "kt%d" % d, [128, NCH, 128], BF16) for d in range(2)]
            kt_t = [[T(kt[d][:, 0:1, :]) for _ in TILES] for d in range(2)]
            vt = sbuf(stack, "vt", [128, NCH, 256], BF16); vt_t = [T(vt[:, 0:1, :]) for _ in TILES]
            sb_ = [sbuf(stack, "sb%d" % d, [128, NCH, 256], BF16) for d in range(2)]
            sb_t = [[T(sb_[d][:, n, :]) for n in range(NCH)] for d in range(2)]
            el = sbuf(stack, "el", [128, 2, NCH]); el_t = [[T(el[:, d, 0:1]) for _ in TILES] for d in range(2)]
            S = [sbuf(stack, "S%d" % d, [128, 256]) for d in range(2)]; S_t = [T(S[0]), T(S[1])]
            Stmp = [sbuf(stack, "Stmp%d" % d, [128, 256]) for d in range(2)]; Stmp_t = [T(Stmp[0]), T(Stmp[1])]
            tm = {}
            for d in range(2):
                for p_ in range(2):
                    for nm in ("A", "B", "C"):
                        tns = sbuf(stack, "g%s%d%d" % (nm, d, p_), [128, 512])
                        tm[(nm, d, p_)] = (tns, T(tns))
            scm_all = sbuf(stack, "scm_all", [128, 2, NCH, 128], BF16)
            scm_t = [[T(scm_all[:, d, n, :]) for n in range(NCH)] for d in range(2)]
            sq = sbuf(stack, "gsq", [128, 2, 512], BF16); sq_t = T(sq)
            rs = sbuf(stack, "grs", [128, 512]); rs_t = T(rs)
            sg = sbuf(stack, "gsg", [128, 2, 512]); sg_t = T(sg)
            t1 = sbuf(stack, "gt1", [128, 512]); t1_t = T(t1)
            ogt = [sbuf(stack, "ogt%d" % i, [128, 2, 512], BF16) for i in range(2)]; ogt_t = [T(ogt[0]), T(ogt[1])]
            ogv = og_d.ap.rearrange("(c p) n -> p c n", p=128)
            og_i = 0
            border = [1, 0] + list(range(NCH - 1, 1, -1))
            forder = list(range(NCH))

            kdt = {}
            for d in range(2):
                for p_ in range(2):
                    tns = sbuf(stack, "gkd%d%d" % (d, p_), [128, 512], BF16)
                    kdt[(d, p_)] = (tns, T(tns))

            for h in range(4):
                W = ws[h % 2]
                if h + 1 < 4:
                    load_head(h + 1)
                P.mark("gla_prep")

                def stage1(ti):
                    a, b = TILES[ti]
                    n = b - a
                    nch = n // 128
                    c0 = a // 128
                    p_ = ti % 2
                    pq, pk = banks[0 + p_], banks[2 + p_]
                    for kc in range(8):
                        MM(pq[:, :n], W["q"].ap[:, kc, :], hx[:, kc, a:b], kc == 0, kc == 7, [W["q"], hx_t[ti]], [pq])
                    for kc in range(8):
                        MM(pk[:, :n], W["k"].ap[:, kc, :], hx[:, kc, a:b], kc == 0, kc == 7, [W["k"], hx_t[ti]], [pk])
                    for j in range(nch):
                        pv = banks[6 + (j // 2) % 2]
                        hs_ = (j % 2) * 256
                        for kc in range(8):
                            MM(pv[:, hs_:hs_ + 256], hx[:, kc, a + j * 128:a + (j + 1) * 128], W["v"].ap[:, kc, :], kc == 0, kc == 7, [hx_t[ti], W["v"]], [pv])
                        if j % 2 == 1 or j == nch - 1:
                            j0 = j - (j % 2)
                            w_ = (j - j0 + 1) * 256
                            DCOPY(vt[:, c0 + j0:c0 + j + 1, :], pv[:, 0:w_].rearrange("p (c v) -> p c v", v=256), [pv], [vt_t[ti]])
                    for d in range(2):
                        pl = banks[4]
                        MM(pl[:, :n], wlr[:, d, h * 128:(h + 1) * 128], lrT[:, d, a:b], True, True, [wlr_t, lrT_t[ti]], [pl])
                        A_, A_t = tm[("A", d, p_)]
                        nb = dv[:, DV_NLRB + d * 4 + h:DV_NLRB + d * 4 + h + 1]
                        ACT(A_[:, :n], pl[:, :n], AF.Exp, [pl, dv_t], [A_t], bias=nb, scale=-1.0)

                def stage2(ti):
                    a, b = TILES[ti]
                    n = b - a
                    nch = n // 128
                    c0 = a // 128
                    p_ = ti % 2
                    pq, pk = banks[0 + p_], banks[2 + p_]
                    X = [(tm[("A", d, p_)], tm[("B", d, p_)], tm[("C", d, p_)]) for d in range(2)]
                    for d in range(2):
                        (A_, A_t), (B_, B_t), (C_, C_t) = X[d]
                        ACT(B_[:, :n], A_[:, :n], AF.Ln, [A_t], [B_t], bias=1.0)
                    for d in range(2):
                        (A_, A_t), (B_, B_t), (C_, C_t) = X[d]
                        if d == 0:
                            SCAN(C_[:, :n], cst[:, CSF:CSF + n], B_[:, :n], 0.0, [cst_t, B_t], [C_t])
                        else:
                            SCAN(C_[:, :n][:, ::-1], cst[:, CSB + 512 - n:CSB + 512][:, ::-1], B_[:, :n][:, ::-1], 0.0, [cst_t, B_t], [C_t])
                    for d in range(2):
                        (A_, A_t), (B_, B_t), (C_, C_t) = X[d]
                        if d == 0:
                            ACT(el[:, 0, c0:c0 + nch], C_[:, 127:n:128], AF.Exp, [C_t], [el_t[0][ti]], scale=-1.0 / 16)
                        else:
                            ACT(el[:, 1, c0:c0 + nch], C_[:, 0:n:128], AF.Exp, [C_t], [el_t[1][ti]], scale=-1.0 / 16)
                        ACT(A_[:, :n], C_[:, :n], AF.Exp, [C_t], [A_t], scale=-1.0 / 16)
                        ACT(B_[:, :n], C_[:, :n], AF.Exp, [C_t], [B_t], scale=1.0 / 16)
                    for d in range(2):
                        (A_, A_t), (B_, B_t), (C_, C_t) = X[d]
                        kd, kd_t = kdt[(d, p_)]
                        STT(qd[d][:, a:b], pq[:, :n], 128.0 ** -0.5, A_[:, :n], ALU.mult, ALU.mult, [pq, A_t], [qd_t[d][ti]])
                        TT(ki[d][:, a:b], pk[:, :n], B_[:, :n], ALU.mult, [pk, B_t], [ki_t[d][ti]])
                        TT(kd[:, :n].rearrange("p (c k) -> p c k", k=128), ki[d][:, a:b].rearrange("p (c k) -> p c k", k=128),
                           el[:, d, c0:c0 + nch].to_broadcast([128, nch, 128]) if False else el[:, d, c0:c0 + nch, None].to_broadcast([128, nch, 128]),
                           ALU.mult, [ki_t[d][ti], el_t[d][ti]], [kd_t])

                def stage3(ti):
                    a, b = TILES[ti]
                    n = b - a
                    nch = n // 128
                    c0 = a // 128
                    p_ = ti % 2
                    ptr = banks[5]
                    ptb = ptr.ap[:, :].bitcast(BF16)
                    for d in range(2):
                        kd, kd_t = kdt[(d, p_)]
                        for j in range(nch):
                            TR(ptb[:, d * 512 + j * 128:d * 512 + (j + 1) * 128], kd[:, j * 128:(j + 1) * 128], cstb[:, :], [kd_t, cstb_t], [ptr])
                    for d in range(2):
                        ACT(kt[d][:, c0:c0 + nch, :], ptb[:, d * 512:d * 512 + nch * 128].rearrange("p (c k) -> p c k", k=128), AF.Copy,
                            [ptr], [kt_t[d][ti]])

                NTI = len(TILES)
                stage1(0)
                for ti in range(NTI):
                    stage2(ti)
                    if ti + 1 < NTI:
                        stage1(ti + 1)
                    stage3(ti)

                P.mark("gla_state")
                chunks = [n_ for n_ in range(NCH) if not (last and n_ < 2)]

                def scores(n_):
                    P.cur = "sc%d_%d" % (h, n_)
                    ti = tile_of_chunk(n_)
                    cs = slice(n_ * 128, (n_ + 1) * 128)
                    p_ = n_ % 2
                    psc = banks[4 + p_]
                    for d in range(2):
                        MM(psc[:, d * 128:(d + 1) * 128], ki[d][:, cs], qd[d][:, cs], True, True, [ki_t[d][ti], qd_t[d][ti]], [psc])
                    for d in range(2):
                        TT(scm_all[:, d, n_, :], psc[:, d * 128:(d + 1) * 128], cst[:, (CMF, CMB)[d]:(CMF, CMB)[d] + 128], ALU.mult, [psc, cst_t], [scm_t[d][n_]])

                for d in range(2):
                    MEMSET(S[d][:], 0.0, [S_t[d]])
                for idx in range(NCH):
                    if idx < len(chunks):
                        scores(chunks[idx])
                    P.cur = "state"
                    for d, order in ((1, border), (0, forder)):
                        n_ = order[idx]
                        ti = tile_of_chunk(n_)
                        ACT(sb_[d][:, n_, :], S[d][:], AF.Copy, [S_t[d]], [sb_t[d][n_]])
                        if idx == NCH - 1:
                            continue
                        pp = banks[2 * d + (idx % 2)]
                        MM(pp[:, 0:256], kt[d][:, n_, :], vt[:, n_, :], True, True, [kt_t[d][ti], vt_t[ti]], [pp])
                        STT(S[d][:], S[d][:], el[:, d, n_:n_ + 1], pp[:, 0:256], ALU.mult, ALU.add, [S_t[d], el_t[d][ti], pp], [S_t[d]])

                P.mark("gla_out")
                def outs(n_):
                    P.cur = "out%d_%d" % (h, n_)
                    ti = tile_of_chunk(n_)
                    a, b = TILES[ti]
                    j = n_ - a // 128
                    ob = [banks[0 + 2 * (ti % 2)], banks[1 + 2 * (ti % 2)]]
                    cs = slice(n_ * 128, (n_ + 1) * 128)
                    p_ = n_ % 2
                    for vh in range(2):
                        o = ob[vh][:, j * 128:(j + 1) * 128]
                        vs = slice(vh * 128, (vh + 1) * 128)
                        MM(o, sb_[0][:, n_, vs], qd[0][:, cs], True, False, [sb_t[0][n_], qd_t[0][ti]], [ob[vh]])
                        MM(o, sb_[1][:, n_, vs], qd[1][:, cs], False, False, [sb_t[1][n_], qd_t[1][ti]], [ob[vh]])
                        MM(o, vt[:, n_, vs], scm_all[:, 0, n_, :], False, False, [vt_t[ti], scm_t[0][n_]], [ob[vh]])
                        MM(o, vt[:, n_, vs], scm_all[:, 1, n_, :], False, True, [vt_t[ti], scm_t[1][n_]], [ob[vh]])

                def epilogue(ti, k2):
                    P.cur = "epi%d_%d" % (h, ti)
                    a, b = TILES[ti]
                    nn = b - a
                    ob = [banks[0 + 2 * (ti % 2)], banks[1 + 2 * (ti % 2)]]
                    pg = banks[7]
                    pss = banks[6]
                    for vh in range(2):
                        ACT(sq[:, vh, :nn], ob[vh][:, :nn], AF.Square, [ob[vh]], [sq_t])
                    for kc in range(8):
                        MM(pg[:, :nn], W["g"].ap[:, kc, 0:128], hx[:, kc, a:b], kc == 0, kc == 7, [W["g"], hx_t[ti]], [pg])
                    yield
                    P.cur = "epi%d_%d" % (h, ti)
                    for vh in range(2):
                        MM(pss[:, :nn], ones_b[:, :], sq[:, vh, :nn], vh == 0, vh == 1, [ones_t, sq_t], [pss])
                    ACT(rs[:, :nn], pss[:, :nn], AF.Ln, [pss], [rs_t], bias=EPS, scale=1.0 / 256)
                    ACT(rs[:, :nn], rs[:, :nn], AF.Exp, [rs_t], [rs_t], scale=-0.5)
                    ACT(sg[:, 0, :nn], pg[:, :nn], AF.Silu, [pg], [sg_t])
                    TT(t1[:, :nn], ob[0][:, :nn], rs[:, :nn], ALU.mult, [ob[0], rs_t], [t1_t])
                    STT(ogt[k2][:, 0, :nn], t1[:, :nn], V("gnw%d" % l, h * 2 + 0), sg[:, 0, :nn], ALU.mult, ALU.mult,
                        [t1_t, vecs_t, sg_t], [ogt_t[k2]])
                    yield
                    P.cur = "epi%d_%d" % (h, ti)
                    for kc in range(8):
                        MM(pg[:, :nn], W["g"].ap[:, kc, 128:256], hx[:, kc, a:b], kc == 0, kc == 7, [W["g"], hx_t[ti]], [pg])
                    ACT(sg[:, 1, :nn], pg[:, :nn], AF.Silu, [pg], [sg_t])
                    TT(t1[:, :nn], ob[1][:, :nn], rs[:, :nn], ALU.mult, [ob[1], rs_t], [t1_t])
                    STT(ogt[k2][:, 1, :nn], t1[:, :nn], V("gnw%d" % l, h * 2 + 1), sg[:, 1, :nn], ALU.mult, ALU.mult,
                        [t1_t, vecs_t, sg_t], [ogt_t[k2]])
                    DMA("sp", ogv[:, h * 2:h * 2 + 2, a:b], ogt[k2][:, :, :nn], [ogt_t[k2]], [og_d])

                pending = []

                def advance():
                    if pending:
                        try:
                            next(pending[0])
                        except StopIteration:
                            pending.pop(0)
                            advance()

                for ci, n_ in enumerate(chunks):
                    outs(n_)
                    advance()
                    ti = tile_of_chunk(n_)
                    if (n_ + 1) * 128 == TILES[ti][1]:
                        pending.append(epilogue(ti, og_i % 2))
                        og_i += 1
                while pending:
                    advance()

        def rnn_phase(l, last, stack):
            win = kc_view(w_in[l])
            wsl = []
            for i in range(3):
                d = {}
                d["xr"] = T(sbuf(stack, "rwx%d" % i, [128, 8, 128], BF16))
                d["yr"] = T(sbuf(stack, "rwy%d" % i, [128, 8, 128], BF16))
                d["g"] = T(sbuf(stack, "rwg%d" % i, [128, 4, 128], BF16))
                wsl.append(d)

            def load_blk(g):
                d = wsl[g % 3]
                DMA("pool", d["xr"].ap[:], win[:, :, C_XR + g * 128:C_XR + (g + 1) * 128], [dummy], [d["xr"]])
                DMA("pool", d["yr"].ap[:], win[:, :, C_YR + g * 128:C_YR + (g + 1) * 128], [dummy], [d["yr"]])
                DMA("pool", d["g"].ap[:, 0:2, :], rnn_wa[l, :, g].rearrange("d i j -> i d j"), [dummy], [d["g"]])
                DMA("pool", d["g"].ap[:, 2:4, :], rnn_wx[l, :, g].rearrange("d i j -> i d j"), [dummy], [d["g"]])
            load_blk(0)
            load_blk(1)
            XP = 2312
            xrp = sbuf(stack, "xrp", [128, XP], BF16); xrp_t = T(xrp)
            MEMSET(xrp[:], 0.0, [xrp_t])
            xc2 = [sbuf(stack, "xc%d" % i, [128, NT]) for i in range(2)]; xc2_t = [T(xc2[0]), T(xc2[1])]
            xcb2 = [sbuf(stack, "xcb%d" % i, [128, NT], BF16) for i in range(2)]; xcb2_t = [T(xcb2[0]), T(xcb2[1])]
            _gy = sbuf(stack, "rgy", [128, NT]); _gyt = T(_gy)
            gy = [_gy, _gy, _gy]; gy_t = [_gyt, _gyt, _gyt]
            dgr2 = [sbuf(stack, "dgr%d" % i, [128, 4, 128], BF16) for i in range(2)]; dgr2_t = [T(dgr2[0]), T(dgr2[1])]
            rb = [[sbuf(stack, "rb%d%d" % (q, d), [128, NT]) for d in range(2)] for q in range(2)]
            ib = [[sbuf(stack, "ib%d%d" % (q, d), [128, NT]) for d in range(2)] for q in range(2)]
            rb_t = [[T(rb[q][d]) for d in range(2)] for q in range(2)]
            ib_t = [[T(ib[q][d]) for d in range(2)] for q in range(2)]
            hv = [sbuf(stack, "rh%d" % d, [128, NT]) for d in range(2)]; hv_t = [T(hv[0]), T(hv[1])]
            hvb = hv[1][:, :].bitcast(BF16)
            rrv = rr_d.ap.rearrange("(c p) n -> p c n", p=128)
            o0 = NCTX if last else 0

            def pos(t):
                return 1 + t if t < NCTX else 260 + (t - NCTX)

            def front_x_tiles(g):
                W = wsl[g % 3]
                steps = []

                def mk(ti, a, b):
                    def step():
                        P.cur = "frontx%d" % g
                        if ti == 0:
                            for tap in range(4):
                                TS(dgr2[g % 2][:, tap, :], cst[:, CI:CI + 128], V("rcw%d" % l, tap * 8 + g), None, ALU.mult, ALU.bypass,
                                   [cst_t, vecs_t], [dgr2_t[g % 2]])
                        n = b - a
                        pb = banks[ti % 2]
                        for kc in range(8):
                            MM(pb[:, :n], W["xr"].ap[:, kc, :], hx[:, kc, a:b], kc == 0, kc == 7, [W["xr"], hx_t[ti]], [pb])
                        ACT(xrp[:, pos(a):pos(a) + n], pb[:, :n], AF.Copy, [pb], [xrp_t])
                    return step
                for ti, (a, b) in enumerate(TILES):
                    steps.append(mk(ti, a, b))
                return steps

            def front_x(g):
                for st_ in front_x_tiles(g):
                    st_()

            def front_c(g):
                P.cur = "frontc%d" % g
                W = wsl[g % 3]
                q_ = g % 2
                xc, xc_t, xcb, xcb_t = xc2[q_], xc2_t[q_], xcb2[q_], xcb2_t[q_]
                for ti, (a, b) in enumerate(TILES):
                    n = b - a
                    pb = banks[2 + (ti % 2)]
                    for tap in range(4):
                        o_ = pos(a) + tap - 1
                        MM(pb[:, :n], dgr2[g % 2][:, tap, :], xrp[:, o_:o_ + n], tap == 0, tap == 3, [dgr2_t[g % 2], xrp_t], [pb])
                    ACT(xc[:, a:b], pb[:, :n], AF.Identity, [pb, vecs_t], [xc_t], bias=V("rcb%d" % l, g))
                    DCOPY(xcb[:, a:b], xc[:, a:b], [xc_t], [xcb_t])

            def yr_tiles(g):
                W = wsl[g % 3]
                steps = []

                def mk(ti, a, b):
                    def step():
                        P.cur = "yr%d" % g
                        n = b - a
                        pb = banks[(4, 5, 6, 7, 0)[ti]]
                        for kc in range(8):
                            MM(pb[:, :n], W["yr"].ap[:, kc, :], hx[:, kc, a:b], kc == 0, kc == 7, [W["yr"], hx_t[ti]], [pb])
                        ACT(gy[g % 3][:, a:b], pb[:, :n], AF.Gelu_apprx_tanh, [pb], [gy_t[g % 3]])
                    return step
                for ti, (a, b) in enumerate(TILES):
                    if last and ti == 0:
                        continue
                    steps.append(mk(ti, a, b))
                return steps

            def gate_steps(g):
                W = wsl[g % 3]
                q_ = g % 2
                xc, xc_t, xcb, xcb_t = xc2[q_], xc2_t[q_], xcb2[q_], xcb2_t[q_]
                steps = []
                bi = [0]

                def mk(d, ti, a, b):
                    def step():
                        P.cur = "gates%d" % g
                        n = b - a
                        pr, pi = banks[4 + (bi[0] % 4)], banks[4 + ((bi[0] + 1) % 4)]
                        bi[0] += 2
                        MM(pr[:, :n], W["g"].ap[:, d, :], xcb[:, a:b], True, True, [W["g"], xcb_t], [pr])
                        MM(pi[:, :n], W["g"].ap[:, 2 + d, :], xcb[:, a:b], True, True, [W["g"], xcb_t], [pi])
                        ACT(rb[q_][d][:, a:b], pr[:, :n], AF.Sigmoid, [pr, vecs_t], [rb_t[q_][d]], bias=V("rba%d" % l, d * 8 + g))
                        ACT(ib[q_][d][:, a:b], pi[:, :n], AF.Sigmoid, [pi, vecs_t], [ib_t[q_][d]], bias=V("rbx%d" % l, d * 8 + g))
                        if ti == len(TILES) - 1:
                            TT(ib[q_][d][:, :], ib[q_][d][:, :], xc[:, :], ALU.mult, [ib_t[q_][d], xc_t], [ib_t[q_][d]])
                    return step
                for d in range(2):
                    for ti, (a, b) in enumerate(TILES):
                        steps.append(mk(d, ti, a, b))
                return steps

            def mid_steps(g):
                q_ = g % 2
                L0 = dv[:, DV_L + 0 * 8 + g:DV_L + 0 * 8 + g + 1]
                L1 = dv[:, DV_L + 1 * 8 + g:DV_L + 1 * 8 + g + 1]

                def s0():
                    P.cur = "mid%d" % g
                    ACT(rb[q_][0][:, :], rb[q_][0][:, :], AF.Exp, [rb_t[q_][0], dv_t], [rb_t[q_][0]], scale=L0)

                def s1():
                    P.cur = "mid%d" % g
                    ACT(rb[q_][1][:, :], rb[q_][1][:, :], AF.Exp, [rb_t[q_][1], dv_t], [rb_t[q_][1]], scale=L1)
                    for d in range(2):
                        TT(hv[d][:, :], rb[q_][d][:, :], rb[q_][d][:, :], ALU.mult, [rb_t[q_][d]], [hv_t[d]])

                def s2():
                    P.cur = "mid%d" % g
                    ACT(hv[0][:, :], hv[0][:, :], AF.Ln, [hv_t[0]], [hv_t[0]], bias=1.0, scale=-1.0)

                def s3():
                    P.cur = "mid%d" % g
                    ACT(hv[0][:, :], hv[0][:, :], AF.Exp, [hv_t[0]], [hv_t[0]], scale=0.5)

                def s4():
                    P.cur = "mid%d" % g
                    ACT(hv[1][:, :], hv[1][:, :], AF.Ln, [hv_t[1]], [hv_t[1]], bias=1.0, scale=-1.0)
                    TT(ib[q_][0][:, :], ib[q_][0][:, :], hv[0][:, :], ALU.mult, [ib_t[q_][0], hv_t[0]], [ib_t[q_][0]])

                def s5():
                    P.cur = "mid%d" % g
                    ACT(hv[1][:, :], hv[1][:, :], AF.Exp, [hv_t[1]], [hv_t[1]], scale=0.5)
                    TT(ib[q_][1][:, :], ib[q_][1][:, :], hv[1][:, :], ALU.mult, [ib_t[q_][1], hv_t[1]], [ib_t[q_][1]])
                return [s0, s1, s2, s3, s4, s5]

            def mid_a(g):
                P.cur = "mid_a%d" % g
                q_ = g % 2
                for d in range(2):
                    Lc = dv[:, DV_L + d * 8 + g:DV_L + d * 8 + g + 1]
                    ACT(rb[q_][d][:, :], rb[q_][d][:, :], AF.Exp, [rb_t[q_][d], dv_t], [rb_t[q_][d]], scale=Lc)
                for d in range(2):
                    TT(hv[d][:, :], rb[q_][d][:, :], rb[q_][d][:, :], ALU.mult, [rb_t[q_][d]], [hv_t[d]])

            def mid_b(g):
                P.cur = "mid_b%d" % g
                q_ = g % 2
                for d in range(2):
                    ACT(hv[d][:, :], hv[d][:, :], AF.Ln, [hv_t[d]], [hv_t[d]], bias=1.0, scale=-1.0)
                    ACT(hv[d][:, :], hv[d][:, :], AF.Exp, [hv_t[d]], [hv_t[d]], scale=0.5)
                for d in range(2):
                    TT(ib[q_][d][:, :], ib[q_][d][:, :], hv[d][:, :], ALU.mult, [ib_t[q_][d], hv_t[d]], [ib_t[q_][d]])

            def tail(g):
                P.cur = "tail%d" % g
                q_ = g % 2
                A0, A1, U0, U1 = rb[q_][0], rb[q_][1], ib[q_][0], ib[q_][1]
                SCAN(hv[0][:, :], A0[:, :], U0[:, :], 0.0, [rb_t[q_][0], ib_t[q_][0]], [hv_t[0]])
                SCAN(hv[1][:, 0:NCTX][:, ::-1], A1[:, 0:NCTX][:, ::-1], U1[:, 0:NCTX][:, ::-1], 0.0, [rb_t[q_][1], ib_t[q_][1]], [hv_t[1]])
                SCAN(hv[1][:, NCTX:NT][:, ::-1], A1[:, NCTX:NT][:, ::-1], U1[:, NCTX:NT][:, ::-1], hv[1][:, 0:1],
                     [rb_t[q_][1], ib_t[q_][1], hv_t[1]], [hv_t[1]])
                TT(hv[0][:, o0:NT], hv[0][:, o0:NT], hv[1][:, o0:NT], ALU.add, [hv_t[0], hv_t[1]], [hv_t[0]])
                TT(hvb[:, o0:NT], hv[0][:, o0:NT], gy[g % 3][:, o0:NT], ALU.mult, [hv_t[0], gy_t[g % 3]], [hv_t[1]])
                DMA("sp", rrv[:, g, o0:NT], hvb[:, o0:NT], [hv_t[1]], [rr_d])

            front_x(0)
            front_c(0)
            front_x(1)
            front_c(1)
            for g in range(8):
                fx = []
                if g + 2 < 8:
                    load_blk(g + 2)
                    fx = front_x_tiles(g + 2)
                for k_, st_ in enumerate(gate_steps(g)):
                    st_()
                    if k_ % 2 == 1 and fx:
                        fx.pop(0)()
                while fx:
                    fx.pop(0)()
                if g + 2 < 8:
                    front_c(g + 2)
                for st_ in mid_steps(g):
                    st_()
                for st_ in yr_tiles(g):
                    st_()
                tail(g)

        def merge_phase(l, last, src, dst, stack):
            win = kc_view(w_in[l])
            names = ["go", "ga", "ro", "gb", "out"]
            srcs = [kc_view(w_go[l]), win[:, :, C_GA:C_GA + D], kc_view(w_ro[l]), win[:, :, C_GB:C_GB + D], kc_view(w_out[l])]
            Wm = {}
            Wh = {}
            for nm, sv in zip(names, srcs):
                tns = sbuf(stack, "mw" + nm, [128, 8, D], BF16)
                Wm[nm] = T(tns)
                segs = [(0, 512), (512, 1024)] if nm == "out" else [(0, 256), (256, 512), (512, 1024)]
                Wh[nm] = [(T(tns[:, :, c0:c1]), c0, c1) for (c0, c1) in segs]
            order = [(nm, sv, si) for si in range(3) for nm, sv in list(zip(names, srcs))[:4]]
            order += [(names[4], srcs[4], si) for si in range(2)]
            for nm, sv, si in order:
                wt_, c0, c1 = Wh[nm][si]
                DMA("pool", Wm[nm].ap[:, :, c0:c1], sv[:, :, c0:c1], [dummy], [wt_])

            def Wsel(nm, j):
                for wt_, c0, c1 in Wh[nm]:
                    if c0 <= j * 128 < c1:
                        return wt_
            ogs2 = [sbuf(stack, "m_og%d" % i, [128, 8, 512], BF16) for i in range(2)]; ogs2_t = [T(ogs2[0]), T(ogs2[1])]
            rrs2 = [sbuf(stack, "m_rr%d" % i, [128, 8, 512], BF16) for i in range(2)]; rrs2_t = [T(rrs2[0]), T(rrs2[1])]
            xt = sbuf(stack, "m_x", [128, 8, 512]); xt_t = T(xt)
            xn, xn_t = xt, xt_t
            mg = sbuf(stack, "m_mg", [128, 8, 512], BF16); mg_t = T(mg)
            sa = sbuf(stack, "m_sa", [128, 512]); sa_t = T(sa)
            sb_ = sbuf(stack, "m_sb", [128, 512]); sb_t = T(sb_)
            m1 = sbuf(stack, "m_m1", [128, 512]); m1_t = T(m1)
            m2 = sbuf(stack, "m_m2", [128, 512]); m2_t = T(m2)
            rs = sbuf(stack, "m_rs", [128, 512])
            t1 = [sbuf(stack, "m_t1%d" % i, [128, 512]) for i in range(2)]
            tmp = (mg, mg_t, rs, T(rs), t1, [T(t1[0]), T(t1[1])])
            ogv = og_d.ap.rearrange("(c p) n -> p c n", p=128)
            rrv = rr_d.ap.rearrange("(c p) n -> p c n", p=128)
            srcv = kc_view(src.ap)
            dstv = kc_view(dst.ap)
            m_tiles = [ti for ti in range(len(TILES)) if not (last and ti == 0)]

            def m_loads(ix):
                ti_ = m_tiles[ix]
                a_, b_ = TILES[ti_]
                DMA("sp", ogs2[ix % 2][:, :, :b_ - a_], ogv[:, :, a_:b_], [og_d], [ogs2_t[ix % 2]])
                DMA("sp", rrs2[ix % 2][:, :, :b_ - a_], rrv[:, :, a_:b_], [rr_d], [rrs2_t[ix % 2]])

            def n2_square(tp):
                ap_, bp_ = TILES[tp]
                ACT(mg[:, :, :bp_ - ap_], xn[:, :, :bp_ - ap_], AF.Square, [xn_t], [mg_t])

            def n2_stat(tp):
                ap_, bp_ = TILES[tp]
                np_ = bp_ - ap_
                pb = banks[7]
                for kc in range(8):
                    MM(pb[:, :np_], ones_b[:, :], mg[:, kc, :np_], kc == 0, kc == 7, [ones_t, mg_t], [pb])

            def n2_rest(tp):
                ap_, bp_ = TILES[tp]
                np_ = bp_ - ap_
                sp_ = 0 if tp > 0 else 1
                rs_t_, t1l, t1l_t = tmp[3], tmp[4], tmp[5]
                pb = banks[7]
                ACT(rs[:, :np_], pb[:, :np_], AF.Ln, [pb], [rs_t_], bias=EPS, scale=1.0 / D)
                ACT(rs[:, :np_], rs[:, :np_], AF.Exp, [rs_t_], [rs_t_], scale=-0.5)
                for kc in range(8):
                    TT(t1l[kc % 2][:, :np_], xn[:, kc, :np_], rs[:, :np_], ALU.mult, [xn_t, rs_t_], [t1l_t[kc % 2]])
                    so = (DV_S2X if sp_ == 0 else DV_S2C) + kc
                    ACT(hx[:, kc, ap_:bp_], t1l[kc % 2][:, :np_], AF.Identity, [t1l_t[kc % 2], dv_t, mod_t], [hx_t[tp]],
                        bias=modcol(3, kc, sp_), scale=dv[:, so:so + 1])

            m_loads(0)
            for ix, ti in enumerate(m_tiles):
                a, b = TILES[ti]
                n = b - a
                s = 0 if ti > 0 else 1
                ogs, ogs_t = ogs2[ix % 2], ogs2_t[ix % 2]
                rrs, rrs_t = rrs2[ix % 2], rrs2_t[ix % 2]
                if ix + 1 < len(m_tiles):
                    m_loads(ix + 1)
                if ix > 0:
                    n2_square(m_tiles[ix - 1])
                else:
                    DMA("sp", xt[:, :, :n], srcv[:, :, a:b], [src], [xt_t])
                for j in range(8):
                    if j == 1 and ix > 0:
                        n2_rest(m_tiles[ix - 1])
                        DMA("sp", xt[:, :, :n], srcv[:, :, a:b], [src], [xt_t])
                    k4 = 4 * (j % 2)
                    pA, pGA, pB, pGB = banks[k4], banks[k4 + 1], banks[k4 + 2], banks[k4 + 3]
                    js = slice(j * 128, (j + 1) * 128)
                    for kc in range(8):
                        MM(pA[:, :n], Wm["go"].ap[:, kc, js], ogs[:, kc, :n], kc == 0, kc == 7, [Wsel("go", j), ogs_t], [pA])
                    for kc in range(8):
                        MM(pGA[:, :n], Wm["ga"].ap[:, kc, js], hx[:, kc, a:b], kc == 0, kc == 7, [Wsel("ga", j), hx_t[ti]], [pGA])
                    for kc in range(8):
                        MM(pB[:, :n], Wm["ro"].ap[:, kc, js], rrs[:, kc, :n], kc == 0, kc == 7, [Wsel("ro", j), rrs_t], [pB])
                    for kc in range(8):
                        MM(pGB[:, :n], Wm["gb"].ap[:, kc, js], hx[:, kc, a:b], kc == 0, kc == 7, [Wsel("gb", j), hx_t[ti]], [pGB])
                    if j == 0 and ix > 0:
                        n2_stat(m_tiles[ix - 1])
                    ACT(sa[:, :n], pGA[:, :n], AF.Sigmoid, [pGA], [sa_t])
                    ACT(sb_[:, :n], pGB[:, :n], AF.Sigmoid, [pGB], [sb_t])
                    TT(m1[:, :n], pA[:, :n], sa[:, :n], ALU.mult, [pA, sa_t], [m1_t])
                    TT(m2[:, :n], pB[:, :n], sb_[:, :n], ALU.mult, [pB, sb_t], [m2_t])
                    TT(mg[:, j, :n], m1[:, :n], m2[:, :n], ALU.add, [m1_t, m2_t], [mg_t])
                for j in range(8):
                    pM = banks[j % 4]
                    js = slice(j * 128, (j + 1) * 128)
                    for kc in range(8):
                        MM(pM[:, :n], Wm["out"].ap[:, kc, js], mg[:, kc, :n], kc == 0, kc == 7, [Wsel("out", j), mg_t], [pM])
                    STT(xn[:, j, :n], pM[:, :n], modcol(2, j, s), xt[:, j, :n], ALU.mult, ALU.add, [pM, mod_t, xt_t], [xn_t])
                DMA("sp", dstv[:, :, a:b], xn[:, :, :n], [xn_t], [dst])
                if ix == len(m_tiles) - 1:
                    norm_tile(xn, xn_t, ti, (DV_S2X, DV_S2C), 3, hx_t[ti], hx[:, :, a:b], tmp)

        def ffn_phase(l, last, src, dst, stack):
            upv = kc_view(ffn_up[l])
            act = sbuf(stack, "f_act", [128, NHC, NT], BF16)
            act_t = [T(act[:, :, a:b]) for (a, b) in TILES]
            with ExitStack() as s2:
                wsl = [T(sbuf(s2, "fw%d" % i, [128, 8, 256], BF16)) for i in range(3)]

                def load_hc(hc):
                    w_ = wsl[hc % 3]
                    DMA("pool", w_.ap[:, :, 0:128], upv[:, :, hc * 128:(hc + 1) * 128], [dummy], [w_])
                    DMA("pool", w_.ap[:, :, 128:256], upv[:, :, FH + hc * 128:FH + (hc + 1) * 128], [dummy], [w_])
                load_hc(0)
                load_hc(1)
                apad = sbuf(s2, "f_apad", [128, 34, 66], BF16); apad_t = T(apad)
                cpad = sbuf(s2, "f_cpad", [128, 258], BF16); cpad_t = T(cpad)
                dg = [sbuf(s2, "f_dg%d" % i, [128, 9, 128], BF16) for i in range(2)]; dg_t = [T(dg[0]), T(dg[1])]
                gl = sbuf(s2, "f_gl", [128, 512]); gl_t = T(gl)
                MEMSET(apad[:], 0.0, [apad_t])
                MEMSET(cpad[:], 0.0, [cpad_t])
                for hc in range(NHC):
                    W = wsl[hc % 3]
                    if hc + 2 < NHC:
                        load_hc(hc + 2)
                    D_ = dg[hc % 2]
                    D_t = dg_t[hc % 2]
                    for tap in range(9):
                        TS(D_[:, tap, :], cst[:, CI:CI + 128], V("fcw%d" % l, tap * NHC + hc), None, ALU.mult, ALU.bypass, [cst_t, vecs_t], [D_t])
                    for ti, (a, b) in enumerate(TILES):
                        if last and ti == 0:
                            continue
                        n = b - a
                        pb = banks[ti % 2]
                        for kc in range(8):
                            MM(pb[:, :n], W.ap[:, kc, 0:128], hx[:, kc, a:b], kc == 0, kc == 7, [W, hx_t[ti]], [pb])
                        if ti == 0:
                            ACT(cpad[:, 1:257], pb[:, :n], AF.Copy, [pb], [cpad_t])
                        else:
                            r0 = (ti - 1) * 8
                            ACT(apad[:, r0 + 1:r0 + 9, 1:65], pb[:, :n].rearrange("p (r c) -> p r c", c=64), AF.Copy, [pb], [apad_t])
                    for ti, (a, b) in enumerate(TILES):
                        if last and ti == 0:
                            continue
                        n = b - a
                        pc = banks[2 + (ti % 2)]
                        pg = banks[4 + (ti % 2)]
                        if ti == 0:
                            for i_, dc in enumerate((-1, 0, 1)):
                                MM(pc[:, :n], D_[:, 3 + dc + 1, :], cpad[:, 1 + dc:257 + dc], i_ == 0, i_ == 2, [D_t, cpad_t], [pc])
                        else:
                            r0 = (ti - 1) * 8
                            i_ = 0
                            for dr in (-1, 0, 1):
                                for dc in (-1, 0, 1):
                                    tap = (dr + 1) * 3 + (dc + 1)
                                    MM(pc[:, :n].rearrange("p (r c) -> p r c", c=64), D_[:, tap, :],
                                       apad[:, r0 + 1 + dr:r0 + 9 + dr, 1 + dc:65 + dc], i_ == 0, i_ == 8, [D_t, apad_t], [pc])
                                    i_ += 1
                        for kc in range(8):
                            MM(pg[:, :n], W.ap[:, kc, 128:256], hx[:, kc, a:b], kc == 0, kc == 7, [W, hx_t[ti]], [pg])
                        ACT(gl[:, :n], pc[:, :n], AF.Gelu_apprx_tanh, [pc, vecs_t], [gl_t], bias=V("fcb%d" % l, hc))
                        TT(act[:, hc, a:b], pg[:, :n], gl[:, :n], ALU.mult, [pg, gl_t], [act_t[ti]])
            P.barrier()
            with ExitStack() as s3:
                dnv = ffn_dn[l].rearrange("(hc p) n -> p hc n", p=128)
                wdp_t = [T(wd[:, q:q + 2, :]) for q in range(0, NHC, 2)]
                for q in range(0, NHC, 2):
                    DMA("pool", wd[:, q:q + 2, :], dnv[:, q:q + 2, :], [dummy], [wdp_t[q // 2]] + (hx_t if q == 0 else []) + ([wd_t] if q == 0 else []))
                xt = sbuf(s3, "d_x", [128, 8, 512]); xt_t = T(xt)
                xn, xn_t = xt, xt_t
                yo, yo_t = xt, xt_t
                sq = sbuf(s3, "d_sq", [128, 8, 512], BF16); rs = sbuf(s3, "d_rs", [128, 512])
                t1 = [sbuf(s3, "d_t1%d" % i, [128, 512]) for i in range(2)]
                tmp = (sq, T(sq), rs, T(rs), t1, [T(t1[0]), T(t1[1])])
                srcv = kc_view(src.ap)
                for ti, (a, b) in enumerate(TILES):
                    if last and ti == 0:
                        continue
                    n = b - a
                    s = 0 if ti > 0 else 1
                    DMA("sp", xt[:, :, :n], srcv[:, :, a:b], [src], [xt_t])
                    for j in range(8):
                        pb = banks[j]
                        for hc in range(NHC):
                            MM(pb[:, :n], wd[:, hc, j * 128:(j + 1) * 128], act[:, hc, a:b], hc == 0, hc == NHC - 1, [wdp_t[hc // 2], wd_t, act_t[ti]], [pb])
                    for j in range(8):
                        STT(xn[:, j, :n], banks[j][:, :n], modcol(5, j, s), xt[:, j, :n], ALU.mult, ALU.add, [banks[j], mod_t, xt_t], [xn_t])
                    if not last:
                        DMA("sp", kc_view(dst.ap)[:, :, a:b], xn[:, :, :n], [xn_t], [dst])
                        if ti == len(TILES) - 1:
                            ACT(dv[:, 255:256], vecs[:, 0:1], AF.Copy, [vecs_t], [wd_t])
                    else:
                        norm_tile(xn, xn_t, ti, None, None, yo_t, yo, tmp, final=True)
                        ev = DMA("sp", kc_view(outT)[:, :, a - NCTX:b - NCTX], yo[:, :, :n], [yo_t], [outT_t])
                        out_evs.append(ev)

        for l in range(DEPTH):
            last = (l == DEPTH - 1)
            src0 = xs_s[2 * l]
            mid = xs_s[2 * l + 1]
            nxt = xs_s[2 * l + 2] if not last else None
            P.barrier()
            with ExitStack() as s1:
                wslots = [T(sbuf(s1, "aw%d" % i, [128, 4096], BF16)) for i in range(3)]
                ada_phase(l, s1, wslots)
            P.barrier()
            if debug:
                out_evs.append(DMA("sp", dbg["mod%d" % l], mod[:, :], [mod_t], [T(None)]))
                out_evs.append(DMA("sp", dbg["dv%d" % l], dv[:, :], [dv_t], [T(None)]))
            with ExitStack() as s1:
                norm1_phase(l, src0, s1)
            if debug:
                out_evs.append(DMA("sp", dbg["hx%d" % l], RR[:, 0:8 * NT], hx_t, [T(None)]))
            P.barrier()
            with ExitStack() as s1:
                gla_phase(l, last, s1)
            P.barrier()
            with ExitStack() as s1:
                rnn_phase(l, last, s1)
            P.barrier()
            if debug:
                out_evs.append(DMA("sp", dbg["og%d" % l], og_d.ap, [og_d], [T(None)]))
                out_evs.append(DMA("sp", dbg["rr%d" % l], rr_d.ap, [rr_d], [T(None)]))
            with ExitStack() as s1:
                merge_phase(l, last, src0, mid, s1)
            if debug:
                out_evs.append(DMA("sp", dbg["hx2_%d" % l], RR[:, 0:8 * NT], hx_t, [T(None)]))
            P.barrier()
            with ExitStack() as s1:
                ffn_phase(l, last, mid, nxt, s1)
        P.finish(out_evs)
        with nc.Block() as block:
            P.emit(sems, block)
    return nc


_NC_CACHE = {}


def kernel(**inp):
    inp = {k: np.asarray(v) for k, v in inp.items()}
    B = inp["x"].shape[0]
    if "nc" not in _NC_CACHE:
        _NC_CACHE["nc"] = build_nc()
    nc = _NC_CACHE["nc"]
    cst = _consts()
    shared = {
        "cst": cst,
        "ada_w": np.ascontiguousarray(inp["ada_w"], np.float32),
        "w_in": np.ascontiguousarray(inp["w_in"], np.float32),
        "gla_lr_w": np.ascontiguousarray(inp["gla_lr_w"], np.float32),
        "rnn_wa": np.ascontiguousarray(inp["rnn_wa"], np.float32),
        "rnn_wx": np.ascontiguousarray(inp["rnn_wx"], np.float32),
        "w_gla_o": np.ascontiguousarray(inp["w_gla_o"], np.float32),
        "w_rnn_o": np.ascontiguousarray(inp["w_rnn_o"], np.float32),
        "w_out": np.ascontiguousarray(inp["w_out"], np.float32),
        "ffn_up": np.ascontiguousarray(inp["ffn_up"], np.float32),
        "ffn_down": np.ascontiguousarray(inp["ffn_down"], np.float32),
    }
    in_maps = []
    for b in range(B):
        m = dict(shared)
        m["xs0"] = np.ascontiguousarray(np.concatenate([inp["ctx"][b].T, inp["x"][b].T], axis=1), np.float32)
        m["vecs"] = _pack_vecs(inp, b)
        in_maps.append(m)
    res = run_bass_kernel_spmd(nc, in_maps, core_ids=list(range(B)))
    out = np.stack([np.asarray(r["outT"]).T for r in res.results], axis=0)
    return np.ascontiguousarray(out, np.float32)
```

```python
import numpy as np
from contextlib import ExitStack
import concourse.bass as bass
import concourse.mybir as mybir
from concourse.bass_utils import run_bass_kernel_spmd

F32 = mybir.dt.float32
BF16 = mybir.dt.bfloat16
ALU = mybir.AluOpType
AF = mybir.ActivationFunctionType

D = 1024
NCTX = 256
SEQ = 2048
NT = NCTX + SEQ
DEPTH = 2
D_IN = 7200
FH = 2816
NHC = FH // 128
NCH = NT // 128
TILES = [(0, 256), (256, 768), (768, 1280), (1280, 1792), (1792, 2304)]
EPS = 1e-6
NDMA_SEM = 8

C_Q, C_K, C_V, C_G, C_LR, C_XR, C_YR, C_GA, C_GB = 0, 512, 1024, 2048, 3072, 3104, 4128, 5152, 6176


class T:
    __slots__ = ("ap", "w", "r", "name")

    def __init__(self, ap, name=""):
        self.ap = ap
        self.w = None
        self.r = []
        self.name = name

    def __getitem__(self, k):
        return self.ap[k]


class Eng:
    def __init__(self, name):
        self.name = name
        self.ops = []
        self.count = 0
        self.seen = {}
        self.dma_i = 0
        self.pending = {}


class Prog:
    def __init__(self, nc):
        self.nc = nc
        self.E = {n: Eng(n) for n in ("pe", "act", "dve", "pool", "sp")}
        self.semnames = ["s_" + n for n in self.E]
        for q in ("sp", "pool"):
            for j in range(NDMA_SEM):
                self.semnames.append("d_%s_%d" % (q, j))

    def _need(self, eng, ev, waits):
        if ev is None:
            return
        key, val, _ = ev
        if eng.seen.get(key, 0) >= val:
            return
        waits[key] = max(waits.get(key, 0), val)

    def op(self, engname, fn, reads=(), writes=()):
        eng = self.E[engname]
        waits = {}
        for b in reads:
            if b.w is not None and not (b.w[2] == engname and engname == "pe"):
                self._need(eng, b.w, waits)
        for b in writes:
            if b.w is not None and b.w[2] != engname:
                self._need(eng, b.w, waits)
            for ev in b.r:
                if ev[2] != engname:
                    self._need(eng, ev, waits)
        self._merge_pending(eng, waits)
        for k, v in waits.items():
            eng.seen[k] = v
        eng.count += 1
        key = "s_" + engname
        ev = (key, eng.count, engname)
        eng.ops.append((list(waits.items()), fn, (key, 1)))
        if not hasattr(self, "labels"):
            self.labels = {}
        self.labels.setdefault(engname, []).append(getattr(self, "cur", ""))
        for b in writes:
            b.w = ev
            b.r = []
        for b in reads:
            if b not in writes:
                b.r = [e for e in b.r if e[2] != engname] + [ev]
        return ev

    def dma(self, qname, fn, reads=(), writes=()):
        eng = self.E[qname]
        j = eng.dma_i % NDMA_SEM
        rnd = eng.dma_i // NDMA_SEM
        eng.dma_i += 1
        key = "d_%s_%d" % (qname, j)
        waits = {}
        if rnd > 0:
            self._need(eng, (key, 16 * rnd, "dma"), waits)
        for b in reads:
            self._need(eng, b.w, waits)
        for b in writes:
            self._need(eng, b.w, waits)
            for ev in b.r:
                self._need(eng, ev, waits)
        self._merge_pending(eng, waits)
        for k, v in waits.items():
            eng.seen[k] = v
        ev = (key, 16 * (rnd + 1), "dma_" + qname)
        eng.ops.append((list(waits.items()), fn, (key, 16)))
        for b in writes:
            b.w = ev
            b.r = []
        for b in reads:
            if b not in writes:
                b.r = b.r + [ev]
        return ev

    def _merge_pending(self, eng, waits):
        for k, v in eng.pending.items():
            if eng.seen.get(k, 0) < v:
                waits[k] = max(waits.get(k, 0), v)
        eng.pending = {}

    def mark(self, name):
        if not hasattr(self, "marks"):
            self.marks = []
        self.marks.append((name, {n: e.count for n, e in self.E.items()}))

    def barrier(self):
        snap = {}
        for n, e in self.E.items():
            if e.count > 0:
                snap["s_" + n] = e.count
            if n in ("sp", "pool"):
                for i in range(min(e.dma_i, NDMA_SEM)):
                    cnt = (e.dma_i - 1 - i) // NDMA_SEM + 1
                    snap["d_%s_%d" % (n, i)] = 16 * cnt
        for n, e in self.E.items():
            for k, v in snap.items():
                if k == "s_" + n:
                    continue
                e.pending[k] = max(e.pending.get(k, 0), v)

    def finish(self, evs):
        eng = self.E["sp"]
        waits = {}
        for ev in evs:
            self._need(eng, ev, waits)
        eng.ops.append((list(waits.items()), None, None))

    def emit(self, sems, block):
        hw = {"pe": "tensor", "act": "scalar", "dve": "vector", "pool": "gpsimd", "sp": "sync"}

        def mk(engname):
            eng = self.E[engname]

            def body(e):
                for waits, fn, inc in eng.ops:
                    for k, v in waits:
                        e.wait_ge(sems[k], v)
                    if fn is not None:
                        fn(e).then_inc(sems[inc[0]], inc[1])
            return body

        for n in self.E:
            if self.E[n].ops:
                getattr(block, hw[n])(mk(n))


def _vec_layout():
    off = {}
    n = 0

    def add(name, cols):
        nonlocal n
        off[name] = n
        n += cols
    add("c", 8)
    add("cctx", 8)
    add("fnw", 8)
    for l in range(DEPTH):
        add("adab%d" % l, 48)
        add("n1w%d" % l, 8)
        add("n2w%d" % l, 8)
        add("lrb%d" % l, 8)
        add("gnw%d" % l, 8)
        add("rcw%d" % l, 32)
        add("rcb%d" % l, 8)
        add("rba%d" % l, 16)
        add("rbx%d" % l, 16)
        add("rlam%d" % l, 16)
        add("fcw%d" % l, 9 * NHC)
        add("fcb%d" % l, NHC)
    return off, n


VOFF, NV = _vec_layout()
CI, CMF, CMB, CSF, CSB, NCST = 0, 128, 256, 384, 896, 1408


def _col(v):
    v = np.asarray(v, np.float32).reshape(-1, 128)
    return v.T


def _pack_vecs(inp, b):
    V = np.zeros((128, NV), np.float32)

    def put(name, arr):
        a = _col(arr)
        V[:, VOFF[name]:VOFF[name] + a.shape[1]] = a
    put("c", inp["c"][b])
    put("cctx", inp["c_ctx"])
    put("fnw", inp["final_norm_w"])
    for l in range(DEPTH):
        put("adab%d" % l, inp["ada_b"][l])
        put("n1w%d" % l, inp["norm1_w"][l])
        put("n2w%d" % l, inp["norm2_w"][l])
        put("lrb%d" % l, inp["gla_lr_b"][l])
        put("gnw%d" % l, inp["gla_norm_w"][l])
        put("rcw%d" % l, inp["rnn_conv_w"][l])
        put("rcb%d" % l, inp["rnn_conv_b"][l])
        put("rba%d" % l, inp["rnn_ba"][l])
        put("rbx%d" % l, inp["rnn_bx"][l])
        put("rlam%d" % l, inp["rnn_lambda"][l])
        put("fcw%d" % l, inp["ffn_conv_w"][l])
        put("fcb%d" % l, inp["ffn_conv_b"][l])
    return V


def _consts():
    C = np.zeros((128, NCST), np.float32)
    C[:, CI:CI + 128] = np.eye(128, dtype=np.float32)
    s = np.arange(128)[:, None]
    c = np.arange(128)[None, :]
    C[:, CMF:CMF + 128] = (s <= c)
    C[:, CMB:CMB + 128] = (s >= c)
    t = np.arange(512)
    C[:, CSF:CSF + 512] = (t % 128 != 0)[None, :]
    C[:, CSB:CSB + 512] = (t % 128 != 127)[None, :]
    return C


def build_nc(debug=False):
    nc = bass.Bass("TRN2", target_bir_lowering=False)
    P = Prog(nc)

    def din(name, shape):
        return nc.dram_tensor(name, list(shape), F32, kind="ExternalInput").ap()
    xs0 = din("xs0", [D, NT])
    vecs_d = din("vecs", [128, NV])
    cst_d = din("cst", [128, NCST])
    ada_w = din("ada_w", [DEPTH, D, 6 * D])
    w_in = din("w_in", [DEPTH, D, D_IN])
    lr_w = din("gla_lr_w", [DEPTH, 2, 16, 512])
    rnn_wa = din("rnn_wa", [DEPTH, 2, 8, 128, 128])
    rnn_wx = din("rnn_wx", [DEPTH, 2, 8, 128, 128])
    w_go = din("w_gla_o", [DEPTH, D, D])
    w_ro = din("w_rnn_o", [DEPTH, D, D])
    w_out = din("w_out", [DEPTH, D, D])
    ffn_up = din("ffn_up", [DEPTH, D, 2 * FH])
    ffn_dn = din("ffn_down", [DEPTH, FH, D])
    outT = nc.dram_tensor("outT", [D, SEQ], F32, kind="ExternalOutput").ap()
    skind = "ExternalOutput" if debug else "Internal"
    xs_s = [T(xs0, "xs0")] + [T(nc.dram_tensor("xs%d" % i, [D, NT], F32, kind=skind).ap(), "xs%d" % i) for i in (1, 2, 3)]
    og_d = T(nc.dram_tensor("og", [D, NT], BF16, kind=skind).ap(), "og")
    rr_d = T(nc.dram_tensor("rr", [D, NT], BF16, kind=skind).ap(), "rr")
    dbg = {}
    if debug:
        for l in range(DEPTH):
            dbg["mod%d" % l] = nc.dram_tensor("dbg_mod%d" % l, [128, 96], F32, kind="ExternalOutput").ap()
            dbg["dv%d" % l] = nc.dram_tensor("dbg_dv%d" % l, [128, 256], F32, kind="ExternalOutput").ap()
            dbg["hx%d" % l] = nc.dram_tensor("dbg_hx%d" % l, [128, 8 * NT], BF16, kind="ExternalOutput").ap()
            dbg["hx2_%d" % l] = nc.dram_tensor("dbg_hx2_%d" % l, [128, 8 * NT], BF16, kind="ExternalOutput").ap()
            dbg["og%d" % l] = nc.dram_tensor("dbg_og%d" % l, [D, NT], BF16, kind="ExternalOutput").ap()
            dbg["rr%d" % l] = nc.dram_tensor("dbg_rr%d" % l, [D, NT], BF16, kind="ExternalOutput").ap()
    outT_t = T(outT, "outT")
    dummy = T(None, "wdram")

    def kc_view(ap2d):
        return ap2d.rearrange("(kc p) n -> p kc n", p=128)

    out_evs = []
    st = ExitStack()
    with st:
        sems = {n: st.enter_context(nc.semaphore(n)) for n in P.semnames}

        _uid = [0]

        def sbuf(stack, name, shape, dt=F32):
            _uid[0] += 1
            return stack.enter_context(nc.sbuf_tensor("sb%d_%s" % (_uid[0], name), list(shape), dt))

        def ACT(out, in_, func, r, w, bias=0.0, scale=1.0):
            P.op("act", lambda e: e.activation(out=out, in_=in_, func=func, bias=bias, scale=scale), r, w)

        def TT(out, a, b, op, r, w):
            P.op("dve", lambda e: e.tensor_tensor(out, a, b, op), r, w)

        def TS(out, a, s1, s2, op0, op1, r, w):
            if s2 is None:
                P.op("dve", lambda e: e.tensor_scalar(out, a, s1, None, op0), r, w)
            else:
                P.op("dve", lambda e: e.tensor_scalar(out, a, s1, s2, op0, op1), r, w)

        def SCAN(out, d0, d1, init, r, w):
            P.op("dve", lambda e: e.tensor_tensor_scan(out, d0, d1, init, ALU.mult, ALU.add), r, w)

        def STT(out, a, s, b, op0, op1, r, w):
            P.op("dve", lambda e: e.scalar_tensor_tensor(out, a, s, b, op0, op1), r, w)

        def PCOPY(out, a, r, w):
            P.op("pool", lambda e: e.tensor_copy(out, a), r, w)

        def PTT(out, a, b, op, r, w):
            P.op("pool", lambda e: e.tensor_tensor(out, a, b, op), r, w)

        def DCOPY(out, a, r, w):
            P.op("dve", lambda e: e.tensor_copy(out, a), r, w)

        def MEMSET(out, v, w):
            P.op("dve", lambda e: e.memset(out, v), (), w)

        def MM(out, lhsT, rhs, start, stop, r, w):
            P.op("pe", lambda e: e.matmul(out, lhsT, rhs, start=start, stop=stop), r, w)

        def TR(out, in_, ident, r, w):
            P.op("pe", lambda e: e.transpose(out, in_, ident), r, w)

        def DMA(q, out, in_, r, w):
            return P.dma(q, lambda e: e.dma_start(out=out, in_=in_), r, w)

        vecs = sbuf(st, "vecs", [128, NV]); vecs_t = T(vecs, "vecs")
        cst = sbuf(st, "cst", [128, NCST]); cst_t = T(cst, "cst")
        cstb = sbuf(st, "cstb", [128, 128], BF16); cstb_t = T(cstb, "cstb")
        ones_b = sbuf(st, "ones_b", [128, 128], BF16); ones_t = T(ones_b, "ones")
        dv = sbuf(st, "dv", [128, 256]); dv_t = T(dv, "dv")
        mod = sbuf(st, "mod", [128, 96]); mod_t = T(mod, "mod")
        scb = sbuf(st, "scb", [128, 8, 2], BF16); scb_t = T(scb, "scb")
        RR = sbuf(st, "RR", [128, NHC * D], BF16)
        hx = RR[:, 0:8 * NT].rearrange("p (kc n) -> p kc n", kc=8)
        wd = RR[:, :].rearrange("p (hc n) -> p hc n", hc=NHC)
        wd_t = T(wd, "wd")
        hx_t = [T(hx[:, :, a:b], "hx%d" % i) for i, (a, b) in enumerate(TILES)]
        banks = [T(st.enter_context(nc.psum_tensor("pb%d" % i, [128, 512], F32)), "pb%d" % i) for i in range(8)]

        DMA("sp", vecs[:], vecs_d, [dummy], [vecs_t])
        DMA("sp", cst[:], cst_d, [dummy], [cst_t])
        DCOPY(cstb[:], cst[:, CI:CI + 128], [cst_t], [cstb_t])
        MEMSET(ones_b[:], 1.0, [ones_t])

        def V(name, i=0, n=1):
            o = VOFF[name] + i
            return vecs[:, o:o + n]

        def tile_of_chunk(n):
            return 0 if n < 2 else 1 + (n - 2) // 4

        DV_S1X, DV_S1C, DV_S2X, DV_S2C, DV_NLRB, DV_L, DV_SILU = 0, 8, 16, 24, 32, 40, 56

        def ada_phase(l, wst, wslots):
            ACT(scb[:, :, 0], V("c", 0, 8), AF.Silu, [vecs_t], [scb_t])
            ACT(scb[:, :, 1], V("cctx", 0, 8), AF.Silu, [vecs_t], [scb_t])
            pb = banks[0]
            aw = kc_view(ada_w[l])
            for grp in range(12):
                ws = wslots[grp % len(wslots)]
                DMA("pool", ws.ap[:, :].rearrange("p (kc n) -> p kc n", kc=8), aw[:, :, grp * 512:(grp + 1) * 512], [dummy], [ws])
                wv = ws.ap[:, :].rearrange("p (kc n) -> p kc n", kc=8)
                for jj in range(4):
                    j = grp * 4 + jj
                    for kc in range(8):
                        MM(pb[:, j * 2:j * 2 + 2], wv[:, kc, jj * 128:(jj + 1) * 128], scb[:, kc, :],
                           kc == 0, kc == 7, [ws, scb_t], [pb])
            m3 = mod[:, :].rearrange("p (j s) -> p j s", s=2)
            p3 = pb[:, 0:96].rearrange("p (j s) -> p j s", s=2)
            for s in range(2):
                TT(m3[:, :, s], p3[:, :, s], V("adab%d" % l, 0, 48), ALU.add, [pb, vecs_t], [mod_t])
            for s, (o1, o2) in enumerate(((DV_S1X, DV_S2X), (DV_S1C, DV_S2C))):
                STT(dv[:, o1:o1 + 8], m3[:, 8:16, s], 1.0, V("n1w%d" % l, 0, 8), ALU.add, ALU.mult, [mod_t, vecs_t], [dv_t])
                STT(dv[:, o2:o2 + 8], m3[:, 32:40, s], 1.0, V("n2w%d" % l, 0, 8), ALU.add, ALU.mult, [mod_t, vecs_t], [dv_t])
            TS(dv[:, DV_NLRB:DV_NLRB + 8], V("lrb%d" % l, 0, 8), -1.0, None, ALU.mult, ALU.bypass, [vecs_t], [dv_t])
            ACT(dv[:, DV_L:DV_L + 16], V("rlam%d" % l, 0, 16), AF.Exp, [vecs_t], [dv_t], scale=-1.0)
            ACT(dv[:, DV_L:DV_L + 16], dv[:, DV_L:DV_L + 16], AF.Ln, [dv_t], [dv_t], bias=1.0)
            TS(dv[:, DV_L:DV_L + 16], dv[:, DV_L:DV_L + 16], -8.0, None, ALU.mult, ALU.bypass, [dv_t], [dv_t])

        def modcol(part, kc, s):
            j = part * 8 + kc
            return mod[:, j * 2 + s:j * 2 + s + 1]

        def norm_tile(xt_ap, xt_T, ti, sc_off, sh_part, out_tile_T, out_ap, tmp, nw_scale_from_dv=True, final=False):
            a, b = TILES[ti]
            n = b - a
            s = 0 if ti > 0 else 1
            sq, sq_t, rs, rs_t, t1l, t1l_t = tmp
            ACT(sq[:, :, :n], xt_ap[:, :, :n], AF.Square, [xt_T], [sq_t])
            pb = banks[7]
            for kc in range(8):
                MM(pb[:, :n], ones_b[:, :], sq[:, kc, :n], kc == 0, kc == 7, [ones_t, sq_t], [pb])
            ACT(rs[:, :n], pb[:, :n], AF.Ln, [pb], [rs_t], bias=EPS, scale=1.0 / D)
            ACT(rs[:, :n], rs[:, :n], AF.Exp, [rs_t], [rs_t], scale=-0.5)
            for kc in range(8):
                t1, t1_t = t1l[kc % 2], t1l_t[kc % 2]
                TT(t1[:, :n], xt_ap[:, kc, :n], rs[:, :n], ALU.mult, [xt_T, rs_t], [t1_t])
                if final:
                    TS(out_ap[:, kc, :n], t1[:, :n], V("fnw", kc), None, ALU.mult, ALU.bypass, [t1_t, vecs_t], [out_tile_T])
                else:
                    so = (sc_off[0] if s == 0 else sc_off[1]) + kc
                    ACT(out_ap[:, kc, :n], t1[:, :n], AF.Identity, [t1_t, dv_t, mod_t], [out_tile_T],
                        bias=modcol(sh_part, kc, s), scale=dv[:, so:so + 1])

        def norm1_phase(l, src, stack):
            xt = [sbuf(stack, "n1x%d" % i, [128, 8, 512]) for i in range(2)]
            xt_T = [T(x, "n1x") for x in xt]
            sq = [sbuf(stack, "n1sq%d" % i, [128, 8, 512], BF16) for i in range(2)]; sq_t = [T(sq[0]), T(sq[1])]
            rs = [sbuf(stack, "n1rs%d" % i, [128, 512]) for i in range(2)]; rs_t = [T(rs[0]), T(rs[1])]
            t1 = [sbuf(stack, "n1t1%d" % i, [128, 512]) for i in range(2)]; t1_t = [T(t1[0]), T(t1[1])]
            srcv = kc_view(src.ap)

            def stat1(ti):
                a, b = TILES[ti]
                n = b - a
                k = ti % 2
                DMA("sp", xt[k][:, :, :n], srcv[:, :, a:b], [src], [xt_T[k]])
                ACT(sq[k][:, :, :n], xt[k][:, :, :n], AF.Square, [xt_T[k]], [sq_t[k]])
                pb = banks[6 + k]
                for kc in range(8):
                    MM(pb[:, :n], ones_b[:, :], sq[k][:, kc, :n], kc == 0, kc == 7, [ones_t, sq_t[k]], [pb])

            def stat2(ti):
                a, b = TILES[ti]
                n = b - a
                k = ti % 2
                pb = banks[6 + k]
                ACT(rs[k][:, :n], pb[:, :n], AF.Ln, [pb], [rs_t[k]], bias=EPS, scale=1.0 / D)
                ACT(rs[k][:, :n], rs[k][:, :n], AF.Exp, [rs_t[k]], [rs_t[k]], scale=-0.5)

            def apply(ti):
                a, b = TILES[ti]
                n = b - a
                k = ti % 2
                s_ = 0 if ti > 0 else 1
                for kc in range(8):
                    tt, tt_t = t1[kc % 2], t1_t[kc % 2]
                    TT(tt[:, :n], xt[k][:, kc, :n], rs[k][:, :n], ALU.mult, [xt_T[k], rs_t[k]], [tt_t])
                    so = (DV_S1X if s_ == 0 else DV_S1C) + kc
                    ACT(hx[:, kc, a:b], tt[:, :n], AF.Identity, [tt_t, dv_t, mod_t], [hx_t[ti]],
                        bias=modcol(0, kc, s_), scale=dv[:, so:so + 1])

            stat1(0)
            stat2(0)
            for ti in range(len(TILES)):
                if ti + 1 < len(TILES):
                    stat1(ti + 1)
                apply(ti)
                if ti + 1 < len(TILES):
                    stat2(ti + 1)

        def gla_phase(l, last, stack):
            win = kc_view(w_in[l])
            ws = []
            for i in range(2):
                d = {}
                for nm, cols in (("q", 128), ("k", 128), ("v", 256), ("g", 256)):
                    tns = sbuf(stack, "gw%s%d" % (nm, i), [128, 8, cols], BF16)
                    d[nm] = T(tns, "gw" + nm)
                ws.append(d)

            def load_head(h):
                d = ws[h % 2]
                DMA("pool", d["q"].ap[:], win[:, :, C_Q + h * 128:C_Q + (h + 1) * 128], [dummy], [d["q"]])
                DMA("pool", d["k"].ap[:], win[:, :, C_K + h * 128:C_K + (h + 1) * 128], [dummy], [d["k"]])
                DMA("pool", d["v"].ap[:], win[:, :, C_V + h * 256:C_V + (h + 1) * 256], [dummy], [d["v"]])
                DMA("pool", d["g"].ap[:], win[:, :, C_G + h * 256:C_G + (h + 1) * 256], [dummy], [d["g"]])
            wlrc = sbuf(stack, "wlrc", [128, 8, 32], BF16); wlrc_t = T(wlrc)
            wlr = sbuf(stack, "wlr", [16, 2, 512], BF16); wlr_t = T(wlr)
            DMA("pool", wlrc[:], win[:, :, C_LR:C_LR + 32], [dummy], [wlrc_t])
            DMA("pool", wlr[:], lr_w[l].rearrange("d r k -> r d k"), [dummy], [wlr_t])
            load_head(0)
            lrT = sbuf(stack, "lrT", [16, 2, NT], BF16); lrT_t = [T(lrT[:, :, a:b]) for (a, b) in TILES]
            for ti, (a, b) in enumerate(TILES):
                n = b - a
                for d in range(2):
                    pb = banks[d]
                    for kc in range(8):
                        MM(pb[0:16, :n], wlrc[:, kc, d * 16:(d + 1) * 16], hx[:, kc, a:b], kc == 0, kc == 7, [wlrc_t, hx_t[ti]], [pb])
                    ACT(lrT[:, d, a:b], pb[0:16, :n], AF.Identity, [pb], [lrT_t[ti]])

            qd = [sbuf(stack, "qd%d" % d, [128, NT], BF16) for d in range(2)]
            ki = [sbuf(stack, "ki%d" % d, [128, NT], BF16) for d in range(2)]
            qd_t = [[T(qd[d][:, a:b]) for (a, b) in TILES] for d in range(2)]
            ki_t = [[T(ki[d][:, a:b]) for (a, b) in TILES] for d in range(2)]
            kt = [sbuf(stack, "kt%d" % d, [128, NCH, 128], BF16) for d in range(2)]
            kt_t = [[T(kt[d][:, 0:1, :]) for _ in TILES] for d in range(2)]
            vt = sbuf(stack, "vt", [128, NCH, 256], BF16); vt_t = [T(vt[:, 0:1, :]) for _ in TILES]
            sb_ = [sbuf(stack, "sb%d" % d, [128, NCH, 256], BF16) for d in range(2)]
            sb_t = [[T(sb_[d][:, n, :]) for n in range(NCH)] for d in range(2)]
            el = sbuf(stack, "el", [128, 2, NCH]); el_t = [[T(el[:, d, 0:1]) for _ in TILES] for d in range(2)]
            S = [sbuf(stack, "S%d" % d, [128, 256]) for d in range(2)]; S_t = [T(S[0]), T(S[1])]
            Stmp = [sbuf(stack, "Stmp%d" % d, [128, 256]) for d in range(2)]; Stmp_t = [T(Stmp[0]), T(Stmp[1])]
            tm = {}
            for d in range(2):
                for p_ in range(2):
                    for nm in ("A", "B", "C"):
                        tns = sbuf(stack, "g%s%d%d" % (nm, d, p_), [128, 512])
                        tm[(nm, d, p_)] = (tns, T(tns))
            scm_all = sbuf(stack, "scm_all", [128, 2, NCH, 128], BF16)
            scm_t = [[T(scm_all[:, d, n, :]) for n in range(NCH)] for d in range(2)]
            sq = sbuf(stack, "gsq", [128, 2, 512], BF16); sq_t = T(sq)
            rs = sbuf(stack, "grs", [128, 512]); rs_t = T(rs)
            sg = sbuf(stack, "gsg", [128, 2, 512]); sg_t = T(sg)
            t1 = sbuf(stack, "gt1", [128, 512]); t1_t = T(t1)
            ogt = [sbuf(stack, "ogt%d" % i, [128, 2, 512], BF16) for i in range(2)]; ogt_t = [T(ogt[0]), T(ogt[1])]
            ogv = og_d.ap.rearrange("(c p) n -> p c n", p=128)
            og_i = 0
            border = [1, 0] + list(range(NCH - 1, 1, -1))
            forder = list(range(NCH))

            kdt = {}
            for d in range(2):
                for p_ in range(2):
                    tns = sbuf(stack, "gkd%d%d" % (d, p_), [128, 512], BF16)
                    kdt[(d, p_)] = (tns, T(tns))

            for h in range(4):
                W = ws[h % 2]
                if h + 1 < 4:
                    load_head(h + 1)
                P.mark("gla_prep")

                def stage1(ti):
                    a, b = TILES[ti]
                    n = b - a
                    nch = n // 128
                    c0 = a // 128
                    p_ = ti % 2
                    pq, pk = banks[0 + p_], banks[2 + p_]
                    for kc in range(8):
                        MM(pq[:, :n], W["q"].ap[:, kc, :], hx[:, kc, a:b], kc == 0, kc == 7, [W["q"], hx_t[ti]], [pq])
                    for kc in range(8):
                        MM(pk[:, :n], W["k"].ap[:, kc, :], hx[:, kc, a:b], kc == 0, kc == 7, [W["k"], hx_t[ti]], [pk])
                    for j in range(nch):
                        pv = banks[6 + (j // 2) % 2]
                        hs_ = (j % 2) * 256
                        for kc in range(8):
                            MM(pv[:, hs_:hs_ + 256], hx[:, kc, a + j * 128:a + (j + 1) * 128], W["v"].ap[:, kc, :], kc == 0, kc == 7, [hx_t[ti], W["v"]], [pv])
                        if j % 2 == 1 or j == nch - 1:
                            j0 = j - (j % 2)
                            w_ = (j - j0 + 1) * 256
                            DCOPY(vt[:, c0 + j0:c0 + j + 1, :], pv[:, 0:w_].rearrange("p (c v) -> p c v", v=256), [pv], [vt_t[ti]])
                    for d in range(2):
                        pl = banks[4]
                        MM(pl[:, :n], wlr[:, d, h * 128:(h + 1) * 128], lrT[:, d, a:b], True, True, [wlr_t, lrT_t[ti]], [pl])
                        A_, A_t = tm[("A", d, p_)]
                        nb = dv[:, DV_NLRB + d * 4 + h:DV_NLRB + d * 4 + h + 1]
                        ACT(A_[:, :n], pl[:, :n], AF.Exp, [pl, dv_t], [A_t], bias=nb, scale=-1.0)

                def stage2(ti):
                    a, b = TILES[ti]
                    n = b - a
                    nch = n // 128
                    c0 = a // 128
                    p_ = ti % 2
                    pq, pk = banks[0 + p_], banks[2 + p_]
                    X = [(tm[("A", d, p_)], tm[("B", d, p_)], tm[("C", d, p_)]) for d in range(2)]
                    for d in range(2):
                        (A_, A_t), (B_, B_t), (C_, C_t) = X[d]
                        ACT(B_[:, :n], A_[:, :n], AF.Ln, [A_t], [B_t], bias=1.0)
                    for d in range(2):
                        (A_, A_t), (B_, B_t), (C_, C_t) = X[d]
                        if d == 0:
                            SCAN(C_[:, :n], cst[:, CSF:CSF + n], B_[:, :n], 0.0, [cst_t, B_t], [C_t])
                        else:
                            SCAN(C_[:, :n][:, ::-1], cst[:, CSB + 512 - n:CSB + 512][:, ::-1], B_[:, :n][:, ::-1], 0.0, [cst_t, B_t], [C_t])
                    for d in range(2):
                        (A_, A_t), (B_, B_t), (C_, C_t) = X[d]
                        if d == 0:
                            ACT(el[:, 0, c0:c0 + nch], C_[:, 127:n:128], AF.Exp, [C_t], [el_t[0][ti]], scale=-1.0 / 16)
                        else:
                            ACT(el[:, 1, c0:c0 + nch], C_[:, 0:n:128], AF.Exp, [C_t], [el_t[1][ti]], scale=-1.0 / 16)
                        ACT(A_[:, :n], C_[:, :n], AF.Exp, [C_t], [A_t], scale=-1.0 / 16)
                        ACT(B_[:, :n], C_[:, :n], AF.Exp, [C_t], [B_t], scale=1.0 / 16)
                    for d in range(2):
                        (A_, A_t), (B_, B_t), (C_, C_t) = X[d]
                        kd, kd_t = kdt[(d, p_)]
                        STT(qd[d][:, a:b], pq[:, :n], 128.0 ** -0.5, A_[:, :n], ALU.mult, ALU.mult, [pq, A_t], [qd_t[d][ti]])
                        TT(ki[d][:, a:b], pk[:, :n], B_[:, :n], ALU.mult, [pk, B_t], [ki_t[d][ti]])
                        TT(kd[:, :n].rearrange("p (c k) -> p c k", k=128), ki[d][:, a:b].rearrange("p (c k) -> p c k", k=128),
                           el[:, d, c0:c0 + nch].to_broadcast([128, nch, 128]) if False else el[:, d, c0:c0 + nch, None].to_broadcast([128, nch, 128]),
                           ALU.mult, [ki_t[d][ti], el_t[d][ti]], [kd_t])

                def stage3(ti):
                    a, b = TILES[ti]
                    n = b - a
                    nch = n // 128
                    c0 = a // 128
                    p_ = ti % 2
                    ptr = banks[5]
                    ptb = ptr.ap[:, :].bitcast(BF16)
                    for d in range(2):
                        kd, kd_t = kdt[(d, p_)]
                        for j in range(nch):
                            TR(ptb[:, d * 512 + j * 128:d * 512 + (j + 1) * 128], kd[:, j * 128:(j + 1) * 128], cstb[:, :], [kd_t, cstb_t], [ptr])
                    for d in range(2):
                        ACT(kt[d][:, c0:c0 + nch, :], ptb[:, d * 512:d * 512 + nch * 128].rearrange("p (c k) -> p c k", k=128), AF.Copy,
                            [ptr], [kt_t[d][ti]])

                NTI = len(TILES)
                stage1(0)
                for ti in range(NTI):
                    stage2(ti)
                    if ti + 1 < NTI:
                        stage1(ti + 1)
                    stage3(ti)

                P.mark("gla_state")
                chunks = [n_ for n_ in range(NCH) if not (last and n_ < 2)]

                def scores(n_):
                    P.cur = "sc%d_%d" % (h, n_)
                    ti = tile_of_chunk(n_)
                    cs = slice(n_ * 128, (n_ + 1) * 128)
                    p_ = n_ % 2
                    psc = banks[4 + p_]
                    for d in range(2):
                        MM(psc[:, d * 128:(d + 1) * 128], ki[d][:, cs], qd[d][:, cs], True, True, [ki_t[d][ti], qd_t[d][ti]], [psc])
                    for d in range(2):
                        TT(scm_all[:, d, n_, :], psc[:, d * 128:(d + 1) * 128], cst[:, (CMF, CMB)[d]:(CMF, CMB)[d] + 128], ALU.mult, [psc, cst_t], [scm_t[d][n_]])

                for d in range(2):
                    MEMSET(S[d][:], 0.0, [S_t[d]])
                for idx in range(NCH):
                    if idx < len(chunks):
                        scores(chunks[idx])
                    P.cur = "state"
                    for d, order in ((1, border), (0, forder)):
                        n_ = order[idx]
                        ti = tile_of_chunk(n_)
                        ACT(sb_[d][:, n_, :], S[d][:], AF.Copy, [S_t[d]], [sb_t[d][n_]])
                        if idx == NCH - 1:
                            continue
                        pp = banks[2 * d + (idx % 2)]
                        MM(pp[:, 0:256], kt[d][:, n_, :], vt[:, n_, :], True, True, [kt_t[d][ti], vt_t[ti]], [pp])
                        STT(S[d][:], S[d][:], el[:, d, n_:n_ + 1], pp[:, 0:256], ALU.mult, ALU.add, [S_t[d], el_t[d][ti], pp], [S_t[d]])

                P.mark("gla_out")
                def outs(n_):
                    P.cur = "out%d_%d" % (h, n_)
                    ti = tile_of_chunk(n_)
                    a, b = TILES[ti]
                    j = n_ - a // 128
                    ob = [banks[0 + 2 * (ti % 2)], banks[1 + 2 * (ti % 2)]]
                    cs = slice(n_ * 128, (n_ + 1) * 128)
                    p_ = n_ % 2
                    for vh in range(2):
                        o = ob[vh][:, j * 128:(j + 1) * 128]
                        vs = slice(vh * 128, (vh + 1) * 128)
                        MM(o, sb_[0][:, n_, vs], qd[0][:, cs], True, False, [sb_t[0][n_], qd_t[0][ti]], [ob[vh]])
                        MM(o, sb_[1][:, n_, vs], qd[1][:, cs], False, False, [sb_t[1][n_], qd_t[1][ti]], [ob[vh]])
                        MM(o, vt[:, n_, vs], scm_all[:, 0, n_, :], False, False, [vt_t[ti], scm_t[0][n_]], [ob[vh]])
                        MM(o, vt[:, n_, vs], scm_all[:, 1, n_, :], False, True, [vt_t[ti], scm_t[1][n_]], [ob[vh]])

                def epilogue(ti, k2):
                    P.cur = "epi%d_%d" % (h, ti)
                    a, b = TILES[ti]
                    nn = b - a
                    ob = [banks[0 + 2 * (ti % 2)], banks[1 + 2 * (ti % 2)]]
                    pg = banks[7]
                    pss = banks[6]
                    for vh in range(2):
                        ACT(sq[:, vh, :nn], ob[vh][:, :nn], AF.Square, [ob[vh]], [sq_t])
                    for kc in range(8):
                        MM(pg[:, :nn], W["g"].ap[:, kc, 0:128], hx[:, kc, a:b], kc == 0, kc == 7, [W["g"], hx_t[ti]], [pg])
                    yield
                    P.cur = "epi%d_%d" % (h, ti)
                    for vh in range(2):
                        MM(pss[:, :nn], ones_b[:, :], sq[:, vh, :nn], vh == 0, vh == 1, [ones_t, sq_t], [pss])
                    ACT(rs[:, :nn], pss[:, :nn], AF.Ln, [pss], [rs_t], bias=EPS, scale=1.0 / 256)
                    ACT(rs[:, :nn], rs[:, :nn], AF.Exp, [rs_t], [rs_t], scale=-0.5)
                    ACT(sg[:, 0, :nn], pg[:, :nn], AF.Silu, [pg], [sg_t])
                    TT(t1[:, :nn], ob[0][:, :nn], rs[:, :nn], ALU.mult, [ob[0], rs_t], [t1_t])
                    STT(ogt[k2][:, 0, :nn], t1[:, :nn], V("gnw%d" % l, h * 2 + 0), sg[:, 0, :nn], ALU.mult, ALU.mult,
                        [t1_t, vecs_t, sg_t], [ogt_t[k2]])
                    yield
                    P.cur = "epi%d_%d" % (h, ti)
                    for kc in range(8):
                        MM(pg[:, :nn], W["g"].ap[:, kc, 128:256], hx[:, kc, a:b], kc == 0, kc == 7, [W["g"], hx_t[ti]], [pg])
                    ACT(sg[:, 1, :nn], pg[:, :nn], AF.Silu, [pg], [sg_t])
                    TT(t1[:, :nn], ob[1][:, :nn], rs[:, :nn], ALU.mult, [ob[1], rs_t], [t1_t])
                    STT(ogt[k2][:, 1, :nn], t1[:, :nn], V("gnw%d" % l, h * 2 + 1), sg[:, 1, :nn], ALU.mult, ALU.mult,
                        [t1_t, vecs_t, sg_t], [ogt_t[k2]])
                    DMA("sp", ogv[:, h * 2:h * 2 + 2, a:b], ogt[k2][:, :, :nn], [ogt_t[k2]], [og_d])

                pending = []

                def advance():
                    if pending:
                        try:
                            next(pending[0])
                        except StopIteration:
                            pending.pop(0)
                            advance()

                for ci, n_ in enumerate(chunks):
                    outs(n_)
                    advance()
                    ti = tile_of_chunk(n_)
                    if (n_ + 1) * 128 == TILES[ti][1]:
                        pending.append(epilogue(ti, og_i % 2))
                        og_i += 1
                while pending:
                    advance()

        def rnn_phase(l, last, stack):
            win = kc_view(w_in[l])
            wsl = []
            for i in range(3):
                d = {}
                d["xr"] = T(sbuf(stack, "rwx%d" % i, [128, 8, 128], BF16))
                d["yr"] = T(sbuf(stack, "rwy%d" % i, [128, 8, 128], BF16))
                d["g"] = T(sbuf(stack, "rwg%d" % i, [128, 4, 128], BF16))
                wsl.append(d)

            def load_blk(g):
                d = wsl[g % 3]
                DMA("pool", d["xr"].ap[:], win[:, :, C_XR + g * 128:C_XR + (g + 1) * 128], [dummy], [d["xr"]])
                DMA("pool", d["yr"].ap[:], win[:, :, C_YR + g * 128:C_YR + (g + 1) * 128], [dummy], [d["yr"]])
                DMA("pool", d["g"].ap[:, 0:2, :], rnn_wa[l, :, g].rearrange("d i j -> i d j"), [dummy], [d["g"]])
                DMA("pool", d["g"].ap[:, 2:4, :], rnn_wx[l, :, g].rearrange("d i j -> i d j"), [dummy], [d["g"]])
            load_blk(0)
            load_blk(1)
            XP = 2312
            xrp = sbuf(stack, "xrp", [128, XP], BF16); xrp_t = T(xrp)
            MEMSET(xrp[:], 0.0, [xrp_t])
            xc2 = [sbuf(stack, "xc%d" % i, [128, NT]) for i in range(2)]; xc2_t = [T(xc2[0]), T(xc2[1])]
            xcb2 = [sbuf(stack, "xcb%d" % i, [128, NT], BF16) for i in range(2)]; xcb2_t = [T(xcb2[0]), T(xcb2[1])]
            _gy = sbuf(stack, "rgy", [128, NT]); _gyt = T(_gy)
            gy = [_gy, _gy, _gy]; gy_t = [_gyt, _gyt, _gyt]
            dgr2 = [sbuf(stack, "dgr%d" % i, [128, 4, 128], BF16) for i in range(2)]; dgr2_t = [T(dgr2[0]), T(dgr2[1])]
            rb = [[sbuf(stack, "rb%d%d" % (q, d), [128, NT]) for d in range(2)] for q in range(2)]
            ib = [[sbuf(stack, "ib%d%d" % (q, d), [128, NT]) for d in range(2)] for q in range(2)]
            rb_t = [[T(rb[q][d]) for d in range(2)] for q in range(2)]
            ib_t = [[T(ib[q][d]) for d in range(2)] for q in range(2)]
            hv = [sbuf(stack, "rh%d" % d, [128, NT]) for d in range(2)]; hv_t = [T(hv[0]), T(hv[1])]
            hvb = hv[1][:, :].bitcast(BF16)
            rrv = rr_d.ap.rearrange("(c p) n -> p c n", p=128)
            o0 = NCTX if last else 0

            def pos(t):
                return 1 + t if t < NCTX else 260 + (t - NCTX)

            def front_x_tiles(g):
                W = wsl[g % 3]
                steps = []

                def mk(ti, a, b):
                    def step():
                        P.cur = "frontx%d" % g
                        if ti == 0:
                            for tap in range(4):
                                TS(dgr2[g % 2][:, tap, :], cst[:, CI:CI + 128], V("rcw%d" % l, tap * 8 + g), None, ALU.mult, ALU.bypass,
                                   [cst_t, vecs_t], [dgr2_t[g % 2]])
                        n = b - a
                        pb = banks[ti % 2]
                        for kc in range(8):
                            MM(pb[:, :n], W["xr"].ap[:, kc, :], hx[:, kc, a:b], kc == 0, kc == 7, [W["xr"], hx_t[ti]], [pb])
                        ACT(xrp[:, pos(a):pos(a) + n], pb[:, :n], AF.Copy, [pb], [xrp_t])
                    return step
                for ti, (a, b) in enumerate(TILES):
                    steps.append(mk(ti, a, b))
                return steps

            def front_x(g):
                for st_ in front_x_tiles(g):
                    st_()

            def front_c(g):
                P.cur = "frontc%d" % g
                W = wsl[g % 3]
                q_ = g % 2
                xc, xc_t, xcb, xcb_t = xc2[q_], xc2_t[q_], xcb2[q_], xcb2_t[q_]
                for ti, (a, b) in enumerate(TILES):
                    n = b - a
                    pb = banks[2 + (ti % 2)]
                    for tap in range(4):
                        o_ = pos(a) + tap - 1
                        MM(pb[:, :n], dgr2[g % 2][:, tap, :], xrp[:, o_:o_ + n], tap == 0, tap == 3, [dgr2_t[g % 2], xrp_t], [pb])
                    ACT(xc[:, a:b], pb[:, :n], AF.Identity, [pb, vecs_t], [xc_t], bias=V("rcb%d" % l, g))
                    DCOPY(xcb[:, a:b], xc[:, a:b], [xc_t], [xcb_t])

            def yr_tiles(g):
                W = wsl[g % 3]
                steps = []

                def mk(ti, a, b):
                    def step():
                        P.cur = "yr%d" % g
                        n = b - a
                        pb = banks[(4, 5, 6, 7, 0)[ti]]
                        for kc in range(8):
                            MM(pb[:, :n], W["yr"].ap[:, kc, :], hx[:, kc, a:b], kc == 0, kc == 7, [W["yr"], hx_t[ti]], [pb])
                        ACT(gy[g % 3][:, a:b], pb[:, :n], AF.Gelu_apprx_tanh, [pb], [gy_t[g % 3]])
                    return step
                for ti, (a, b) in enumerate(TILES):
                    if last and ti == 0:
                        continue
                    steps.append(mk(ti, a, b))
                return steps

            def gate_steps(g):
                W = wsl[g % 3]
                q_ = g % 2
                xc, xc_t, xcb, xcb_t = xc2[q_], xc2_t[q_], xcb2[q_], xcb2_t[q_]
                steps = []
                bi = [0]

                def mk(d, ti, a, b):
                    def step():
                        P.cur = "gates%d" % g
                        n = b - a
                        pr, pi = banks[4 + (bi[0] % 4)], banks[4 + ((bi[0] + 1) % 4)]
                        bi[0] += 2
                        MM(pr[:, :n], W["g"].ap[:, d, :], xcb[:, a:b], True, True, [W["g"], xcb_t], [pr])
                        MM(pi[:, :n], W["g"].ap[:, 2 + d, :], xcb[:, a:b], True, True, [W["g"], xcb_t], [pi])
                        ACT(rb[q_][d][:, a:b], pr[:, :n], AF.Sigmoid, [pr, vecs_t], [rb_t[q_][d]], bias=V("rba%d" % l, d * 8 + g))
                        ACT(ib[q_][d][:, a:b], pi[:, :n], AF.Sigmoid, [pi, vecs_t], [ib_t[q_][d]], bias=V("rbx%d" % l, d * 8 + g))
                        if ti == len(TILES) - 1:
                            TT(ib[q_][d][:, :], ib[q_][d][:, :], xc[:, :], ALU.mult, [ib_t[q_][d], xc_t], [ib_t[q_][d]])
                    return step
                for d in range(2):
                    for ti, (a, b) in enumerate(TILES):
                        steps.append(mk(d, ti, a, b))
                return steps

            def mid_steps(g):
                q_ = g % 2
                L0 = dv[:, DV_L + 0 * 8 + g:DV_L + 0 * 8 + g + 1]
                L1 = dv[:, DV_L + 1 * 8 + g:DV_L + 1 * 8 + g + 1]

                def s0():
                    P.cur = "mid%d" % g
                    ACT(rb[q_][0][:, :], rb[q_][0][:, :], AF.Exp, [rb_t[q_][0], dv_t], [rb_t[q_][0]], scale=L0)

                def s1():
                    P.cur = "mid%d" % g
                    ACT(rb[q_][1][:, :], rb[q_][1][:, :], AF.Exp, [rb_t[q_][1], dv_t], [rb_t[q_][1]], scale=L1)
                    for d in range(2):
                        TT(hv[d][:, :], rb[q_][d][:, :], rb[q_][d][:, :], ALU.mult, [rb_t[q_][d]], [hv_t[d]])

                def s2():
                    P.cur = "mid%d" % g
                    ACT(hv[0][:, :], hv[0][:, :], AF.Ln, [hv_t[0]], [hv_t[0]], bias=1.0, scale=-1.0)

                def s3():
                    P.cur = "mid%d" % g
                    ACT(hv[0][:, :], hv[0][:, :], AF.Exp, [hv_t[0]], [hv_t[0]], scale=0.5)

                def s4():
                    P.cur = "mid%d" % g
                    ACT(hv[1][:, :], hv[1][:, :], AF.Ln, [hv_t[1]], [hv_t[1]], bias=1.0, scale=-1.0)
                    TT(ib[q_][0][:, :], ib[q_][0][:, :], hv[0][:, :], ALU.mult, [ib_t[q_][0], hv_t[0]], [ib_t[q_][0]])

                def s5():
                    P.cur = "mid%d" % g
                    ACT(hv[1][:, :], hv[1][:, :], AF.Exp, [hv_t[1]], [hv_t[1]], scale=0.5)
                    TT(ib[q_][1][:, :], ib[q_][1][:, :], hv[1][:, :], ALU.mult, [ib_t[q_][1], hv_t[1]], [ib_t[q_][1]])
                return [s0, s1, s2, s3, s4, s5]

            def mid_a(g):
                P.cur = "mid_a%d" % g
                q_ = g % 2
                for d in range(2):
                    Lc = dv[:, DV_L + d * 8 + g:DV_L + d * 8 + g + 1]
                    ACT(rb[q_][d][:, :], rb[q_][d][:, :], AF.Exp, [rb_t[q_][d], dv_t], [rb_t[q_][d]], scale=Lc)
                for d in range(2):
                    TT(hv[d][:, :], rb[q_][d][:, :], rb[q_][d][:, :], ALU.mult, [rb_t[q_][d]], [hv_t[d]])

            def mid_b(g):
                P.cur = "mid_b%d" % g
                q_ = g % 2
                for d in range(2):
                    ACT(hv[d][:, :], hv[d][:, :], AF.Ln, [hv_t[d]], [hv_t[d]], bias=1.0, scale=-1.0)
                    ACT(hv[d][:, :], hv[d][:, :], AF.Exp, [hv_t[d]], [hv_t[d]], scale=0.5)
                for d in range(2):
                    TT(ib[q_][d][:, :], ib[q_][d][:, :], hv[d][:, :], ALU.mult, [ib_t[q_][d], hv_t[d]], [ib_t[q_][d]])

            def tail(g):
                P.cur = "tail%d" % g
                q_ = g % 2
                A0, A1, U0, U1 = rb[q_][0], rb[q_][1], ib[q_][0], ib[q_][1]
                SCAN(hv[0][:, :], A0[:, :], U0[:, :], 0.0, [rb_t[q_][0], ib_t[q_][0]], [hv_t[0]])
                SCAN(hv[1][:, 0:NCTX][:, ::-1], A1[:, 0:NCTX][:, ::-1], U1[:, 0:NCTX][:, ::-1], 0.0, [rb_t[q_][1], ib_t[q_][1]], [hv_t[1]])
                SCAN(hv[1][:, NCTX:NT][:, ::-1], A1[:, NCTX:NT][:, ::-1], U1[:, NCTX:NT][:, ::-1], hv[1][:, 0:1],
                     [rb_t[q_][1], ib_t[q_][1], hv_t[1]], [hv_t[1]])
                TT(hv[0][:, o0:NT], hv[0][:, o0:NT], hv[1][:, o0:NT], ALU.add, [hv_t[0], hv_t[1]], [hv_t[0]])
                TT(hvb[:, o0:NT], hv[0][:, o0:NT], gy[g % 3][:, o0:NT], ALU.mult, [hv_t[0], gy_t[g % 3]], [hv_t[1]])
                DMA("sp", rrv[:, g, o0:NT], hvb[:, o0:NT], [hv_t[1]], [rr_d])

            front_x(0)
            front_c(0)
            front_x(1)
            front_c(1)
            for g in range(8):
                fx = []
                if g + 2 < 8:
                    load_blk(g + 2)
                    fx = front_x_tiles(g + 2)
                for k_, st_ in enumerate(gate_steps(g)):
                    st_()
                    if k_ % 2 == 1 and fx:
                        fx.pop(0)()
                while fx:
                    fx.pop(0)()
                if g + 2 < 8:
                    front_c(g + 2)
                for st_ in mid_steps(g):
                    st_()
                for st_ in yr_tiles(g):
                    st_()
                tail(g)

        def merge_phase(l, last, src, dst, stack):
            win = kc_view(w_in[l])
            names = ["go", "ga", "ro", "gb", "out"]
            srcs = [kc_view(w_go[l]), win[:, :, C_GA:C_GA + D], kc_view(w_ro[l]), win[:, :, C_GB:C_GB + D], kc_view(w_out[l])]
            Wm = {}
            Wh = {}
            for nm, sv in zip(names, srcs):
                tns = sbuf(stack, "mw" + nm, [128, 8, D], BF16)
                Wm[nm] = T(tns)
                Wh[nm] = [T(tns[:, :, 0:512]), T(tns[:, :, 512:1024])]
            order = [(nm, sv, half) for half in range(2) for nm, sv in list(zip(names, srcs))[:4]]
            order += [(names[4], srcs[4], half) for half in range(2)]
            for nm, sv, half in order:
                DMA("pool", Wm[nm].ap[:, :, half * 512:(half + 1) * 512], sv[:, :, half * 512:(half + 1) * 512], [dummy], [Wh[nm][half]])
            ogs2 = [sbuf(stack, "m_og%d" % i, [128, 8, 512], BF16) for i in range(2)]; ogs2_t = [T(ogs2[0]), T(ogs2[1])]
            rrs2 = [sbuf(stack, "m_rr%d" % i, [128, 8, 512], BF16) for i in range(2)]; rrs2_t = [T(rrs2[0]), T(rrs2[1])]
            xt = sbuf(stack, "m_x", [128, 8, 512]); xt_t = T(xt)
            xn, xn_t = xt, xt_t
            mg = sbuf(stack, "m_mg", [128, 8, 512], BF16); mg_t = T(mg)
            sa = sbuf(stack, "m_sa", [128, 512]); sa_t = T(sa)
            sb_ = sbuf(stack, "m_sb", [128, 512]); sb_t = T(sb_)
            m1 = sbuf(stack, "m_m1", [128, 512]); m1_t = T(m1)
            m2 = sbuf(stack, "m_m2", [128, 512]); m2_t = T(m2)
            rs = sbuf(stack, "m_rs", [128, 512])
            t1 = [sbuf(stack, "m_t1%d" % i, [128, 512]) for i in range(2)]
            tmp = (mg, mg_t, rs, T(rs), t1, [T(t1[0]), T(t1[1])])
            ogv = og_d.ap.rearrange("(c p) n -> p c n", p=128)
            rrv = rr_d.ap.rearrange("(c p) n -> p c n", p=128)
            srcv = kc_view(src.ap)
            dstv = kc_view(dst.ap)
            m_tiles = [ti for ti in range(len(TILES)) if not (last and ti == 0)]

            def m_loads(ix):
                ti_ = m_tiles[ix]
                a_, b_ = TILES[ti_]
                DMA("sp", ogs2[ix % 2][:, :, :b_ - a_], ogv[:, :, a_:b_], [og_d], [ogs2_t[ix % 2]])
                DMA("sp", rrs2[ix % 2][:, :, :b_ - a_], rrv[:, :, a_:b_], [rr_d], [rrs2_t[ix % 2]])

            def n2_square(tp):
                ap_, bp_ = TILES[tp]
                ACT(mg[:, :, :bp_ - ap_], xn[:, :, :bp_ - ap_], AF.Square, [xn_t], [mg_t])

            def n2_stat(tp):
                ap_, bp_ = TILES[tp]
                np_ = bp_ - ap_
                pb = banks[7]
                for kc in range(8):
                    MM(pb[:, :np_], ones_b[:, :], mg[:, kc, :np_], kc == 0, kc == 7, [ones_t, mg_t], [pb])

            def n2_rest(tp):
                ap_, bp_ = TILES[tp]
                np_ = bp_ - ap_
                sp_ = 0 if tp > 0 else 1
                rs_t_, t1l, t1l_t = tmp[3], tmp[4], tmp[5]
                pb = banks[7]
                ACT(rs[:, :np_], pb[:, :np_], AF.Ln, [pb], [rs_t_], bias=EPS, scale=1.0 / D)
                ACT(rs[:, :np_], rs[:, :np_], AF.Exp, [rs_t_], [rs_t_], scale=-0.5)
                for kc in range(8):
                    TT(t1l[kc % 2][:, :np_], xn[:, kc, :np_], rs[:, :np_], ALU.mult, [xn_t, rs_t_], [t1l_t[kc % 2]])
                    so = (DV_S2X if sp_ == 0 else DV_S2C) + kc
                    ACT(hx[:, kc, ap_:bp_], t1l[kc % 2][:, :np_], AF.Identity, [t1l_t[kc % 2], dv_t, mod_t], [hx_t[tp]],
                        bias=modcol(3, kc, sp_), scale=dv[:, so:so + 1])

            m_loads(0)
            for ix, ti in enumerate(m_tiles):
                a, b = TILES[ti]
                n = b - a
                s = 0 if ti > 0 else 1
                ogs, ogs_t = ogs2[ix % 2], ogs2_t[ix % 2]
                rrs, rrs_t = rrs2[ix % 2], rrs2_t[ix % 2]
                if ix + 1 < len(m_tiles):
                    m_loads(ix + 1)
                if ix > 0:
                    n2_square(m_tiles[ix - 1])
                else:
                    DMA("sp", xt[:, :, :n], srcv[:, :, a:b], [src], [xt_t])
                for j in range(8):
                    if j == 1 and ix > 0:
                        n2_rest(m_tiles[ix - 1])
                        DMA("sp", xt[:, :, :n], srcv[:, :, a:b], [src], [xt_t])
                    k4 = 4 * (j % 2)
                    pA, pGA, pB, pGB = banks[k4], banks[k4 + 1], banks[k4 + 2], banks[k4 + 3]
                    js = slice(j * 128, (j + 1) * 128)
                    for kc in range(8):
                        MM(pA[:, :n], Wm["go"].ap[:, kc, js], ogs[:, kc, :n], kc == 0, kc == 7, [Wh["go"][j // 4], ogs_t], [pA])
                    for kc in range(8):
                        MM(pGA[:, :n], Wm["ga"].ap[:, kc, js], hx[:, kc, a:b], kc == 0, kc == 7, [Wh["ga"][j // 4], hx_t[ti]], [pGA])
                    for kc in range(8):
                        MM(pB[:, :n], Wm["ro"].ap[:, kc, js], rrs[:, kc, :n], kc == 0, kc == 7, [Wh["ro"][j // 4], rrs_t], [pB])
                    for kc in range(8):
                        MM(pGB[:, :n], Wm["gb"].ap[:, kc, js], hx[:, kc, a:b], kc == 0, kc == 7, [Wh["gb"][j // 4], hx_t[ti]], [pGB])
                    if j == 0 and ix > 0:
                        n2_stat(m_tiles[ix - 1])
                    ACT(sa[:, :n], pGA[:, :n], AF.Sigmoid, [pGA], [sa_t])
                    ACT(sb_[:, :n], pGB[:, :n], AF.Sigmoid, [pGB], [sb_t])
                    TT(m1[:, :n], pA[:, :n], sa[:, :n], ALU.mult, [pA, sa_t], [m1_t])
                    TT(m2[:, :n], pB[:, :n], sb_[:, :n], ALU.mult, [pB, sb_t], [m2_t])
                    TT(mg[:, j, :n], m1[:, :n], m2[:, :n], ALU.add, [m1_t, m2_t], [mg_t])
                for j in range(8):
                    pM = banks[j % 4]
                    js = slice(j * 128, (j + 1) * 128)
                    for kc in range(8):
                        MM(pM[:, :n], Wm["out"].ap[:, kc, js], mg[:, kc, :n], kc == 0, kc == 7, [Wh["out"][j // 4], mg_t], [pM])
                    STT(xn[:, j, :n], pM[:, :n], modcol(2, j, s), xt[:, j, :n], ALU.mult, ALU.add, [pM, mod_t, xt_t], [xn_t])
                DMA("sp", dstv[:, :, a:b], xn[:, :, :n], [xn_t], [dst])
                if ix == len(m_tiles) - 1:
                    norm_tile(xn, xn_t, ti, (DV_S2X, DV_S2C), 3, hx_t[ti], hx[:, :, a:b], tmp)

        def ffn_phase(l, last, src, dst, stack):
            upv = kc_view(ffn_up[l])
            act = sbuf(stack, "f_act", [128, NHC, NT], BF16)
            act_t = [T(act[:, :, a:b]) for (a, b) in TILES]
            with ExitStack() as s2:
                wsl = [T(sbuf(s2, "fw%d" % i, [128, 8, 256], BF16)) for i in range(3)]

                def load_hc(hc):
                    w_ = wsl[hc % 3]
                    DMA("pool", w_.ap[:, :, 0:128], upv[:, :, hc * 128:(hc + 1) * 128], [dummy], [w_])
                    DMA("pool", w_.ap[:, :, 128:256], upv[:, :, FH + hc * 128:FH + (hc + 1) * 128], [dummy], [w_])
                load_hc(0)
                load_hc(1)
                apad = sbuf(s2, "f_apad", [128, 34, 66], BF16); apad_t = T(apad)
                cpad = sbuf(s2, "f_cpad", [128, 258], BF16); cpad_t = T(cpad)
                dg = [sbuf(s2, "f_dg%d" % i, [128, 9, 128], BF16) for i in range(2)]; dg_t = [T(dg[0]), T(dg[1])]
                gl = sbuf(s2, "f_gl", [128, 512]); gl_t = T(gl)
                MEMSET(apad[:], 0.0, [apad_t])
                MEMSET(cpad[:], 0.0, [cpad_t])
                for hc in range(NHC):
                    W = wsl[hc % 3]
                    if hc + 2 < NHC:
                        load_hc(hc + 2)
                    D_ = dg[hc % 2]
                    D_t = dg_t[hc % 2]
                    for tap in range(9):
                        TS(D_[:, tap, :], cst[:, CI:CI + 128], V("fcw%d" % l, tap * NHC + hc), None, ALU.mult, ALU.bypass, [cst_t, vecs_t], [D_t])
                    for ti, (a, b) in enumerate(TILES):
                        if last and ti == 0:
                            continue
                        n = b - a
                        pb = banks[ti % 2]
                        for kc in range(8):
                            MM(pb[:, :n], W.ap[:, kc, 0:128], hx[:, kc, a:b], kc == 0, kc == 7, [W, hx_t[ti]], [pb])
                        if ti == 0:
                            ACT(cpad[:, 1:257], pb[:, :n], AF.Copy, [pb], [cpad_t])
                        else:
                            r0 = (ti - 1) * 8
                            ACT(apad[:, r0 + 1:r0 + 9, 1:65], pb[:, :n].rearrange("p (r c) -> p r c", c=64), AF.Copy, [pb], [apad_t])
                    for ti, (a, b) in enumerate(TILES):
                        if last and ti == 0:
                            continue
                        n = b - a
                        pc = banks[2 + (ti % 2)]
                        pg = banks[4 + (ti % 2)]
                        if ti == 0:
                            for i_, dc in enumerate((-1, 0, 1)):
                                MM(pc[:, :n], D_[:, 3 + dc + 1, :], cpad[:, 1 + dc:257 + dc], i_ == 0, i_ == 2, [D_t, cpad_t], [pc])
                        else:
                            r0 = (ti - 1) * 8
                            i_ = 0
                            for dr in (-1, 0, 1):
                                for dc in (-1, 0, 1):
                                    tap = (dr + 1) * 3 + (dc + 1)
                                    MM(pc[:, :n].rearrange("p (r c) -> p r c", c=64), D_[:, tap, :],
                                       apad[:, r0 + 1 + dr:r0 + 9 + dr, 1 + dc:65 + dc], i_ == 0, i_ == 8, [D_t, apad_t], [pc])
                                    i_ += 1
                        for kc in range(8):
                            MM(pg[:, :n], W.ap[:, kc, 128:256], hx[:, kc, a:b], kc == 0, kc == 7, [W, hx_t[ti]], [pg])
                        ACT(gl[:, :n], pc[:, :n], AF.Gelu_apprx_tanh, [pc, vecs_t], [gl_t], bias=V("fcb%d" % l, hc))
                        TT(act[:, hc, a:b], pg[:, :n], gl[:, :n], ALU.mult, [pg, gl_t], [act_t[ti]])
            dnv = ffn_dn[l].rearrange("(hc p) n -> p hc n", p=128)
            wdp_t = [T(wd[:, q:q + 2, :]) for q in range(0, NHC, 2)]
            EARLY = (18, 20)
            for q in EARLY:
                DMA("pool", wd[:, q:q + 2, :], dnv[:, q:q + 2, :], [dummy], [wdp_t[q // 2]])
            HC_ORDER = list(range(18, NHC)) + list(range(0, 18))
            P.barrier()
            with ExitStack() as s3:
                for q in range(0, NHC, 2):
                    if q in EARLY:
                        continue
                    DMA("pool", wd[:, q:q + 2, :], dnv[:, q:q + 2, :], [dummy], [wdp_t[q // 2]] + (hx_t if q == 0 else []) + ([wd_t] if q == 0 else []))
                xt = sbuf(s3, "d_x", [128, 8, 512]); xt_t = T(xt)
                xn, xn_t = xt, xt_t
                yo, yo_t = xt, xt_t
                sq = sbuf(s3, "d_sq", [128, 8, 512], BF16); rs = sbuf(s3, "d_rs", [128, 512])
                t1 = [sbuf(s3, "d_t1%d" % i, [128, 512]) for i in range(2)]
                tmp = (sq, T(sq), rs, T(rs), t1, [T(t1[0]), T(t1[1])])
                srcv = kc_view(src.ap)
                for ti, (a, b) in enumerate(TILES):
                    if last and ti == 0:
                        continue
                    n = b - a
                    s = 0 if ti > 0 else 1
                    DMA("sp", xt[:, :, :n], srcv[:, :, a:b], [src], [xt_t])
                    for j in range(8):
                        pb = banks[j]
                        for hi, hc in enumerate(HC_ORDER):
                            MM(pb[:, :n], wd[:, hc, j * 128:(j + 1) * 128], act[:, hc, a:b], hi == 0, hi == NHC - 1, [wdp_t[hc // 2], wd_t, act_t[ti]], [pb])
                    for j in range(8):
                        STT(xn[:, j, :n], banks[j][:, :n], modcol(5, j, s), xt[:, j, :n], ALU.mult, ALU.add, [banks[j], mod_t, xt_t], [xn_t])
                    if not last:
                        DMA("sp", kc_view(dst.ap)[:, :, a:b], xn[:, :, :n], [xn_t], [dst])
                        if ti == len(TILES) - 1:
                            ACT(dv[:, 255:256], vecs[:, 0:1], AF.Copy, [vecs_t], [wd_t])
                    else:
                        norm_tile(xn, xn_t, ti, None, None, yo_t, yo, tmp, final=True)
                        ev = DMA("sp", kc_view(outT)[:, :, a - NCTX:b - NCTX], yo[:, :, :n], [yo_t], [outT_t])
                        out_evs.append(ev)

        for l in range(DEPTH):
            last = (l == DEPTH - 1)
            src0 = xs_s[2 * l]
            mid = xs_s[2 * l + 1]
            nxt = xs_s[2 * l + 2] if not last else None
            P.barrier()
            with ExitStack() as s1:
                wslots = [T(sbuf(s1, "aw%d" % i, [128, 4096], BF16)) for i in range(3)]
                ada_phase(l, s1, wslots)
            P.barrier()
            if debug:
                out_evs.append(DMA("sp", dbg["mod%d" % l], mod[:, :], [mod_t], [T(None)]))
                out_evs.append(DMA("sp", dbg["dv%d" % l], dv[:, :], [dv_t], [T(None)]))
            with ExitStack() as s1:
                norm1_phase(l, src0, s1)
            if debug:
                out_evs.append(DMA("sp", dbg["hx%d" % l], RR[:, 0:8 * NT], hx_t, [T(None)]))
            P.barrier()
            with ExitStack() as s1:
                gla_phase(l, last, s1)
            P.barrier()
            with ExitStack() as s1:
                rnn_phase(l, last, s1)
            P.barrier()
            if debug:
                out_evs.append(DMA("sp", dbg["og%d" % l], og_d.ap, [og_d], [T(None)]))
                out_evs.append(DMA("sp", dbg["rr%d" % l], rr_d.ap, [rr_d], [T(None)]))
            with ExitStack() as s1:
                merge_phase(l, last, src0, mid, s1)
            if debug:
                out_evs.append(DMA("sp", dbg["hx2_%d" % l], RR[:, 0:8 * NT], hx_t, [T(None)]))
            P.barrier()
            with ExitStack() as s1:
                ffn_phase(l, last, mid, nxt, s1)
        P.finish(out_evs)
        with nc.Block() as block:
            P.emit(sems, block)
    return nc


_NC_CACHE = {}


def kernel(**inp):
    inp = {k: np.asarray(v) for k, v in inp.items()}
    B = inp["x"].shape[0]
    if "nc" not in _NC_CACHE:
        _NC_CACHE["nc"] = build_nc()
    nc = _NC_CACHE["nc"]
    cst = _consts()
    shared = {
        "cst": cst,
        "ada_w": np.ascontiguousarray(inp["ada_w"], np.float32),
        "w_in": np.ascontiguousarray(inp["w_in"], np.float32),
        "gla_lr_w": np.ascontiguousarray(inp["gla_lr_w"], np.float32),
        "rnn_wa": np.ascontiguousarray(inp["rnn_wa"], np.float32),
        "rnn_wx": np.ascontiguousarray(inp["rnn_wx"], np.float32),
        "w_gla_o": np.ascontiguousarray(inp["w_gla_o"], np.float32),
        "w_rnn_o": np.ascontiguousarray(inp["w_rnn_o"], np.float32),
        "w_out": np.ascontiguousarray(inp["w_out"], np.float32),
        "ffn_up": np.ascontiguousarray(inp["ffn_up"], np.float32),
        "ffn_down": np.ascontiguousarray(inp["ffn_down"], np.float32),
    }
    in_maps = []
    for b in range(B):
        m = dict(shared)
        m["xs0"] = np.ascontiguousarray(np.concatenate([inp["ctx"][b].T, inp["x"][b].T], axis=1), np.float32)
        m["vecs"] = _pack_vecs(inp, b)
        in_maps.append(m)
    res = run_bass_kernel_spmd(nc, in_maps, core_ids=list(range(B)))
    out = np.stack([np.asarray(r["outT"]).T for r in res.results], axis=0)
    return np.ascontiguousarray(out, np.float32)
```
